# Optimizing a Trainium2 kernel written in Bass

```python
import math
import jax, jax.numpy as jnp
from jax import lax
import numpy as np

D_MODEL = 2048
BATCH = 4
SEQ = 2048
DEPTH = 2

N_MIXERS = 2
N_CONV_LAYERS = (DEPTH + N_MIXERS - 1) // N_MIXERS
N_ATTN_LAYERS = DEPTH // N_MIXERS
CONV_WIDTH = 31
DIFF_HEADS = 8
DIFF_HEAD_DIM = D_MODEL // (2 * DIFF_HEADS)
DIFF_V_DIM = 2 * DIFF_HEAD_DIM
D_FF = 4 * D_MODEL
ROPE_THETA = 10000.0
Q_BLOCK = 128
DEEPNORM_ALPHA = (2.0 * DEPTH) ** 0.25
DEEPNORM_BETA = (8.0 * DEPTH) ** -0.25
LN_EPS = 1e-5
RMS_EPS = 1e-5
LAMBDA_PARAM_STD = 0.1
MAX_POS_OFFSET = 1024

kernel_name = "hybrid_conformer_conv_diff_attn_deepnorm_adaln"


def layer_norm(x, g, b):
    xf = x.astype(jnp.float32)
    mu = jnp.mean(xf, axis=-1, keepdims=True)
    var = jnp.mean(jnp.square(xf - mu), axis=-1, keepdims=True)
    return ((xf - mu) * lax.rsqrt(var + LN_EPS) * g.astype(jnp.float32) + b.astype(jnp.float32)).astype(x.dtype)


def apply_rope(t, cos, sin):
    tf = t.astype(jnp.float32)
    t1, t2 = jnp.split(tf, 2, axis=-1)
    return jnp.concatenate([t1 * cos - t2 * sin, t2 * cos + t1 * sin], axis=-1).astype(t.dtype)


def conformer_conv(h, w1, b1, w_dw, b_dw, g, b, w2, b2):
    u = h @ w1 + b1
    a, gt = jnp.split(u, 2, axis=-1)
    u = a * jax.nn.sigmoid(gt)
    u = lax.conv_general_dilated(
        u, w_dw[:, None, :], window_strides=(1,), padding=[(CONV_WIDTH - 1, 0)],
        dimension_numbers=('NWC', 'WIO', 'NWC'), feature_group_count=D_MODEL) + b_dw
    u = jax.nn.silu(layer_norm(u, g, b))
    return u @ w2 + b2


def diff_attention(h, positions, w_qkv, lq1, lk1, lq2, lk2, subln_g, w_o, lambda_init):
    B, T, _ = h.shape
    q, k, v = jnp.split(h @ w_qkv, 3, axis=-1)
    q = q.reshape(B, T, 2 * DIFF_HEADS, DIFF_HEAD_DIM)
    k = k.reshape(B, T, 2 * DIFF_HEADS, DIFF_HEAD_DIM)
    v = v.reshape(B, T, DIFF_HEADS, DIFF_V_DIM)
    inv_freq = ROPE_THETA ** (-jnp.arange(0, DIFF_HEAD_DIM, 2, dtype=jnp.float32) / DIFF_HEAD_DIM)
    ang = positions.astype(jnp.float32)[..., None] * inv_freq
    cos = jnp.cos(ang)[:, :, None, :]
    sin = jnp.sin(ang)[:, :, None, :]
    q = apply_rope(q, cos, sin) * (DIFF_HEAD_DIM ** -0.5)
    k = apply_rope(k, cos, sin)
    f32 = jnp.float32
    lam = (jnp.exp(jnp.sum(lq1.astype(f32) * lk1.astype(f32)))
           - jnp.exp(jnp.sum(lq2.astype(f32) * lk2.astype(f32))) + lambda_init)
    outs = []
    for i in range(T // Q_BLOCK):
        q0 = i * Q_BLOCK
        kend = q0 + Q_BLOCK
        qb = q[:, q0:kend]
        kb = k[:, :kend]
        vb = v[:, :kend]
        s = jnp.einsum('bqhd,bkhd->bhqk', qb, kb, preferred_element_type=jnp.float32)
        mask = (q0 + jnp.arange(Q_BLOCK))[:, None] >= jnp.arange(kend)[None, :]
        s = jnp.where(mask, s, -jnp.inf)
        p = jax.nn.softmax(s, axis=-1).reshape(B, DIFF_HEADS, 2, Q_BLOCK, kend)
        a = p[:, :, 0] - lam * p[:, :, 1]
        outs.append(jnp.einsum('bhqk,bkhe->bqhe', a.astype(vb.dtype), vb))
    o = jnp.concatenate(outs, axis=1).astype(jnp.float32)
    o = o * lax.rsqrt(jnp.mean(jnp.square(o), axis=-1, keepdims=True) + RMS_EPS)
    o = o * subln_g.astype(jnp.float32) * (1.0 - lambda_init)
    return o.astype(h.dtype).reshape(B, T, D_MODEL) @ w_o


def squared_relu_mlp(h, w1, b1, w2, b2):
    u = jnp.square(jax.nn.relu(h @ w1 + b1))
    return u @ w2 + b2


def setup_inputs(seed: int = 0) -> dict:
    key = jax.random.key(seed)
    ks = iter(jax.random.split(key, 40))
    f32 = jnp.float32
    D = D_MODEL

    def nrm(shape, scale):
        return jax.random.normal(next(ks), shape, f32) * scale

    def gain(shape):
        return 1.0 + nrm(shape, 0.02)

    x = jax.random.normal(next(ks), (BATCH, SEQ, D), f32)
    c = jax.random.normal(next(ks), (BATCH, D), f32)
    positions = (jnp.arange(SEQ, dtype=jnp.int32)[None, :]
                 + jax.random.randint(next(ks), (BATCH, 1), 0, MAX_POS_OFFSET, dtype=jnp.int32))
    ada_w = nrm((DEPTH, D, 6 * D), D ** -0.5)
    ada_b = nrm((DEPTH, 6 * D), 0.01)
    ln_mix_g = gain((DEPTH, D)); ln_mix_b = nrm((DEPTH, D), 0.01)
    ln_ffn_g = gain((DEPTH, D)); ln_ffn_b = nrm((DEPTH, D), 0.01)
    conv_pw1_w = nrm((N_CONV_LAYERS, D, 2 * D), D ** -0.5)
    conv_pw1_b = nrm((N_CONV_LAYERS, 2 * D), 0.01)
    conv_dw_w = nrm((N_CONV_LAYERS, CONV_WIDTH, D), CONV_WIDTH ** -0.5)
    conv_dw_b = nrm((N_CONV_LAYERS, D), 0.01)
    conv_ln_g = gain((N_CONV_LAYERS, D)); conv_ln_b = nrm((N_CONV_LAYERS, D), 0.01)
    conv_pw2_w = nrm((N_CONV_LAYERS, D, D), D ** -0.5 * DEEPNORM_BETA)
    conv_pw2_b = nrm((N_CONV_LAYERS, D), 0.01)
    attn_qkv_w = jnp.concatenate([
        nrm((N_ATTN_LAYERS, D, 2 * D), D ** -0.5),
        nrm((N_ATTN_LAYERS, D, D), D ** -0.5 * DEEPNORM_BETA)],
        axis=-1)
    attn_lq1 = nrm((N_ATTN_LAYERS, DIFF_HEAD_DIM), LAMBDA_PARAM_STD)
    attn_lk1 = nrm((N_ATTN_LAYERS, DIFF_HEAD_DIM), LAMBDA_PARAM_STD)
    attn_lq2 = nrm((N_ATTN_LAYERS, DIFF_HEAD_DIM), LAMBDA_PARAM_STD)
    attn_lk2 = nrm((N_ATTN_LAYERS, DIFF_HEAD_DIM), LAMBDA_PARAM_STD)
    attn_subln_g = gain((N_ATTN_LAYERS, DIFF_V_DIM))
    attn_o_w = nrm((N_ATTN_LAYERS, D, D), D ** -0.5 * DEEPNORM_BETA)
    mlp_w1 = nrm((DEPTH, D, D_FF), D ** -0.5)
    mlp_b1 = nrm((DEPTH, D_FF), 0.01)
    mlp_w2 = nrm((DEPTH, D_FF, D), D_FF ** -0.5 * DEEPNORM_BETA)
    mlp_b2 = nrm((DEPTH, D), 0.01)
    return {"x": x, "c": c, "positions": positions,
            "ada_w": ada_w, "ada_b": ada_b,
            "ln_mix_g": ln_mix_g, "ln_mix_b": ln_mix_b, "ln_ffn_g": ln_ffn_g, "ln_ffn_b": ln_ffn_b,
            "conv_pw1_w": conv_pw1_w, "conv_pw1_b": conv_pw1_b, "conv_dw_w": conv_dw_w,
            "conv_dw_b": conv_dw_b, "conv_ln_g": conv_ln_g, "conv_ln_b": conv_ln_b,
            "conv_pw2_w": conv_pw2_w, "conv_pw2_b": conv_pw2_b,
            "attn_qkv_w": attn_qkv_w, "attn_lq1": attn_lq1, "attn_lk1": attn_lk1,
            "attn_lq2": attn_lq2, "attn_lk2": attn_lk2, "attn_subln_g": attn_subln_g,
            "attn_o_w": attn_o_w,
            "mlp_w1": mlp_w1, "mlp_b1": mlp_b1, "mlp_w2": mlp_w2, "mlp_b2": mlp_b2}


def reference(x, c, positions, ada_w, ada_b, ln_mix_g, ln_mix_b, ln_ffn_g, ln_ffn_b,
              conv_pw1_w, conv_pw1_b, conv_dw_w, conv_dw_b, conv_ln_g, conv_ln_b,
              conv_pw2_w, conv_pw2_b, attn_qkv_w, attn_lq1, attn_lk1, attn_lq2, attn_lk2,
              attn_subln_g, attn_o_w, mlp_w1, mlp_b1, mlp_w2, mlp_b2):
    cond = jax.nn.silu(c)
    for i in range(DEPTH):
        mod = cond @ ada_w[i] + ada_b[i]
        sh_m, sc_m, g_m, sh_f, sc_f, g_f = [m[:, None, :] for m in jnp.split(mod, 6, axis=-1)]
        h = x * (1.0 + sc_m) + sh_m
        j = i // N_MIXERS
        if i % N_MIXERS == 0:
            y = conformer_conv(h, conv_pw1_w[j], conv_pw1_b[j], conv_dw_w[j], conv_dw_b[j],
                               conv_ln_g[j], conv_ln_b[j], conv_pw2_w[j], conv_pw2_b[j])
        else:
            lambda_init = 0.8 - 0.6 * math.exp(-0.3 * i)
            y = diff_attention(h, positions, attn_qkv_w[j], attn_lq1[j], attn_lk1[j],
                               attn_lq2[j], attn_lk2[j], attn_subln_g[j], attn_o_w[j],
                               lambda_init)
        x = layer_norm(DEEPNORM_ALPHA * x + g_m * y, ln_mix_g[i], ln_mix_b[i])
        h = x * (1.0 + sc_f) + sh_f
        y = squared_relu_mlp(h, mlp_w1[i], mlp_b1[i], mlp_w2[i], mlp_b2[i])
        x = layer_norm(DEEPNORM_ALPHA * x + g_f * y, ln_ffn_g[i], ln_ffn_b[i])
    return x
```

```python
import math
from contextlib import ExitStack

import numpy as np
import concourse.bass as bass
import concourse.mybir as mybir
from concourse.bass_utils import run_bass_kernel_spmd

F32 = mybir.dt.float32
BF16 = mybir.dt.bfloat16
I32 = mybir.dt.int32
AF = mybir.ActivationFunctionType
ALU = mybir.AluOpType

ENGS = ("pe", "act", "dve", "pool", "sp")
SAME_ENG_SYNC = True

D = 2048
KC = 16
NTOK = 1024
NTILE = 8
DFF = 8192
ALPHA = 4.0 ** 0.25
LN_EPS = 1e-5
RMS_EPS = 1e-5
HD = 128
NHEAD = 8
LAMBDA_INIT1 = 0.8 - 0.6 * math.exp(-0.3 * 1)
NEG = -30000.0

PP_PW1B = 0
PP_DWB = 32
PP_CLNG = 48
PP_CLNB = 64
PP_B1 = 80
PP_DWW = 208
PP_FLAG = 704
PP_CT = 708
PP_INVF = 724
PP_ADAB = 728
PP_PW2B = 920
PP_B2 = 936
NPP = 968


class Buf:
    __slots__ = ("name", "w", "rs", "dsem", "dcnt")

    def __init__(self, name):
        self.name = name
        self.w = None
        self.rs = {}
        self.dsem = None
        self.dcnt = 0


class Ins:
    __slots__ = ("eng", "fn", "deps", "sem", "val", "isdma", "needed", "inc")

    def __init__(self, eng, fn, isdma):
        self.eng = eng
        self.fn = fn
        self.deps = []
        self.sem = None
        self.val = 0
        self.isdma = isdma
        self.needed = False
        self.inc = 16


class Prog:
    def __init__(self, nc, es):
        self.nc = nc
        self.es = es
        self.streams = {e: [] for e in ENGS}
        self.esem = {e: es.enter_context(nc.semaphore("s_" + e)) for e in ("pe", "act", "dve", "pool")}
        self.nsem = 4

    def op(self, eng, fn, reads=(), writes=(), dma=False, inc=16):
        ins = Ins(eng, fn, dma)
        ins.inc = inc
        deps = {}
        for b in reads:
            if b.w is not None:
                deps[id(b.w)] = b.w
        for b in writes:
            if b.w is not None:
                deps[id(b.w)] = b.w
            for r in b.rs.values():
                deps[id(r)] = r
        if dma:
            dst = writes[0]
            if dst.dsem is None:
                dst.dsem = self.es.enter_context(self.nc.semaphore("d_" + dst.name))
                self.nsem += 1
            dst.dcnt += inc
            ins.sem = dst.dsem
            ins.val = dst.dcnt
            ins.needed = True
        for d in deps.values():
            if d is ins:
                continue
            if (not d.isdma) and d.eng == eng and (eng == "pe" or not SAME_ENG_SYNC):
                continue
            ins.deps.append(d)
            d.needed = True
        k = ins.eng if not dma else ("d", id(ins.sem))
        for b in reads:
            b.rs[k] = ins
        for b in writes:
            b.w = ins
            b.rs = {}
        self.streams[eng].append(ins)
        return ins

    def dma(self, out, in_, reads, writes, q="sp", **kw):
        return self.op(q, lambda e: e.dma_start(out=out, in_=in_, **kw), reads, writes, dma=True)

    def final_wait(self, bufs, eng="sp"):
        return self.op(eng, lambda e: None, reads=list(bufs), writes=())

    def emit(self):
        nc = self.nc
        for e in ("pe", "act", "dve", "pool"):
            c = 0
            for ins in self.streams[e]:
                if ins.isdma:
                    continue
                if ins.needed:
                    c += 1
                    ins.sem = self.esem[e]
                    ins.val = c
        stats = {}

        def run(eng_obj, ename):
            known = {}
            nw = 0
            for ins in self.streams[ename]:
                req = {}
                for d in ins.deps:
                    k = id(d.sem)
                    if k not in req or req[k][1] < d.val:
                        req[k] = (d.sem, d.val)
                for k, (sem, val) in req.items():
                    if known.get(k, 0) >= val:
                        continue
                    eng_obj.wait_ge(sem, val)
                    known[k] = val
                    nw += 1
                r = ins.fn(eng_obj)
                if r is None:
                    continue
                if ins.isdma:
                    r.then_inc(ins.sem, ins.inc)
                elif ins.needed:
                    r.then_inc(ins.sem, 1)
            stats[ename] = (len(self.streams[ename]), nw)

        with nc.Block() as block:
            @block.tensor
            def _(e):
                run(e, "pe")

            @block.scalar
            def _(e):
                run(e, "act")

            @block.vector
            def _(e):
                run(e, "dve")

            @block.gpsimd
            def _(e):
                run(e, "pool")

            @block.sync
            def _(e):
                run(e, "sp")
        return stats


class _Stop(Exception):
    pass


def build_nc(stage=99, dbg=None, attn_only=False):
    nc = bass.Bass("TRN2", target_bir_lowering=False)

    def din(name, shape, dt=F32):
        return nc.dram_tensor(name, list(shape), dt, kind="ExternalInput").ap()

    x_d = din("x", [NTOK, D])
    xh_d = din("xh", [64, D])
    pos_d = din("pos", [1, NTOK], I32)
    ppar_d = din("ppar", [128, NPP])
    cst_d = din("cst", [128, 640])
    lqk_d = din("lqk", [4, 128])
    ada_w = din("ada_w", [2 if (stage >= 3 and not attn_only) else 1, D, 6 * D])
    ada1_idx = 0 if attn_only else 1
    ln_mix_g = din("ln_mix_g", [2, D]); ln_mix_b = din("ln_mix_b", [2, D])
    ln_ffn_g = din("ln_ffn_g", [2, D]); ln_ffn_b = din("ln_ffn_b", [2, D])
    pw1_w = din("conv_pw1_w", [1, D, 2 * D] if not attn_only else [1, 1, 1])
    pw2_w = din("conv_pw2_w", [1, D, D] if not attn_only else [1, 1, 1])
    qkv_w = din("attn_qkv_w", [1, D, 3 * D] if stage >= 3 else [1, 1, 1])
    o_w = din("attn_o_w", [1, D, D] if stage >= 3 else [1, 1, 1])
    subg_d = din("attn_subln_g", [1, 256])
    nl = 2 if stage >= 3 else 1
    w1_d = din("mlp_w1", [nl, D, DFF] if (stage >= 2 and not attn_only) else [1, 1, 1])
    w2_d = din("mlp_w2", [nl, DFF, D] if (stage >= 2 and not attn_only) else [1, 1, 1])
    out_d = nc.dram_tensor("out", [NTOK, D], F32, kind="ExternalOutput").ap()
    kvl_h = [nc.dram_tensor(f"kvl{h}", [256, 2048], BF16).ap() for h in range(NHEAD)]
    kva_h = [nc.dram_tensor(f"kva{h}", [512, 2048], BF16).ap() for h in range(NHEAD)]
    qsc_d = nc.dram_tensor("qsc", [16 * 128, NTOK], BF16).ap()

    with ExitStack() as es:
        P = Prog(nc, es)

        def sb(name, shape, dt=F32):
            return es.enter_context(nc.sbuf_tensor("sb_" + name, list(shape), dt))

        X = sb("X", [128, NTILE, D])
        XB = [[Buf(f"X{t}_{g}") for g in range(4)] for t in range(NTILE)]
        HT = sb("HT", [128, KC, 1088], BF16)
        NSLOT = 2
        WS = [sb(f"WS{i}", [128, KC, 512], BF16) for i in range(NSLOT)]
        WSB = [Buf(f"WS{i}") for i in range(NSLOT)]
        BC = [sb(f"BC{i}", [128, 512]) for i in range(4)]
        BCB = [Buf(f"BC{i}") for i in range(4)]
        ppar = sb("ppar", [128, NPP]); ppB = Buf("ppar")
        cst = sb("cst", [128, 640]); cstB = Buf("cst")
        identb = sb("identb", [128, 128], BF16); identbB = Buf("identb")
        onesb = sb("onesb", [128, 128], BF16); onesbB = Buf("onesb")
        condT = sb("condT", [128, KC], BF16); condB = Buf("condT")
        modT = [sb(f"modT{i}", [128, 96]) for i in range(2)]
        modB = [Buf(f"modT{i}") for i in range(2)]
        sm = sb("sm", [128, 256]); smB = Buf("sm")
        junk = sb("junk", [128, 8]); junkB = Buf("junk")
        SCR = sb("SCR", [128, 15104])
        PS = [es.enter_context(nc.psum_tensor(f"ps{i}", [128, 512], F32)) for i in range(8)]
        PB = [Buf(f"ps{i}") for i in range(8)]
        PSB16 = [PS[6][:].bitcast(BF16), PS[7][:].bitcast(BF16)]
        outB = Buf("out")

        ident = cst[:, 0:128]
        ones = cst[:, 128:256]
        PT = cst[:, 256:384]
        tri = cst[:, 384:512]

        def pp(c0, n=1):
            return ppar[:, c0:c0 + n]

        psr = [0]

        def nextps():
            i = psr[0] % 8
            psr[0] += 1
            return PS[i], PB[i]

        def dump(name, ap, bufs, ncols):
            if dbg == name:
                P.dma(out_d[0:128, 0:ncols], ap, list(bufs), [outB], q="pool")
                raise _Stop()

        try:
            P.dma(ppar[:], ppar_d, [], [ppB], q="sp")
            P.dma(cst[:], cst_d, [], [cstB], q="sp")
            for t in range(NTILE):
                P.dma(X[:, t, :], x_d[t * 128:(t + 1) * 128, :], [], XB[t], q="sp")
            P.op("dve", lambda e: e.tensor_copy(out=identb[:], in_=ident), [cstB], [identbB])
            P.op("dve", lambda e: e.tensor_copy(out=onesb[:], in_=ones), [cstB], [onesbB])
            P.op("act", lambda e: e.activation(out=condT[:], in_=pp(PP_CT, 16), func=AF.Silu), [ppB], [condB])

            PLAN = []
            if attn_only:
                PLAN = [("ada1", g) for g in range(24)] + [("qkv", q) for q in range(12)] + [("ow", ng) for ng in range(4)]
            for g in range(12 if not attn_only else 0):
                PLAN.append(("ada0", g))
            if not attn_only:
                for hf in range(2):
                    for jp in range(4):
                        PLAN.append(("pw1", jp)); PLAN.append(("pw1", 4 + jp))
                    if hf == 0:
                        for g in range(12, 24):
                            PLAN.append(("ada0", g))
                    for ng in range(4):
                        PLAN.append(("pw2", ng))
                if stage >= 2:
                    for g in range(4):
                        for q in range(4):
                            PLAN.append(("w1_0", g * 4 + q))
                        for ng in range(4):
                            PLAN.append(("w2_0", g * 4 + ng))
                        if stage >= 3:
                            for gg in range(g * 6, g * 6 + 6):
                                PLAN.append(("ada1", gg))
                if stage >= 3:
                    for q in range(12):
                        PLAN.append(("qkv", q))
                    for ng in range(4):
                        PLAN.append(("ow", ng))
                    for g in range(4):
                        for q in range(4):
                            PLAN.append(("w1_1", g * 4 + q))
                        for ng in range(4):
                            PLAN.append(("w2_1", g * 4 + ng))

            def piece_src(spec):
                kind, k = spec
                if kind == "ada0":
                    m, r0, c0 = ada_w[0], 0, k * 512
                elif kind == "ada1":
                    m, r0, c0 = ada_w[ada1_idx], 0, k * 512
                elif kind == "pw1":
                    m, r0, c0 = pw1_w[0], 0, (k % 4) * 512 + (2048 if k >= 4 else 0)
                elif kind == "pw2":
                    m, r0, c0 = pw2_w[0], 0, k * 512
                elif kind.startswith("w1_"):
                    m, r0, c0 = w1_d[int(kind[3])], 0, k * 512
                elif kind.startswith("w2_"):
                    m, r0, c0 = w2_d[int(kind[3])], (k // 4) * 2048, (k % 4) * 512
                elif kind == "qkv":
                    m, r0, c0 = qkv_w[0], 0, k * 512
                elif kind == "ow":
                    m, r0, c0 = o_w[0], 0, k * 512
                return m[r0:r0 + 2048, c0:c0 + 512].rearrange("(kc p) n -> p kc n", p=128)

            wst = {"issued": 0, "cur": 0}

            def wget(spec):
                i = wst["cur"]
                assert PLAN[i] == spec, (i, PLAN[i], spec)
                while wst["issued"] < min(i + NSLOT, len(PLAN)):
                    j = wst["issued"]
                    sl = j % NSLOT
                    P.dma(WS[sl][:], piece_src(PLAN[j]), [], [WSB[sl]], q="pool")
                    wst["issued"] += 1
                wst["cur"] += 1
                return WS[i % NSLOT], WSB[i % NSLOT]

            def compute_mod(i, blocks):
                for g in blocks:
                    w, wB = wget((f"ada{i}", g))
                    ps, psB = nextps()
                    for n in range(4):
                        col = g * 4 + n
                        for kc in range(KC):
                            P.op("pe", lambda e, ps=ps, w=w, n=n, kc=kc, col=col: e.matmul(
                                ps[:, col:col + 1], lhsT=w[:, kc, n * 128:(n + 1) * 128], rhs=condT[:, kc:kc + 1],
                                start=(kc == 0), stop=(kc == KC - 1)), [wB, condB], [psB])
                    P.op("dve", lambda e, ps=ps, g=g, i=i: e.tensor_tensor(
                        out=modT[i][:, g * 4:g * 4 + 4], in0=ps[:, g * 4:g * 4 + 4],
                        in1=pp(PP_ADAB + 96 * i + g * 4, 4), op=ALU.add), [psB, ppB], [modB[i]])

            def bcast_cols(dst, dstB, srcT_ap_fn, srcBs, ng):
                ps, psB = nextps()
                dg = SCRT["diag32"]
                dh = SCRT["diaghl"]
                for n in range(4):
                    c = ng * 4 + n
                    dB = SCRB["diag32"][0]
                    P.op("dve", lambda e, n=n, c=c: e.tensor_scalar(
                        out=dg, in0=ident, scalar1=srcT_ap_fn(c), scalar2=None, op0=ALU.mult),
                        [cstB] + srcBs, [dB])
                    P.op("dve", lambda e, n=n: e.tensor_copy(out=dh[:, 0, :], in_=dg), [dB], [dB])
                    P.op("dve", lambda e, n=n: e.tensor_tensor(out=dh[:, 1, :], in0=dg, in1=dh[:, 0, :],
                                                                op=ALU.subtract), [dB], [dB])
                    for hl in range(2):
                        P.op("pe", lambda e, ps=ps, n=n, hl=hl: e.matmul(
                            ps[:, n * 128:(n + 1) * 128], lhsT=onesb[:], rhs=dh[:, hl, :],
                            start=(hl == 0), stop=(hl == 1)), [onesbB, dB], [psB])
                P.op("act", lambda e, ps=ps: e.activation(out=dst, in_=ps[:], func=AF.Copy), [psB], [dstB])

            def make_hT(scale_col, shift_col, srcBs, tiles, col0, hbufs):
                for kc in range(KC):
                    for g0 in range(0, len(tiles), 4):
                        grp = tiles[g0:g0 + 4]
                        ps, psB = nextps()
                        for q, t in enumerate(grp):
                            P.op("pe", lambda e, ps=ps, q=q, t=t, kc=kc: e.transpose(
                                ps[:, q * 128:(q + 1) * 128], X[:, t, kc * 128:(kc + 1) * 128], ident),
                                [XB[t][kc // 4], cstB], [psB])
                        n = len(grp) * 128
                        c = col0 + g0 * 128
                        P.op("act", lambda e, ps=ps, kc=kc, n=n, c=c: e.activation(
                            out=HT[:, kc, c:c + n], in_=ps[:, 0:n], func=AF.Identity,
                            bias=shift_col(kc), scale=scale_col(kc)), [psB] + srcBs, [hbufs(kc, g0)])

            def prescale_X(ng, gb2_tile, gb2B, tiles):
                for t in tiles:
                    P.op("dve", lambda e, t=t: e.scalar_tensor_tensor(
                        out=X[:, t, ng * 512:(ng + 1) * 512], in0=X[:, t, ng * 512:(ng + 1) * 512], scalar=ALPHA,
                        in1=gb2_tile, op0=ALU.mult, op1=ALU.add), [XB[t][ng], gb2B], [XB[t][ng]])

            ytmp_i = [0]

            def evac_y(ps, psB, t, ng, g_tile, gB):
                k = ytmp_i[0] % 2
                ytmp_i[0] += 1
                yt = SCRT["ytmp"][k]
                ytB = SCRB["ytmp"][k]
                P.op("dve", lambda e: e.tensor_tensor(out=yt, in0=ps[:], in1=g_tile, op=ALU.mult), [psB, gB], [ytB])
                P.op("pool", lambda e: e.tensor_tensor(out=X[:, t, ng * 512:(ng + 1) * 512],
                                                       in0=X[:, t, ng * 512:(ng + 1) * 512], in1=yt, op=ALU.add),
                     [ytB, XB[t][ng]], [XB[t][ng]])

            def layer_norm_X(g_row, b_row, final=False):
                st = SCRT["lnst"]
                stB = SCRB["lnst"]
                for t in range(NTILE):
                    for c in range(4):
                        P.op("dve", lambda e, t=t, c=c: e.bn_stats(out=st[:, t, c, :], in_=X[:, t, c * 512:(c + 1) * 512]),
                             [XB[t][c]], [stB[t]])
                    P.op("dve", lambda e, t=t: e.bn_aggr(out=st[:, t, 4, 0:2], in_=st[:, t, 0:4, :]), [stB[t]], [stB[t]])
                    P.op("dve", lambda e, t=t: e.tensor_scalar(out=st[:, t, 4, 2:3], in0=st[:, t, 4, 1:2], scalar1=LN_EPS,
                                                                scalar2=None, op0=ALU.add), [stB[t]], [stB[t]])
                    P.op("act", lambda e, t=t: e.activation(out=st[:, t, 4, 3:4], in_=st[:, t, 4, 2:3], func=AF.Sqrt),
                         [stB[t]], [stB[t]])
                    P.op("dve", lambda e, t=t: e.reciprocal(out=st[:, t, 4, 4:5], in_=st[:, t, 4, 3:4]), [stB[t]], [stB[t]])
                for ng in range(4):
                    gt, gB = BC[0 + (ng % 2) * 2], BCB[0 + (ng % 2) * 2]
                    bt, bB = BC[1 + (ng % 2) * 2], BCB[1 + (ng % 2) * 2]
                    P.dma(gt[:], g_row[:, ng * 512:(ng + 1) * 512].to_broadcast([128, 512]), [], [gB], q="sp")
                    P.dma(bt[:], b_row[:, ng * 512:(ng + 1) * 512].to_broadcast([128, 512]), [], [bB], q="sp")
                    for t in range(NTILE):
                        xs = X[:, t, ng * 512:(ng + 1) * 512]
                        P.op("dve", lambda e, xs=xs, t=t: e.tensor_scalar(
                            out=xs, in0=xs, scalar1=st[:, t, 4, 0:1], scalar2=st[:, t, 4, 4:5],
                            op0=ALU.subtract, op1=ALU.mult), [XB[t][ng], stB[t]], [XB[t][ng]])
                        P.op("pool", lambda e, xs=xs, gt=gt: e.tensor_tensor(out=xs, in0=xs, in1=gt[:], op=ALU.mult),
                             [XB[t][ng], gB], [XB[t][ng]])
                        P.op("dve", lambda e, xs=xs, bt=bt: e.tensor_tensor(out=xs, in0=xs, in1=bt[:], op=ALU.add),
                             [XB[t][ng], bB], [XB[t][ng]])
                        if final:
                            P.dma(out_d[t * 128:(t + 1) * 128, ng * 512:(ng + 1) * 512], xs, [XB[t][ng]], [outB], q="sp")

            def phase(old, new):
                P.op("pool", lambda e: e.memset(junk[:, 0:1], 0.0), [], list(old) + list(new) + [junkB])

            SCRT = {}
            SCRB = {}
            off = [0]

            def carve(n_f32):
                a = off[0]
                off[0] += n_f32
                assert off[0] <= 15104, off[0]
                return SCR[:, a:a + n_f32]

            SCRT["diag32"] = carve(128)
            SCRT["diaghl"] = carve(128).bitcast(BF16).rearrange("p (h b) -> p h b", h=2)
            SCRB["diag32"] = [Buf(f"diag32_{n}") for n in range(4)]
            SCRT["lnst"] = carve(NTILE * 5 * 6).rearrange("p (t c s) -> p t c s", t=NTILE, c=5)
            SCRB["lnst"] = [Buf(f"lnst{t}") for t in range(NTILE)]
            base_off = off[0]


            def derive(i, which, b2col):
                base = 0 if which == "m" else 32
                mo = 0 if which == "m" else 48
                P.op("dve", lambda e: e.tensor_scalar(out=sm[:, base:base + 16], in0=modT[i][:, mo + 16:mo + 32],
                                                       scalar1=1.0, scalar2=None, op0=ALU.add), [modB[i]], [smB])
                P.op("dve", lambda e: e.tensor_tensor(out=sm[:, base + 16:base + 32], in0=modT[i][:, mo + 32:mo + 48],
                                                       in1=pp(b2col, 16), op=ALU.mult), [modB[i], ppB], [smB])

            if not attn_only:
                dump("condT", ppar[:, 0:64], [ppB, condB], 64)
                compute_mod(0, range(0, 12))
                dump("mod0", modT[0][:], [modB[0]], 96)
                derive(0, "m", PP_PW2B)

                off[0] = base_off
                CT = carve(8192).rearrange("p (a b) -> p a b", a=16)
                CTB = [Buf(f"CT{k}") for k in range(KC)]
                dgc = carve(1984).bitcast(BF16).rearrange("p (a b) -> p a b", a=31)
                dgcB = Buf("dgc")
                xh = carve(2048)
                xhB = Buf("xh")
                tmpA = [carve(544) for _ in range(2)]; tmpAB = [Buf(f"tmpA{k}") for k in range(2)]
                tmpS = [carve(512) for _ in range(2)]; tmpSB = [Buf(f"tmpS{k}") for k in range(2)]
                tmpSb = [t.bitcast(BF16).rearrange("p (a b) -> p a b", a=2) for t in tmpS]
                mean_t = tmpA[0][:, 0:512]; rstd_t = tmpA[1][:, 0:512]; lnB = Buf("convln")
                SCRT["ytmp"] = tmpS
                SCRB["ytmp"] = tmpSB
                CHB = [Buf(f"CH{k}") for k in range(KC)]
                CVB = [Buf(f"CV{k}") for k in range(KC)]
                conv_bufs = CTB + [dgcB, xhB, lnB] + tmpAB + tmpSB + CHB + CVB

                P.dma(xh[0:64, :], xh_d, [], [xhB], q="sp")


                for hf in range(2):
                    tiles = [4 * hf + q for q in range(4)]
                    for kc in range(KC):
                        ps, psB = nextps()
                        P.op("pe", lambda e, ps=ps, kc=kc, hf=hf: e.transpose(
                            ps[:, 0:32], xh[hf * 32:(hf + 1) * 32, kc * 128:(kc + 1) * 128],
                            cst[hf * 32:(hf + 1) * 32, hf * 32:(hf + 1) * 32]), [xhB, cstB], [psB])
                        P.op("act", lambda e, ps=ps, kc=kc: e.activation(
                            out=HT[:, kc, 0:32], in_=ps[:, 0:32], func=AF.Identity,
                            bias=modT[0][:, kc:kc + 1], scale=sm[:, kc:kc + 1]), [psB, modB[0], smB], [CHB[kc]])
                    make_hT(lambda kc: sm[:, kc:kc + 1], lambda kc: modT[0][:, kc:kc + 1], [modB[0], smB], tiles, 32,
                            lambda kc, g0: CHB[kc])
                    if hf == 0:
                        dump("hT0", HT[:, 0, 0:544], CHB, 544)
                    for jp in range(4):
                        wa, waB = wget(("pw1", jp))
                        ph, phB = PS[4], PB[4]
                        for n in range(4):
                            pa, paB = PS[n], PB[n]
                            for kc in range(KC):
                                P.op("pe", lambda e, pa=pa, kc=kc, n=n: e.matmul(
                                    pa[:], lhsT=wa[:, kc, n * 128:(n + 1) * 128], rhs=HT[:, kc, 32:544],
                                    start=(kc == 0), stop=(kc == KC - 1)), [waB, CHB[kc]], [paB])
                            for kc in range(KC):
                                P.op("pe", lambda e, kc=kc, n=n: e.matmul(
                                    ph[:, n * 32:(n + 1) * 32], lhsT=wa[:, kc, n * 128:(n + 1) * 128], rhs=HT[:, kc, 0:32],
                                    start=(kc == 0), stop=(kc == KC - 1)), [waB, CHB[kc]], [phB])
                        wg, wgB = wget(("pw1", 4 + jp))
                        for n in range(4):
                            j = jp * 4 + n
                            pa, paB = PS[n], PB[n]
                            pg, pgB = PS[5 + (n % 2)], PB[5 + (n % 2)]
                            for kc in range(KC):
                                P.op("pe", lambda e, pg=pg, kc=kc, n=n: e.matmul(
                                    pg[:], lhsT=wg[:, kc, n * 128:(n + 1) * 128], rhs=HT[:, kc, 32:544],
                                    start=(kc == 0), stop=(kc == KC - 1)), [wgB, CHB[kc]], [pgB])
                            for kc in range(KC):
                                P.op("pe", lambda e, kc=kc, n=n: e.matmul(
                                    ph[:, 128 + n * 32:128 + (n + 1) * 32], lhsT=wg[:, kc, n * 128:(n + 1) * 128],
                                    rhs=HT[:, kc, 0:32], start=(kc == 0), stop=(kc == KC - 1)), [wgB, CHB[kc]], [phB])
                            k = j % 2
                            ta, taB = tmpA[k], tmpAB[k]
                            P.op("act", lambda e, pg=pg, ta=ta, j=j: e.activation(
                                out=ta[:, 32:544], in_=pg[:], func=AF.Sigmoid, bias=pp(PP_PW1B + 16 + j), scale=1.0),
                                [pgB, ppB], [taB])
                            P.op("act", lambda e, ta=ta, j=j, n=n: e.activation(
                                out=ta[:, 0:32], in_=ph[:, 128 + n * 32:128 + (n + 1) * 32], func=AF.Sigmoid,
                                bias=pp(PP_PW1B + 16 + j), scale=1.0), [phB, ppB, taB], [taB])
                            P.op("dve", lambda e, pa=pa, ta=ta, j=j: e.scalar_tensor_tensor(
                                out=HT[:, j, 544 + 32:544 + 544], in0=pa[:], scalar=pp(PP_PW1B + j), in1=ta[:, 32:544],
                                op0=ALU.add, op1=ALU.mult), [paB, ppB, taB], [CVB[j]])
                            P.op("dve", lambda e, ta=ta, j=j, n=n: e.scalar_tensor_tensor(
                                out=ta[:, 0:32], in0=ph[:, n * 32:(n + 1) * 32], scalar=pp(PP_PW1B + j), in1=ta[:, 0:32],
                                op0=ALU.add, op1=ALU.mult), [phB, ppB, taB], [taB])
                            P.op("dve", lambda e, ta=ta, j=j, hf=hf: e.tensor_scalar(
                                out=HT[:, j, 544:544 + 32], in0=ta[:, 0:32], scalar1=pp(PP_FLAG + hf), scalar2=None,
                                op0=ALU.mult), [taB, ppB, CVB[j]], [CVB[j]])
                    if hf == 0:
                        dump("vT0", HT[:, 0, 544:1088], CVB, 544)
                    s1, s1B = PS[6], PB[6]
                    s2, s2B = PS[7], PB[7]
                    for kc in range(KC):
                        P.op("dve", lambda e, kc=kc: e.tensor_tensor(
                            out=dgc[:], in0=identb[:].unsqueeze(1).to_broadcast([128, 31, 128]),
                            in1=ppar[:, PP_DWW + kc * 31:PP_DWW + (kc + 1) * 31].unsqueeze(2).to_broadcast([128, 31, 128]),
                            op=ALU.mult), [identbB, ppB], [dgcB])
                        pc, pcB = PS[kc % 6], PB[kc % 6]
                        for j in range(31):
                            P.op("pe", lambda e, pc=pc, kc=kc, j=j: e.matmul(
                                pc[:], lhsT=dgc[:, j, :], rhs=HT[:, kc, 544 + 2 + j:544 + 2 + j + 512],
                                start=(j == 0), stop=(j == 30)), [dgcB, CVB[kc]], [pcB])
                        P.op("dve", lambda e, pc=pc, kc=kc: e.tensor_scalar(
                            out=CT[:, kc, :], in0=pc[:], scalar1=pp(PP_DWB + kc), scalar2=None, op0=ALU.add),
                            [pcB, ppB], [CTB[kc]])
                        if hf == 0 and kc == 0:
                            dump("cv_a", CT[:, 0, :], CTB, 512)
                        k = kc % 2
                        cb = tmpSb[k]
                        P.op("pool", lambda e, kc=kc, cb=cb: e.tensor_copy(out=cb[:, 0, :], in_=CT[:, kc, :]),
                             [CTB[kc]], [tmpSB[k]])
                        P.op("pool", lambda e, kc=kc, cb=cb: e.tensor_tensor(out=cb[:, 1, :], in0=CT[:, kc, :],
                                                                              in1=CT[:, kc, :], op=ALU.mult),
                             [CTB[kc], tmpSB[k]], [tmpSB[k]])
                        if hf == 0 and kc == 0:
                            dump("cv_b", CT[:, 0, :], CTB + tmpSB, 512)
                        P.op("pe", lambda e, kc=kc, cb=cb: e.matmul(s1[:], lhsT=onesb[:], rhs=cb[:, 0, :], start=(kc == 0),
                                                                    stop=(kc == KC - 1)), [onesbB, tmpSB[k]], [s1B])
                        P.op("pe", lambda e, kc=kc, cb=cb: e.matmul(s2[:], lhsT=onesb[:], rhs=cb[:, 1, :], start=(kc == 0),
                                                                    stop=(kc == KC - 1)), [onesbB, tmpSB[k]], [s2B])
                    if hf == 0:
                        dump("ct0", CT[:, 0, :], CTB, 512)
                    P.op("dve", lambda e: e.tensor_scalar(out=mean_t, in0=s1[:], scalar1=1.0 / D, scalar2=None,
                                                           op0=ALU.mult), [s1B], [lnB, tmpAB[0], tmpAB[1]])
                    P.op("dve", lambda e: e.tensor_tensor(out=tmpS[0], in0=mean_t, in1=mean_t, op=ALU.mult),
                         [lnB], [tmpSB[0]])
                    P.op("dve", lambda e: e.scalar_tensor_tensor(out=rstd_t, in0=s2[:], scalar=1.0 / D, in1=tmpS[0],
                                                                  op0=ALU.mult, op1=ALU.subtract), [s2B, tmpSB[0]], [lnB])
                    P.op("dve", lambda e: e.tensor_scalar(out=rstd_t, in0=rstd_t, scalar1=LN_EPS, scalar2=None,
                                                           op0=ALU.add), [lnB], [lnB])
                    P.op("act", lambda e: e.activation(out=rstd_t, in_=rstd_t, func=AF.Sqrt), [lnB], [lnB])
                    P.op("dve", lambda e: e.reciprocal(out=rstd_t, in_=rstd_t), [lnB], [lnB])
                    for kc in range(KC):
                        k = kc % 2
                        P.op("pool", lambda e, kc=kc, k=k: e.tensor_tensor(out=tmpS[k], in0=CT[:, kc, :], in1=mean_t,
                                                                            op=ALU.subtract), [CTB[kc], lnB], [tmpSB[k]])
                        P.op("dve", lambda e, k=k: e.tensor_tensor(out=tmpS[k], in0=tmpS[k], in1=rstd_t, op=ALU.mult),
                             [tmpSB[k], lnB], [tmpSB[k]])
                        P.op("act", lambda e, kc=kc, k=k: e.activation(
                            out=HT[:, kc, 32:544], in_=tmpS[k], func=AF.Silu, bias=pp(PP_CLNB + kc), scale=pp(PP_CLNG + kc)),
                            [tmpSB[k], ppB], [CHB[kc]])
                    if hf == 0:
                        dump("sT0", HT[:, 0, 32:544], CHB, 512)
                    if hf == 0:
                        compute_mod(0, range(12, 24))
                        derive(0, "f", PP_B2)
                    for ng in range(4):
                        w, wB = wget(("pw2", ng))
                        gbt, gbB = BC[(ng % 2) * 2], BCB[(ng % 2) * 2]
                        gt, gB = BC[(ng % 2) * 2 + 1], BCB[(ng % 2) * 2 + 1]
                        bcast_cols(gbt[:], gbB, lambda c: sm[:, 16 + c:17 + c], [smB], ng)
                        bcast_cols(gt[:], gB, lambda c: modT[0][:, 32 + c:33 + c], [modB[0]], ng)
                        prescale_X(ng, gbt[:], gbB, tiles)
                        for q, t in enumerate(tiles):
                            py, pyB = nextps()
                            for kc in range(KC):
                                P.op("pe", lambda e, py=py, kc=kc, q=q, w=w: e.matmul(
                                    py[:], lhsT=HT[:, kc, 32 + q * 128:32 + (q + 1) * 128], rhs=w[:, kc, :],
                                    start=(kc == 0), stop=(kc == KC - 1)), [CHB[kc], wB], [pyB])
                            evac_y(py, pyB, t, ng, gt[:], gB)
                layer_norm_X(ln_mix_g[0:1, :], ln_mix_b[0:1, :], final=(stage == 1))
            HTB = [[Buf(f"HT{k}_{h}") for h in range(2)] for k in range(KC)]

            def mlp(i, old, extra_sched=None, final=False):
                off[0] = base_off
                U = carve(8192).bitcast(BF16).rearrange("p (a b) -> p a b", a=16)
                UB = [[Buf(f"U{i}_{k}_{h}") for h in range(2)] for k in range(KC)]
                rt = [carve(512) for _ in range(2)]
                rtB = [Buf(f"rt{i}_{k}") for k in range(2)]
                SCRT["ytmp"] = [carve(512) for _ in range(2)]
                SCRB["ytmp"] = [Buf(f"ytmp{i}_{k}") for k in range(2)]
                phase(old, [b for r in UB for b in r] + rtB + SCRB["ytmp"] + [b for r in HTB for b in r])
                make_hT(lambda kc: sm[:, 32 + kc:33 + kc], lambda kc: modT[i][:, 48 + kc:49 + kc], [modB[i], smB],
                        list(range(NTILE)), 0, lambda kc, g0: HTB[kc][g0 // 4])
                for g in range(4):
                    for q in range(4):
                        w, wB = wget((f"w1_{i}", g * 4 + q))
                        for n in range(4):
                            hc = q * 4 + n
                            bcol = PP_B1 + 64 * i + g * 16 + hc
                            for h in range(2):
                                pu, puB = nextps()
                                for kc in range(KC):
                                    P.op("pe", lambda e, pu=pu, kc=kc, n=n, h=h, w=w: e.matmul(
                                        pu[:], lhsT=w[:, kc, n * 128:(n + 1) * 128], rhs=HT[:, kc, h * 512:(h + 1) * 512],
                                        start=(kc == 0), stop=(kc == KC - 1)), [wB, HTB[kc][h]], [puB])
                                k = (hc * 2 + h) % 2
                                P.op("act", lambda e, pu=pu, k=k, bcol=bcol: e.activation(
                                    out=rt[k], in_=pu[:], func=AF.Relu, bias=pp(bcol), scale=1.0), [puB, ppB], [rtB[k]])
                                P.op("pool", lambda e, k=k, hc=hc, h=h: e.tensor_tensor(
                                    out=U[:, hc, h * 512:(h + 1) * 512], in0=rt[k], in1=rt[k], op=ALU.mult),
                                    [rtB[k]], [UB[hc][h]])
                    for ng in range(4):
                        w, wB = wget((f"w2_{i}", g * 4 + ng))
                        gbt, gbB = BC[(ng % 2) * 2], BCB[(ng % 2) * 2]
                        gt, gB = BC[(ng % 2) * 2 + 1], BCB[(ng % 2) * 2 + 1]
                        bcast_cols(gt[:], gB, lambda c: modT[i][:, 80 + c:81 + c], [modB[i]], ng)
                        if g == 0:
                            bcast_cols(gbt[:], gbB, lambda c: sm[:, 48 + c:49 + c], [smB], ng)
                            prescale_X(ng, gbt[:], gbB, list(range(NTILE)))
                        for t in range(NTILE):
                            py, pyB = nextps()
                            for hc in range(KC):
                                P.op("pe", lambda e, py=py, hc=hc, t=t, w=w: e.matmul(
                                    py[:], lhsT=U[:, hc, t * 128:(t + 1) * 128], rhs=w[:, hc, :],
                                    start=(hc == 0), stop=(hc == KC - 1)), [UB[hc][t // 4], wB], [pyB])
                            evac_y(py, pyB, t, ng, gt[:], gB)
                    if extra_sched is not None:
                        extra_sched(g)
                lg = ln_ffn_g[i:i + 1, :]
                lb = ln_ffn_b[i:i + 1, :]
                layer_norm_X(lg, lb, final=final)
                return [b for r in UB for b in r] + rtB + SCRB["ytmp"]

            old = []
            if attn_only:
                compute_mod(1, range(0, 24))
            if stage >= 2 and not attn_only:
                def sched_ada1(g):
                    compute_mod(1, range(g * 6, g * 6 + 6))

                old = mlp(0, conv_bufs, extra_sched=sched_ada1 if stage >= 3 else None, final=(stage == 2))
            if stage >= 3:

                TWO_PI = 2.0 * math.pi
                off[0] = base_off
                cs_all = carve(2048); cosT = cs_all[:, 0:1024]; sinT = cs_all[:, 1024:2048]; csB = Buf("cs")
                tq = [carve(512) for _ in range(3)]; tqB = [Buf(f"tq{k}") for k in range(3)]
                stg = [carve(256).bitcast(BF16) for _ in range(2)]; stgB = [Buf(f"stg{k}") for k in range(2)]
                Qh = carve(1024).bitcast(BF16).rearrange("p (m t) -> p m t", m=2); QhB = Buf("Qh")
                Kh = carve(2048).bitcast(BF16).rearrange("p (m t) -> p m t", m=2); KhB = Buf("Kh")
                Vh = carve(2056).bitcast(BF16).rearrange("p (t e) -> p t e", t=16); VhB = Buf("Vh")
                ETf = carve(3712); ET = ETf.bitcast(BF16); ETB = Buf("ET")
                oacc = carve(514).rearrange("p (m e) -> p m e", m=2); oaB = Buf("oacc")
                osm = carve(16); osB = Buf("osm")
                ob = carve(128).bitcast(BF16); obB = Buf("ob")
                subg = carve(256); sgB = Buf("subg")
                lqf = carve(512); lqt = lqf.rearrange("p (a b) -> p a b", a=4); lqB = Buf("lqt")
                SCRT["o1"] = [cs_all[:, j * 257:(j + 1) * 257] for j in range(4)]
                SCRB["o1"] = [csB] * 4
                SCRT["ytmp"] = [tq[0], tq[1]]
                SCRB["ytmp"] = [tqB[0], tqB[1]]
                attn_bufs = [csB, QhB, KhB, VhB, ETB, oaB, osB, obB, sgB, lqB] + tqB + stgB
                phase(old, attn_bufs + [b for r in HTB for b in r])
                kvlB = [Buf(f"kvl{h}") for h in range(NHEAD)]; kvaB = [Buf(f"kva{h}") for h in range(NHEAD)]; qscB = Buf("qsc")

                for a4 in range(4):
                    P.dma(lqt[:, a4, :], lqk_d[a4:a4 + 1, :].to_broadcast([128, 128]), [], [lqB], q="sp")
                P.dma(subg, subg_d.to_broadcast([128, 256]), [], [sgB], q="sp")
                P.op("dve", lambda e: e.tensor_tensor(out=lqt[:, 0, :], in0=lqt[:, 0, :], in1=lqt[:, 1, :], op=ALU.mult), [lqB], [lqB])
                P.op("dve", lambda e: e.tensor_tensor(out=lqt[:, 2, :], in0=lqt[:, 2, :], in1=lqt[:, 3, :], op=ALU.mult), [lqB], [lqB])
                P.op("dve", lambda e: e.tensor_reduce(out=osm[:, 0:1], in_=lqt[:, 0, :], axis=mybir.AxisListType.X, op=ALU.add), [lqB], [osB])
                P.op("dve", lambda e: e.tensor_reduce(out=osm[:, 1:2], in_=lqt[:, 2, :], axis=mybir.AxisListType.X, op=ALU.add), [lqB, osB], [osB])
                P.op("act", lambda e: e.activation(out=osm[:, 2:4], in_=osm[:, 0:2], func=AF.Exp), [osB], [osB])
                P.op("dve", lambda e: e.tensor_tensor(out=osm[:, 4:5], in0=osm[:, 3:4], in1=osm[:, 2:3], op=ALU.subtract), [osB], [osB])
                P.op("dve", lambda e: e.tensor_scalar(out=osm[:, 4:5], in0=osm[:, 4:5], scalar1=-LAMBDA_INIT1, scalar2=None, op0=ALU.add), [osB], [osB])
                P.op("dve", lambda e: e.tensor_scalar(out=subg, in0=subg, scalar1=(1.0 - LAMBDA_INIT1), scalar2=None, op0=ALU.mult), [sgB], [sgB])
                P.op("dve", lambda e: e.memset(Vh[:, :, 256:257], 1.0), [], [VhB])

                posi = ETf.bitcast(I32)[:, 0:1024]
                P.dma(posi, pos_d.to_broadcast([128, 1024]), [], [ETB], q="sp")
                for (dst, add) in ((sinT, 0.0), (cosT, 0.25)):
                    for h in range(2):
                        sl = slice(h * 512, (h + 1) * 512)
                        P.op("dve", lambda e, sl=sl: e.tensor_copy(out=tq[0], in_=posi[:, sl]), [ETB], [tqB[0]])
                        P.op("dve", lambda e, add=add: e.tensor_scalar(out=tq[0], in0=tq[0], scalar1=pp(PP_INVF), scalar2=1.0 / TWO_PI,
                                                                        op0=ALU.mult, op1=ALU.mult), [tqB[0], ppB], [tqB[0]])
                        if add:
                            P.op("dve", lambda e, add=add: e.tensor_scalar(out=tq[0], in0=tq[0], scalar1=add, scalar2=None, op0=ALU.add), [tqB[0]], [tqB[0]])
                        ti = tq[1].bitcast(I32)
                        P.op("dve", lambda e: e.tensor_copy(out=ti, in_=tq[0]), [tqB[0]], [tqB[1]])
                        P.op("dve", lambda e: e.tensor_copy(out=tq[2], in_=ti), [tqB[1]], [tqB[2]])
                        P.op("dve", lambda e: e.tensor_tensor(out=tq[0], in0=tq[0], in1=tq[2], op=ALU.subtract), [tqB[0], tqB[2]], [tqB[0]])
                        P.op("dve", lambda e: e.tensor_scalar(out=tq[2], in0=tq[0], scalar1=0.5, scalar2=None, op0=ALU.is_gt), [tqB[0]], [tqB[2]])
                        P.op("dve", lambda e: e.tensor_tensor(out=tq[0], in0=tq[0], in1=tq[2], op=ALU.subtract), [tqB[0], tqB[2]], [tqB[0]])
                        P.op("dve", lambda e: e.tensor_scalar(out=tq[2], in0=tq[0], scalar1=-0.5, scalar2=None, op0=ALU.is_lt), [tqB[0]], [tqB[2]])
                        P.op("dve", lambda e: e.tensor_tensor(out=tq[0], in0=tq[0], in1=tq[2], op=ALU.add), [tqB[0], tqB[2]], [tqB[0]])
                        P.op("act", lambda e, dst=dst, sl=sl: e.activation(out=dst[:, sl], in_=tq[0], func=AF.Sin, scale=TWO_PI), [tqB[0]], [csB])

                P.op("dve", lambda e: e.tensor_scalar(out=sm[:, 0:16], in0=modT[1][:, 16:32], scalar1=1.0, scalar2=None, op0=ALU.add), [modB[1]], [smB])
                make_hT(lambda kc: sm[:, kc:kc + 1], lambda kc: modT[1][:, kc:kc + 1], [modB[1], smB],
                        list(range(NTILE)), 0, lambda kc, g0: HTB[kc][g0 // 4])
                for t in range(NTILE):
                    for ng in range(4):
                        P.op("pool", lambda e, t=t, ng=ng: e.tensor_scalar(
                            out=X[:, t, ng * 512:(ng + 1) * 512], in0=X[:, t, ng * 512:(ng + 1) * 512], scalar1=ALPHA, scalar2=None,
                            op0=ALU.mult), [XB[t][ng]], [XB[t][ng]])
                sidx = [0]
                for q in range(8):
                    w, wB = wget(("qkv", q))
                    for n in range(4):
                        mp = (q % 4) * 4 + n
                        for h in range(2):
                            pq, pqB = nextps()
                            for kc in range(KC):
                                P.op("pe", lambda e, pq=pq, kc=kc, n=n, h=h, w=w: e.matmul(
                                    pq[:], lhsT=w[:, kc, n * 128:(n + 1) * 128], rhs=HT[:, kc, h * 512:(h + 1) * 512],
                                    start=(kc == 0), stop=(kc == KC - 1)), [wB, HTB[kc][h]], [pqB])
                            k = sidx[0] % 2
                            sidx[0] += 1
                            P.op("act", lambda e, pq=pq: e.activation(out=tq[2], in_=pq[:], func=AF.Copy), [pqB], [tqB[2]])
                            pr, prB = nextps()
                            P.op("pe", lambda e, pr=pr: e.matmul(pr[:], lhsT=PT, rhs=tq[2], start=True, stop=True), [cstB, tqB[2]], [prB])
                            sl = slice(h * 512, (h + 1) * 512)
                            P.op("dve", lambda e, sl=sl: e.tensor_tensor(out=tq[2], in0=tq[2], in1=cosT[:, sl], op=ALU.mult), [tqB[2], csB], [tqB[2]])
                            P.op("dve", lambda e, pr=pr, sl=sl, k=k: e.tensor_tensor(out=tq[k], in0=pr[:], in1=sinT[:, sl], op=ALU.mult), [prB, csB], [tqB[k]])
                            P.op("dve", lambda e, k=k: e.tensor_tensor(out=stg[k], in0=tq[k], in1=tq[2], op=ALU.add), [tqB[k], tqB[2]], [stgB[k]])
                            if q < 4:
                                P.dma(qsc_d[mp * 128:(mp + 1) * 128, sl], stg[k], [stgB[k]], [qscB], q="sp")
                            else:
                                hd, m = mp // 2, mp % 2
                                P.dma(kvl_h[hd][0:128, m * 1024 + h * 512:m * 1024 + (h + 1) * 512], stg[k], [stgB[k]], [kvlB[hd]], q="sp")
                for q in range(8, 12):
                    w, wB = wget(("qkv", q))
                    for t in range(NTILE):
                        pv, pvB = nextps()
                        for kc in range(KC):
                            P.op("pe", lambda e, pv=pv, kc=kc, t=t, w=w: e.matmul(
                                pv[:], lhsT=HT[:, kc, t * 128:(t + 1) * 128], rhs=w[:, kc, :],
                                start=(kc == 0), stop=(kc == KC - 1)), [HTB[kc][t // 4], wB], [pvB])
                        k = sidx[0] % 2
                        sidx[0] += 1
                        P.op("act", lambda e, pv=pv, k=k: e.activation(out=stg[k], in_=pv[:], func=AF.Copy), [pvB], [stgB[k]])
                        for hh in range(2):
                            hd = (q - 8) * 2 + hh
                            P.dma(kvl_h[hd][128:256, t * 256:(t + 1) * 256], stg[k][:, hh * 256:(hh + 1) * 256],
                                  [stgB[k]], [kvlB[hd]], q="sp")
                    for hh in range(2):
                        hd = (q - 8) * 2 + hh
                        P.op("pool", lambda e, hd=hd: e.collective_compute(
                            "AllGather", ALU.bypass, replica_groups=[[0, 1], [2, 3], [4, 5], [6, 7]],
                            ins=[kvl_h[hd].opt()], outs=[kva_h[hd].opt()]), [kvlB[hd]], [kvaB[hd]], dma=True, inc=1)
                SCALE = HD ** -0.5
                OTB = [[Buf(f"OT{k}_{h}") for h in range(2)] for k in range(KC)]
                phase([b for r in HTB for b in r], [b for r in OTB for b in r])
                for hd in range(NHEAD):
                    P.dma(Qh[:], qsc_d[hd * 256:(hd + 1) * 256, :].rearrange("(m p) t -> p m t", p=128), [qscB], [QhB], q="sp")
                    P.dma(Kh[:, :, 0:1024], kva_h[hd][0:128, :].rearrange("p (m t) -> p m t", m=2), [kvaB[hd]], [KhB], q="sp")
                    P.dma(Kh[:, :, 1024:2048], kvl_h[hd][0:128, :].rearrange("p (m t) -> p m t", m=2), [kvlB[hd]], [KhB], q="sp")
                    P.dma(Vh[:, 0:8, 0:256], kva_h[hd][128:256, :].rearrange("p (t e) -> p t e", t=8), [kvaB[hd]], [VhB], q="sp")
                    P.dma(Vh[:, 8:16, 0:256], kvl_h[hd][128:256, :].rearrange("p (t e) -> p t e", t=8), [kvlB[hd]], [VhB], q="sp")
                    for qh in range(2):
                        q0 = qh * 512
                        for m in range(2):
                            kts = [(t, 0, True) for t in range(8)]
                            for t in range(4 * qh + 4):
                                kts.append((8 + t, max(0, t - 4 * qh), False))
                            eoff = {}
                            eo = 0
                            for (kt, j0, partner) in kts:
                                ncol = 512 - j0 * 128
                                eoff[kt] = (eo, j0)
                                pss, pssB = nextps()
                                P.op("pe", lambda e, pss=pss, kt=kt, m=m, j0=j0, ncol=ncol, q0=q0: e.matmul(
                                    pss[:, 0:ncol], lhsT=Kh[:, m, kt * 128:(kt + 1) * 128], rhs=Qh[:, m, q0 + j0 * 128:q0 + 512],
                                    start=True, stop=True), [KhB, QhB], [pssB])
                                if partner:
                                    P.op("act", lambda e, pss=pss, eo=eo, ncol=ncol: e.activation(
                                        out=ET[:, eo:eo + ncol], in_=pss[:, 0:ncol], func=AF.Exp, bias=pp(PP_FLAG + 2), scale=SCALE),
                                        [pssB, ppB], [ETB])
                                else:
                                    P.op("act", lambda e, pss=pss, eo=eo, ncol=ncol: e.activation(
                                        out=ET[:, eo:eo + ncol], in_=pss[:, 0:ncol], func=AF.Exp, scale=SCALE), [pssB], [ETB])
                                    if (kt - 8) >= 4 * qh:
                                        P.op("dve", lambda e, eo=eo: e.tensor_tensor(out=ET[:, eo:eo + 128], in0=ET[:, eo:eo + 128],
                                                                                     in1=tri, op=ALU.mult), [ETB, cstB], [ETB])
                                eo += ncol
                            for j in range(4):
                                po, poB = nextps()
                                use = [(kt, eoff[kt]) for (kt, j0, _) in kts if j0 <= j]
                                for ui, (kt, (eo, j0)) in enumerate(use):
                                    c0 = eo + (j - j0) * 128
                                    P.op("pe", lambda e, po=po, kt=kt, c0=c0, ui=ui, nu=len(use): e.matmul(
                                        po[:, 0:257], lhsT=ET[:, c0:c0 + 128], rhs=Vh[:, kt, :], start=(ui == 0), stop=(ui == nu - 1)),
                                        [ETB, VhB], [poB])
                                jj = qh * 4 + j
                                if m == 0:
                                    oq = SCRT["o1"][j]
                                    P.op("act", lambda e, po=po, oq=oq: e.activation(out=oq, in_=po[:, 0:257], func=AF.Copy), [poB], [SCRB["o1"][j]])
                                else:
                                    oq = SCRT["o1"][j]
                                    oqB = SCRB["o1"][j]
                                    P.op("dve", lambda e, oq=oq: e.reciprocal(out=osm[:, 8:9], in_=oq[:, 256:257]), [oqB, osB], [osB])
                                    P.op("dve", lambda e, po=po: e.reciprocal(out=osm[:, 9:10], in_=po[:, 256:257]), [poB, osB], [osB])
                                    P.op("dve", lambda e: e.tensor_tensor(out=osm[:, 9:10], in0=osm[:, 9:10], in1=osm[:, 4:5], op=ALU.mult), [osB], [osB])
                                    P.op("dve", lambda e, oq=oq: e.tensor_scalar(out=oq[:, 0:256], in0=oq[:, 0:256], scalar1=osm[:, 8:9], scalar2=None,
                                                                                  op0=ALU.mult), [oqB, osB], [oqB])
                                    P.op("dve", lambda e, oq=oq, po=po: e.scalar_tensor_tensor(out=oq[:, 0:256], in0=po[:, 0:256], scalar=osm[:, 9:10],
                                                                                               in1=oq[:, 0:256], op0=ALU.mult, op1=ALU.add),
                                         [poB, osB, oqB], [oqB])
                                    P.op("dve", lambda e, oq=oq: e.tensor_tensor(out=oacc[:, 0, 0:256], in0=oq[:, 0:256], in1=oq[:, 0:256], op=ALU.mult),
                                         [oqB], [oaB])
                                    P.op("dve", lambda e: e.tensor_reduce(out=osm[:, 10:11], in_=oacc[:, 0, 0:256], axis=mybir.AxisListType.X, op=ALU.add),
                                         [oaB, osB], [osB])
                                    P.op("dve", lambda e: e.tensor_scalar(out=osm[:, 10:11], in0=osm[:, 10:11], scalar1=1.0 / 256, scalar2=RMS_EPS,
                                                                          op0=ALU.mult, op1=ALU.add), [osB], [osB])
                                    P.op("act", lambda e: e.activation(out=osm[:, 11:12], in_=osm[:, 10:11], func=AF.Sqrt), [osB], [osB])
                                    P.op("dve", lambda e: e.reciprocal(out=osm[:, 12:13], in_=osm[:, 11:12]), [osB], [osB])
                                    P.op("dve", lambda e, oq=oq: e.scalar_tensor_tensor(out=ob, in0=oq[:, 0:256], scalar=osm[:, 12:13], in1=subg,
                                                                                       op0=ALU.mult, op1=ALU.mult), [oqB, osB, sgB], [obB])
                                    for cc in range(2):
                                        pt, ptB = PSB16[cc % 2], PB[6 + cc % 2]
                                        P.op("pe", lambda e, pt=pt, cc=cc: e.transpose(pt[:, 0:128], ob[:, cc * 128:(cc + 1) * 128], identb[:]),
                                             [obB, identbB], [ptB])
                                        P.op("act", lambda e, pt=pt, cc=cc, jj=jj, hd=hd: e.activation(
                                            out=HT[:, hd * 2 + cc, jj * 128:(jj + 1) * 128], in_=pt[:, 0:128], func=AF.Copy),
                                            [ptB], [OTB[hd * 2 + cc][jj // 4]])
                for ng in range(4):
                    w, wB = wget(("ow", ng))
                    gt, gB = BC[(ng % 2) * 2 + 1], BCB[(ng % 2) * 2 + 1]
                    bcast_cols(gt[:], gB, lambda c: modT[1][:, 32 + c:33 + c], [modB[1]], ng)
                    for t in range(NTILE):
                        py, pyB = nextps()
                        for kc in range(KC):
                            P.op("pe", lambda e, py=py, kc=kc, t=t, w=w: e.matmul(
                                py[:], lhsT=HT[:, kc, t * 128:(t + 1) * 128], rhs=w[:, kc, :],
                                start=(kc == 0), stop=(kc == KC - 1)), [OTB[kc][t // 4], wB], [pyB])
                        evac_y(py, pyB, t, ng, gt[:], gB)
                layer_norm_X(ln_mix_g[1:2, :], ln_mix_b[1:2, :], final=attn_only)
                if not attn_only:
                    derive(1, "f", PP_B2 + 16)
                    HTB = [[Buf(f"HTb{k}_{h}") for h in range(2)] for k in range(KC)]
                    mlp(1, attn_bufs + [b for r in OTB for b in r], final=True)


        except _Stop:
            pass
        P.final_wait([outB])
        st = P.emit()
    return nc, st


def _pcol(v):
    v = np.asarray(v, np.float32).reshape(-1, 128)
    return np.ascontiguousarray(v.T)


def make_in_maps(inp, stage=99, attn_only=False):
    x = np.asarray(inp["x"], np.float32)
    c = np.asarray(inp["c"], np.float32)
    pos = np.asarray(inp["positions"], np.int32)
    cst = np.zeros((128, 640), np.float32)
    cst[:, 0:128] = np.eye(128, dtype=np.float32)
    cst[:, 128:256] = 1.0
    PT = np.zeros((128, 128), np.float32)
    for do in range(64):
        PT[do + 64, do] = -1.0
        PT[do, do + 64] = 1.0
    cst[:, 256:384] = PT
    kk = np.arange(128)[:, None]
    qq = np.arange(128)[None, :]
    cst[:, 384:512] = (kk <= qq).astype(np.float32)
    inv_freq = (10000.0 ** (-np.arange(0, 128, 2, dtype=np.float32) / 128.0)).astype(np.float32)

    shared = {}
    for k in ("ada_w", "ln_mix_g", "ln_mix_b", "ln_ffn_g", "ln_ffn_b", "conv_pw1_w", "conv_pw2_w",
              "attn_qkv_w", "attn_o_w", "attn_subln_g", "mlp_w1", "mlp_w2"):
        shared[k] = np.ascontiguousarray(np.asarray(inp[k], np.float32))
    if stage < 3 and not attn_only:
        shared["ada_w"] = shared["ada_w"][0:1]
        shared["attn_qkv_w"] = shared["attn_qkv_w"][:, 0:1, 0:1]
        shared["attn_o_w"] = shared["attn_o_w"][:, 0:1, 0:1]
        shared["mlp_w1"] = shared["mlp_w1"][0:1]
        shared["mlp_w2"] = shared["mlp_w2"][0:1]
    if attn_only:
        shared["ada_w"] = np.asarray(inp["ada_w"], np.float32)[1:2]
        shared["attn_qkv_w"] = np.asarray(inp["attn_qkv_w"], np.float32)
        shared["attn_o_w"] = np.asarray(inp["attn_o_w"], np.float32)
        shared["conv_pw1_w"] = shared["conv_pw1_w"][:, 0:1, 0:1]
        shared["conv_pw2_w"] = shared["conv_pw2_w"][:, 0:1, 0:1]
    if stage < 2 or attn_only:
        shared["mlp_w1"] = shared["mlp_w1"][0:1, 0:1, 0:1]
        shared["mlp_w2"] = shared["mlp_w2"][0:1, 0:1, 0:1]
    shared = {k: np.ascontiguousarray(v) for k, v in shared.items()}
    lqk = np.concatenate([np.asarray(inp[k], np.float32).reshape(1, 128)
                          for k in ("attn_lq1", "attn_lk1", "attn_lq2", "attn_lk2")], 0)

    pbase = np.zeros((128, NPP), np.float32)
    pbase[:, PP_PW1B:PP_PW1B + 32] = _pcol(inp["conv_pw1_b"][0])
    pbase[:, PP_DWB:PP_DWB + 16] = _pcol(inp["conv_dw_b"][0])
    pbase[:, PP_CLNG:PP_CLNG + 16] = _pcol(inp["conv_ln_g"][0])
    pbase[:, PP_CLNB:PP_CLNB + 16] = _pcol(inp["conv_ln_b"][0])
    for i in range(2):
        pbase[:, PP_B1 + 64 * i:PP_B1 + 64 * (i + 1)] = _pcol(inp["mlp_b1"][i])
        pbase[:, PP_ADAB + 96 * i:PP_ADAB + 96 * (i + 1)] = _pcol(inp["ada_b"][i])
        pbase[:, PP_B2 + 16 * i:PP_B2 + 16 * (i + 1)] = _pcol(inp["mlp_b2"][i])
    dw = np.asarray(inp["conv_dw_w"], np.float32)[0]
    pbase[:, PP_DWW:PP_DWW + 16 * 31] = dw.T.reshape(16, 128, 31).transpose(1, 0, 2).reshape(128, 16 * 31)
    pbase[:, PP_INVF] = np.concatenate([inv_freq, inv_freq])
    pbase[:, PP_PW2B:PP_PW2B + 16] = _pcol(inp["conv_pw2_b"][0])

    maps = []
    for core in range(8):
        b, half = core // 2, core % 2
        t0 = half * NTOK
        pp = pbase.copy()
        pp[:, PP_FLAG + 0] = float(half)
        pp[:, PP_FLAG + 1] = 1.0
        pp[:, PP_FLAG + 2] = 0.0 if half == 1 else NEG
        pp[:, PP_CT:PP_CT + 16] = _pcol(c[b])
        xs = np.ascontiguousarray(x[b, t0:t0 + NTOK])
        xh = np.zeros((64, D), np.float32)
        if half == 1:
            xh[0:32] = x[b, t0 - 32:t0]
        xh[32:64] = x[b, t0 + 480:t0 + 512]
        m = {"x": xs, "xh": xh, "pos": np.ascontiguousarray(pos[b, t0:t0 + NTOK]).reshape(1, NTOK),
             "ppar": pp, "cst": cst, "lqk": lqk}
        m.update(shared)
        maps.append(m)
    return maps


_NC_CACHE = {}


def kernel(**inputs):
    stage = 99
    if stage not in _NC_CACHE:
        _NC_CACHE[stage] = build_nc(stage)[0]
    nc = _NC_CACHE[stage]
    maps = make_in_maps(inputs)
    res = run_bass_kernel_spmd(nc, maps, core_ids=list(range(8)))
    out = np.empty((4, 2048, D), np.float32)
    for core in range(8):
        b, half = core // 2, core % 2
        out[b, half * NTOK:(half + 1) * NTOK] = res.results[core]["out"]
    return out
```

```python
import math
from contextlib import ExitStack

import numpy as np
import concourse.bass as bass
import concourse.mybir as mybir
from concourse.bass_utils import run_bass_kernel_spmd

F32 = mybir.dt.float32
BF16 = mybir.dt.bfloat16
I32 = mybir.dt.int32
AF = mybir.ActivationFunctionType
ALU = mybir.AluOpType

ENGS = ("pe", "act", "dve", "pool", "sp")
SAME_ENG_SYNC = True

D = 2048
KC = 16
NTOK = 1024
NTILE = 8
DFF = 8192
ALPHA = 4.0 ** 0.25
LN_EPS = 1e-5
RMS_EPS = 1e-5
HD = 128
NHEAD = 8
LAMBDA_INIT1 = 0.8 - 0.6 * math.exp(-0.3 * 1)
NEG = -30000.0

PP_PW1B = 0
PP_DWB = 32
PP_CLNG = 48
PP_CLNB = 64
PP_B1 = 80
PP_DWW = 208
PP_FLAG = 704
PP_CT = 708
PP_CTA = 728
PP_ADABS = 792
PP_SEL = 816
PP_INVF = 724
PP_ADAB = 728
PP_PW2B = 920
PP_B2 = 936
NPP = 968


class Buf:
    __slots__ = ("name", "w", "rs", "dsem", "dcnt")

    def __init__(self, name):
        self.name = name
        self.w = None
        self.rs = {}
        self.dsem = None
        self.dcnt = 0


class Ins:
    __slots__ = ("eng", "fn", "deps", "sem", "val", "isdma", "needed", "inc")

    def __init__(self, eng, fn, isdma):
        self.eng = eng
        self.fn = fn
        self.deps = []
        self.sem = None
        self.val = 0
        self.isdma = isdma
        self.needed = False
        self.inc = 16


class Prog:
    def __init__(self, nc, es):
        self.nc = nc
        self.es = es
        self.streams = {e: [] for e in ENGS}
        self.esem = {e: es.enter_context(nc.semaphore("s_" + e)) for e in ("pe", "act", "dve", "pool")}
        self.nsem = 4

    def op(self, eng, fn, reads=(), writes=(), dma=False, inc=16):
        ins = Ins(eng, fn, dma)
        ins.inc = inc
        deps = {}
        for b in reads:
            if b.w is not None:
                deps[id(b.w)] = b.w
        for b in writes:
            if b.w is not None:
                deps[id(b.w)] = b.w
            for r in b.rs.values():
                deps[id(r)] = r
        if dma:
            dst = writes[0]
            if dst.dsem is None:
                dst.dsem = self.es.enter_context(self.nc.semaphore("d_" + dst.name))
                self.nsem += 1
            dst.dcnt += inc
            ins.sem = dst.dsem
            ins.val = dst.dcnt
            ins.needed = True
        for d in deps.values():
            if d is ins:
                continue
            if (not d.isdma) and d.eng == eng and (eng == "pe" or not SAME_ENG_SYNC):
                continue
            ins.deps.append(d)
            d.needed = True
        k = ins.eng if not dma else ("d", id(ins.sem))
        for b in reads:
            b.rs[k] = ins
        for b in writes:
            b.w = ins
            b.rs = {}
        self.streams[eng].append(ins)
        return ins

    def dma(self, out, in_, reads, writes, q="sp", **kw):
        return self.op(q, lambda e: e.dma_start(out=out, in_=in_, **kw), reads, writes, dma=True)

    def final_wait(self, bufs, eng="sp"):
        return self.op(eng, lambda e: None, reads=list(bufs), writes=())

    def emit(self):
        nc = self.nc
        for e in ("pe", "act", "dve", "pool"):
            c = 0
            for ins in self.streams[e]:
                if ins.isdma:
                    continue
                if ins.needed:
                    c += 1
                    ins.sem = self.esem[e]
                    ins.val = c
        stats = {}

        def run(eng_obj, ename):
            known = {}
            nw = 0
            for ins in self.streams[ename]:
                req = {}
                for d in ins.deps:
                    k = id(d.sem)
                    if k not in req or req[k][1] < d.val:
                        req[k] = (d.sem, d.val)
                for k, (sem, val) in req.items():
                    if known.get(k, 0) >= val:
                        continue
                    eng_obj.wait_ge(sem, val)
                    known[k] = val
                    nw += 1
                r = ins.fn(eng_obj)
                if r is None:
                    continue
                if ins.isdma:
                    r.then_inc(ins.sem, ins.inc)
                elif ins.needed:
                    r.then_inc(ins.sem, 1)
            stats[ename] = (len(self.streams[ename]), nw)

        with nc.Block() as block:
            @block.tensor
            def _(e):
                run(e, "pe")

            @block.scalar
            def _(e):
                run(e, "act")

            @block.vector
            def _(e):
                run(e, "dve")

            @block.gpsimd
            def _(e):
                run(e, "pool")

            @block.sync
            def _(e):
                run(e, "sp")
        return stats


class _Stop(Exception):
    pass


def build_nc(stage=99, dbg=None, attn_only=False):
    nc = bass.Bass("TRN2", target_bir_lowering=False)

    def din(name, shape, dt=F32):
        return nc.dram_tensor(name, list(shape), dt, kind="ExternalInput").ap()

    x_d = din("x", [NTOK, D])
    xh_d = din("xh", [64, D])
    pos_d = din("pos", [1, NTOK], I32)
    ppar_d = din("ppar", [128, NPP])
    cst_d = din("cst", [128, 640])
    lqk_d = din("lqk", [4, 128])
    ada_s = din("ada_s", [2, D, 1536])
    modl_d = nc.dram_tensor("modl", [128, 96], F32).ap()
    moda_d = nc.dram_tensor("moda", [1024, 96], F32).ap()
    ln_mix_g = din("ln_mix_g", [2, D]); ln_mix_b = din("ln_mix_b", [2, D])
    ln_ffn_g = din("ln_ffn_g", [2, D]); ln_ffn_b = din("ln_ffn_b", [2, D])
    pw1_w = din("conv_pw1_w", [1, D, 2 * D] if not attn_only else [1, 1, 1])
    pw2_w = din("conv_pw2_w", [1, D, D] if not attn_only else [1, 1, 1])
    qkv_w = din("attn_qkv_w", [1, D, 3 * D] if stage >= 3 else [1, 1, 1])
    o_w = din("attn_o_w", [1, D, D] if stage >= 3 else [1, 1, 1])
    subg_d = din("attn_subln_g", [1, 256])
    nl = 2 if stage >= 3 else 1
    w1_d = din("mlp_w1", [nl, D, DFF] if (stage >= 2 and not attn_only) else [1, 1, 1])
    w2_d = din("mlp_w2", [nl, DFF, D] if (stage >= 2 and not attn_only) else [1, 1, 1])
    out_d = nc.dram_tensor("out", [NTOK, D], F32, kind="ExternalOutput").ap()
    kvl_h = [nc.dram_tensor(f"kvl{h}", [256, 2048], BF16).ap() for h in range(NHEAD)]
    kva_h = [nc.dram_tensor(f"kva{h}", [512, 2048], BF16).ap() for h in range(NHEAD)]
    qsc_d = nc.dram_tensor("qsc", [16 * 128, NTOK], BF16).ap()

    with ExitStack() as es:
        P = Prog(nc, es)

        def sb(name, shape, dt=F32):
            return es.enter_context(nc.sbuf_tensor("sb_" + name, list(shape), dt))

        X = sb("X", [128, NTILE, D])
        XB = [[Buf(f"X{t}_{g}") for g in range(4)] for t in range(NTILE)]
        HT = sb("HT", [128, KC, 1088], BF16)
        NSLOT = 2
        WS = [sb(f"WS{i}", [128, KC, 512], BF16) for i in range(NSLOT)]
        WSB = [Buf(f"WS{i}") for i in range(NSLOT)]
        BC = [sb(f"BC{i}", [128, 512]) for i in range(4)]
        BCB = [Buf(f"BC{i}") for i in range(4)]
        ppar = sb("ppar", [128, NPP]); ppB = Buf("ppar")
        cst = sb("cst", [128, 640]); cstB = Buf("cst")
        identb = sb("identb", [128, 128], BF16); identbB = Buf("identb")
        onesb = sb("onesb", [128, 128], BF16); onesbB = Buf("onesb")
        condT = sb("condT", [128, KC, 4], BF16); condB = Buf("condT")
        modp = sb("modp", [128, 24, 4]); modpB = Buf("modp")
        modallB = Buf("modall")
        modT = [sb(f"modT{i}", [128, 96]) for i in range(2)]
        modB = [Buf(f"modT{i}") for i in range(2)]
        sm = sb("sm", [128, 256]); smB = Buf("sm")
        junk = sb("junk", [128, 8]); junkB = Buf("junk")
        SCR = sb("SCR", [128, 14880])
        PS = [es.enter_context(nc.psum_tensor(f"ps{i}", [128, 512], F32)) for i in range(8)]
        PB = [Buf(f"ps{i}") for i in range(8)]
        PSB16 = [PS[6][:].bitcast(BF16), PS[7][:].bitcast(BF16)]
        outB = Buf("out")

        ident = cst[:, 0:128]
        ones = cst[:, 128:256]
        PT = cst[:, 256:384]
        tri = cst[:, 384:512]

        def pp(c0, n=1):
            return ppar[:, c0:c0 + n]

        psr = [0]

        def nextps():
            i = psr[0] % 8
            psr[0] += 1
            return PS[i], PB[i]

        def dump(name, ap, bufs, ncols):
            if dbg == name:
                P.dma(out_d[0:128, 0:ncols], ap, list(bufs), [outB], q="pool")
                raise _Stop()

        try:
            P.dma(ppar[:], ppar_d, [], [ppB], q="sp")
            P.dma(cst[:], cst_d, [], [cstB], q="sp")
            for t in range(NTILE):
                P.dma(X[:, t, :], x_d[t * 128:(t + 1) * 128, :], [], XB[t], q="sp")
            P.op("dve", lambda e: e.tensor_copy(out=identb[:], in_=ident), [cstB], [identbB])
            P.op("dve", lambda e: e.tensor_copy(out=onesb[:], in_=ones), [cstB], [onesbB])
            P.op("act", lambda e: e.activation(out=condT[:].rearrange("p a b -> p (a b)"), in_=pp(PP_CTA, 64), func=AF.Silu), [ppB], [condB])

            PLAN = [("adas", k) for k in range(6)]
            if attn_only:
                PLAN += [("qkv", q) for q in range(12)] + [("ow", ng) for ng in range(4)]
            if not attn_only:
                for hf in range(2):
                    for jp in range(4):
                        PLAN.append(("pw1", jp)); PLAN.append(("pw1", 4 + jp))
                    for ng in range(4):
                        PLAN.append(("pw2", ng))
                if stage >= 2:
                    for g in range(4):
                        for q in range(4):
                            PLAN.append(("w1_0", g * 4 + q))
                        for ng in range(4):
                            PLAN.append(("w2_0", g * 4 + ng))
                if stage >= 3:
                    for q in range(12):
                        PLAN.append(("qkv", q))
                    for ng in range(4):
                        PLAN.append(("ow", ng))
                    for g in range(4):
                        for q in range(4):
                            PLAN.append(("w1_1", g * 4 + q))
                        for ng in range(4):
                            PLAN.append(("w2_1", g * 4 + ng))

            def piece_src(spec):
                kind, k = spec
                if kind == "adas":
                    m, r0, c0 = ada_s[k // 3], 0, (k % 3) * 512
                elif kind == "pw1":
                    m, r0, c0 = pw1_w[0], 0, (k % 4) * 512 + (2048 if k >= 4 else 0)
                elif kind == "pw2":
                    m, r0, c0 = pw2_w[0], 0, k * 512
                elif kind.startswith("w1_"):
                    m, r0, c0 = w1_d[int(kind[3])], 0, k * 512
                elif kind.startswith("w2_"):
                    m, r0, c0 = w2_d[int(kind[3])], (k // 4) * 2048, (k % 4) * 512
                elif kind == "qkv":
                    m, r0, c0 = qkv_w[0], 0, k * 512
                elif kind == "ow":
                    m, r0, c0 = o_w[0], 0, k * 512
                return m[r0:r0 + 2048, c0:c0 + 512].rearrange("(kc p) n -> p kc n", p=128)

            wst = {"issued": 0, "cur": 0}

            def wget(spec):
                i = wst["cur"]
                assert PLAN[i] == spec, (i, PLAN[i], spec)
                while wst["issued"] < min(i + NSLOT, len(PLAN)):
                    j = wst["issued"]
                    sl = j % NSLOT
                    P.dma(WS[sl][:], piece_src(PLAN[j]), [], [WSB[sl]], q="pool")
                    wst["issued"] += 1
                wst["cur"] += 1
                return WS[i % NSLOT], WSB[i % NSLOT]

            modall = HT[:, 0:2, :].rearrange("p a b -> p (a b)").bitcast(F32)[:, 0:768].rearrange(
                "p (r c b) -> p r c b", r=8, c=24)

            def compute_mod_all():
                ps, psB = nextps()
                for k in range(6):
                    w, wB = wget(("adas", k))
                    for n in range(4):
                        col = k * 4 + n
                        for kc in range(KC):
                            P.op("pe", lambda e, w=w, n=n, kc=kc, col=col: e.matmul(
                                ps[:, col * 4:col * 4 + 4], lhsT=w[:, kc, n * 128:(n + 1) * 128], rhs=condT[:, kc, :],
                                start=(kc == 0), stop=(kc == KC - 1)), [wB, condB], [psB])
                P.op("dve", lambda e: e.tensor_tensor(
                    out=modp[:], in0=ps[:, 0:96].rearrange("p (c b) -> p c b", b=4),
                    in1=pp(PP_ADABS, 24).unsqueeze(2).to_broadcast([128, 24, 4]), op=ALU.add), [psB, ppB], [modpB])
                modlB = Buf("modl"); modaB = Buf("moda")
                P.dma(modl_d, modp[:].rearrange("p c b -> p (c b)"), [modpB], [modlB], q="sp")
                P.op("pool", lambda e: e.collective_compute(
                    "AllGather", ALU.bypass, replica_groups=[list(range(8))],
                    ins=[modl_d.opt()], outs=[moda_d.opt()]), [modlB], [modaB], dma=True, inc=1)
                P.dma(modall[:].rearrange("p r c b -> p r (c b)"), moda_d.rearrange("(r p) c -> p r c", p=128),
                      [modaB], [modallB], q="sp")
                for i in range(2):
                    dst = modT[i][:].rearrange("p (r c) -> p r c", r=8)
                    P.op("dve", lambda e, i=i, dst=dst: e.tensor_scalar(
                        out=dst, in0=modall[:, :, i * 12:(i + 1) * 12, 0], scalar1=pp(PP_SEL + 0), scalar2=None,
                        op0=ALU.mult), [modallB, ppB], [modB[i]])
                    for bb in range(1, 4):
                        P.op("dve", lambda e, i=i, dst=dst, bb=bb: e.scalar_tensor_tensor(
                            out=dst, in0=modall[:, :, i * 12:(i + 1) * 12, bb], scalar=pp(PP_SEL + bb), in1=dst,
                            op0=ALU.mult, op1=ALU.add), [modallB, ppB, modB[i]], [modB[i]])

            def bcast_cols(dst, dstB, srcT_ap_fn, srcBs, ng):
                ps, psB = nextps()
                dg = SCRT["diag32"]
                dh = SCRT["diaghl"]
                for n in range(4):
                    c = ng * 4 + n
                    dB = SCRB["diag32"][0]
                    P.op("dve", lambda e, n=n, c=c: e.tensor_scalar(
                        out=dg, in0=ident, scalar1=srcT_ap_fn(c), scalar2=None, op0=ALU.mult),
                        [cstB] + srcBs, [dB])
                    P.op("dve", lambda e, n=n: e.tensor_copy(out=dh[:, 0, :], in_=dg), [dB], [dB])
                    P.op("dve", lambda e, n=n: e.tensor_tensor(out=dh[:, 1, :], in0=dg, in1=dh[:, 0, :],
                                                                op=ALU.subtract), [dB], [dB])
                    for hl in range(2):
                        P.op("pe", lambda e, ps=ps, n=n, hl=hl: e.matmul(
                            ps[:, n * 128:(n + 1) * 128], lhsT=onesb[:], rhs=dh[:, hl, :],
                            start=(hl == 0), stop=(hl == 1)), [onesbB, dB], [psB])
                P.op("act", lambda e, ps=ps: e.activation(out=dst, in_=ps[:], func=AF.Copy), [psB], [dstB])

            def make_hT(scale_col, shift_col, srcBs, tiles, col0, hbufs):
                for kc in range(KC):
                    for g0 in range(0, len(tiles), 4):
                        grp = tiles[g0:g0 + 4]
                        ps, psB = nextps()
                        for q, t in enumerate(grp):
                            P.op("pe", lambda e, ps=ps, q=q, t=t, kc=kc: e.transpose(
                                ps[:, q * 128:(q + 1) * 128], X[:, t, kc * 128:(kc + 1) * 128], ident),
                                [XB[t][kc // 4], cstB], [psB])
                        n = len(grp) * 128
                        c = col0 + g0 * 128
                        P.op("act", lambda e, ps=ps, kc=kc, n=n, c=c: e.activation(
                            out=HT[:, kc, c:c + n], in_=ps[:, 0:n], func=AF.Identity,
                            bias=shift_col(kc), scale=scale_col(kc)), [psB] + srcBs, [hbufs(kc, g0)])

            def prescale_X(ng, gb2_tile, gb2B, tiles):
                for t in tiles:
                    P.op("dve", lambda e, t=t: e.scalar_tensor_tensor(
                        out=X[:, t, ng * 512:(ng + 1) * 512], in0=X[:, t, ng * 512:(ng + 1) * 512], scalar=ALPHA,
                        in1=gb2_tile, op0=ALU.mult, op1=ALU.add), [XB[t][ng], gb2B], [XB[t][ng]])

            ytmp_i = [0]

            def evac_y(ps, psB, t, ng, g_tile, gB, alpha=None):
                k = ytmp_i[0] % 2
                ytmp_i[0] += 1
                yt = SCRT["ytmp"][k]
                ytB = SCRB["ytmp"][k]
                P.op("dve", lambda e: e.tensor_tensor(out=yt, in0=ps[:], in1=g_tile, op=ALU.mult), [psB, gB], [ytB])
                if alpha is not None:
                    P.op("dve", lambda e: e.scalar_tensor_tensor(
                        out=X[:, t, ng * 512:(ng + 1) * 512], in0=X[:, t, ng * 512:(ng + 1) * 512], scalar=alpha, in1=yt,
                        op0=ALU.mult, op1=ALU.add), [ytB, XB[t][ng]], [XB[t][ng]])
                    return
                P.op("pool", lambda e: e.tensor_tensor(out=X[:, t, ng * 512:(ng + 1) * 512],
                                                       in0=X[:, t, ng * 512:(ng + 1) * 512], in1=yt, op=ALU.add),
                     [ytB, XB[t][ng]], [XB[t][ng]])

            def layer_norm_X(g_row, b_row, final=False):
                st = SCRT["lnst"]
                stB = SCRB["lnst"]
                for t in range(NTILE):
                    for c in range(4):
                        P.op("dve", lambda e, t=t, c=c: e.bn_stats(out=st[:, t, c, :], in_=X[:, t, c * 512:(c + 1) * 512]),
                             [XB[t][c]], [stB[t]])
                    P.op("dve", lambda e, t=t: e.bn_aggr(out=st[:, t, 4, 0:2], in_=st[:, t, 0:4, :]), [stB[t]], [stB[t]])
                    P.op("dve", lambda e, t=t: e.tensor_scalar(out=st[:, t, 4, 2:3], in0=st[:, t, 4, 1:2], scalar1=LN_EPS,
                                                                scalar2=None, op0=ALU.add), [stB[t]], [stB[t]])
                    P.op("act", lambda e, t=t: e.activation(out=st[:, t, 4, 3:4], in_=st[:, t, 4, 2:3], func=AF.Sqrt),
                         [stB[t]], [stB[t]])
                    P.op("dve", lambda e, t=t: e.reciprocal(out=st[:, t, 4, 4:5], in_=st[:, t, 4, 3:4]), [stB[t]], [stB[t]])
                for ng in range(4):
                    gt, gB = BC[0 + (ng % 2) * 2], BCB[0 + (ng % 2) * 2]
                    bt, bB = BC[1 + (ng % 2) * 2], BCB[1 + (ng % 2) * 2]
                    P.dma(gt[:], g_row[:, ng * 512:(ng + 1) * 512].to_broadcast([128, 512]), [], [gB], q="sp")
                    P.dma(bt[:], b_row[:, ng * 512:(ng + 1) * 512].to_broadcast([128, 512]), [], [bB], q="sp")
                    for t in range(NTILE):
                        xs = X[:, t, ng * 512:(ng + 1) * 512]
                        P.op("dve", lambda e, xs=xs, t=t, gt=gt: e.scalar_tensor_tensor(
                            out=xs, in0=xs, scalar=st[:, t, 4, 0:1], in1=gt[:], op0=ALU.subtract, op1=ALU.mult),
                            [XB[t][ng], stB[t], gB], [XB[t][ng]])
                        P.op("dve", lambda e, xs=xs, t=t, bt=bt: e.scalar_tensor_tensor(
                            out=xs, in0=xs, scalar=st[:, t, 4, 4:5], in1=bt[:], op0=ALU.mult, op1=ALU.add),
                            [XB[t][ng], stB[t], bB], [XB[t][ng]])
                        if final:
                            P.dma(out_d[t * 128:(t + 1) * 128, ng * 512:(ng + 1) * 512], xs, [XB[t][ng]], [outB], q="sp")

            def phase(old, new):
                P.op("pool", lambda e: e.memset(junk[:, 0:1], 0.0), [], list(old) + list(new) + [junkB])

            SCRT = {}
            SCRB = {}
            off = [0]

            def carve(n_f32):
                a = off[0]
                off[0] += n_f32
                assert off[0] <= 14880, off[0]
                return SCR[:, a:a + n_f32]

            SCRT["diag32"] = carve(128)
            SCRT["diaghl"] = carve(128).bitcast(BF16).rearrange("p (h b) -> p h b", h=2)
            SCRB["diag32"] = [Buf(f"diag32_{n}") for n in range(4)]
            SCRT["lnst"] = carve(NTILE * 5 * 6).rearrange("p (t c s) -> p t c s", t=NTILE, c=5)
            SCRB["lnst"] = [Buf(f"lnst{t}") for t in range(NTILE)]
            base_off = off[0]


            def derive(i, which, b2col):
                base = 0 if which == "m" else 32
                mo = 0 if which == "m" else 48
                P.op("dve", lambda e: e.tensor_scalar(out=sm[:, base:base + 16], in0=modT[i][:, mo + 16:mo + 32],
                                                       scalar1=1.0, scalar2=None, op0=ALU.add), [modB[i]], [smB])
                P.op("dve", lambda e: e.tensor_tensor(out=sm[:, base + 16:base + 32], in0=modT[i][:, mo + 32:mo + 48],
                                                       in1=pp(b2col, 16), op=ALU.mult), [modB[i], ppB], [smB])

            compute_mod_all()
            if not attn_only:
                dump("condT", ppar[:, 0:64], [ppB, condB], 64)
                dump("mod0", modT[0][:], [modB[0]], 96)
                derive(0, "m", PP_PW2B)

                off[0] = base_off
                CT = carve(8192).rearrange("p (a b) -> p a b", a=16)
                CTB = [Buf(f"CT{k}") for k in range(KC)]
                dgc = carve(1984).bitcast(BF16).rearrange("p (a b) -> p a b", a=31)
                dgcB = [Buf("dgc0"), Buf("dgc1")]
                xh = carve(2048)
                xhB = Buf("xh")
                tmpA = [carve(544) for _ in range(2)]; tmpAB = [Buf(f"tmpA{k}") for k in range(2)]
                tmpS = [carve(512) for _ in range(2)]; tmpSB = [Buf(f"tmpS{k}") for k in range(2)]
                tmpSb = [t.bitcast(BF16).rearrange("p (a b) -> p a b", a=2) for t in tmpS]
                mean_t = tmpA[0][:, 0:512]; rstd_t = tmpA[1][:, 0:512]; lnB = Buf("convln")
                SCRT["ytmp"] = tmpS
                SCRB["ytmp"] = tmpSB
                CHB = [Buf(f"CH{k}") for k in range(KC)]
                CVB = [Buf(f"CV{k}") for k in range(KC)]
                conv_bufs = CTB + dgcB + [xhB, lnB] + tmpAB + tmpSB + CHB + CVB

                P.dma(xh[0:64, :], xh_d, [], [xhB], q="sp")


                for hf in range(2):
                    tiles = [4 * hf + q for q in range(4)]
                    for kc in range(KC):
                        ps, psB = nextps()
                        P.op("pe", lambda e, ps=ps, kc=kc, hf=hf: e.transpose(
                            ps[:, 0:32], xh[hf * 32:(hf + 1) * 32, kc * 128:(kc + 1) * 128],
                            cst[hf * 32:(hf + 1) * 32, hf * 32:(hf + 1) * 32]), [xhB, cstB], [psB])
                        P.op("act", lambda e, ps=ps, kc=kc: e.activation(
                            out=HT[:, kc, 0:32], in_=ps[:, 0:32], func=AF.Identity,
                            bias=modT[0][:, kc:kc + 1], scale=sm[:, kc:kc + 1]), [psB, modB[0], smB], [CHB[kc]])
                    make_hT(lambda kc: sm[:, kc:kc + 1], lambda kc: modT[0][:, kc:kc + 1], [modB[0], smB], tiles, 32,
                            lambda kc, g0: CHB[kc])
                    for ng in range(4):
                        if hf == 0:
                            gbt, gbB = BC[ng][:], BCB[ng]
                        else:
                            gbt, gbB = tmpA[ng % 2][:, 0:512], tmpAB[ng % 2]
                        bcast_cols(gbt, gbB, lambda c: sm[:, 16 + c:17 + c], [smB], ng)
                        prescale_X(ng, gbt, gbB, tiles)
                    if hf == 0:
                        for ng in range(4):
                            bcast_cols(BC[ng][:], BCB[ng], lambda c: modT[0][:, 32 + c:33 + c], [modB[0]], ng)
                    if hf == 0:
                        dump("hT0", HT[:, 0, 0:544], CHB, 544)
                    for jp in range(4):
                        wa, waB = wget(("pw1", jp))
                        ph, phB = PS[4], PB[4]
                        for n in range(4):
                            pa, paB = PS[n], PB[n]
                            for kc in range(KC):
                                P.op("pe", lambda e, pa=pa, kc=kc, n=n: e.matmul(
                                    pa[:], lhsT=wa[:, kc, n * 128:(n + 1) * 128], rhs=HT[:, kc, 32:544],
                                    start=(kc == 0), stop=(kc == KC - 1)), [waB, CHB[kc]], [paB])
                            for kc in range(KC):
                                P.op("pe", lambda e, kc=kc, n=n: e.matmul(
                                    ph[:, n * 32:(n + 1) * 32], lhsT=wa[:, kc, n * 128:(n + 1) * 128], rhs=HT[:, kc, 0:32],
                                    start=(kc == 0), stop=(kc == KC - 1)), [waB, CHB[kc]], [phB])
                        wg, wgB = wget(("pw1", 4 + jp))
                        for n in range(4):
                            j = jp * 4 + n
                            pa, paB = PS[n], PB[n]
                            pg, pgB = PS[5 + (n % 2)], PB[5 + (n % 2)]
                            for kc in range(KC):
                                P.op("pe", lambda e, pg=pg, kc=kc, n=n: e.matmul(
                                    pg[:], lhsT=wg[:, kc, n * 128:(n + 1) * 128], rhs=HT[:, kc, 32:544],
                                    start=(kc == 0), stop=(kc == KC - 1)), [wgB, CHB[kc]], [pgB])
                            for kc in range(KC):
                                P.op("pe", lambda e, kc=kc, n=n: e.matmul(
                                    ph[:, 128 + n * 32:128 + (n + 1) * 32], lhsT=wg[:, kc, n * 128:(n + 1) * 128],
                                    rhs=HT[:, kc, 0:32], start=(kc == 0), stop=(kc == KC - 1)), [wgB, CHB[kc]], [phB])
                            k = j % 2
                            ta, taB = tmpA[k], tmpAB[k]
                            P.op("act", lambda e, pg=pg, ta=ta, j=j: e.activation(
                                out=ta[:, 32:544], in_=pg[:], func=AF.Sigmoid, bias=pp(PP_PW1B + 16 + j), scale=1.0),
                                [pgB, ppB], [taB])
                            P.op("act", lambda e, ta=ta, j=j, n=n: e.activation(
                                out=ta[:, 0:32], in_=ph[:, 128 + n * 32:128 + (n + 1) * 32], func=AF.Sigmoid,
                                bias=pp(PP_PW1B + 16 + j), scale=1.0), [phB, ppB, taB], [taB])
                            P.op("dve", lambda e, pa=pa, ta=ta, j=j: e.scalar_tensor_tensor(
                                out=HT[:, j, 544 + 32:544 + 544], in0=pa[:], scalar=pp(PP_PW1B + j), in1=ta[:, 32:544],
                                op0=ALU.add, op1=ALU.mult), [paB, ppB, taB], [CVB[j]])
                            P.op("dve", lambda e, ta=ta, j=j, n=n: e.scalar_tensor_tensor(
                                out=ta[:, 0:32], in0=ph[:, n * 32:(n + 1) * 32], scalar=pp(PP_PW1B + j), in1=ta[:, 0:32],
                                op0=ALU.add, op1=ALU.mult), [phB, ppB, taB], [taB])
                            P.op("dve", lambda e, ta=ta, j=j, hf=hf: e.tensor_scalar(
                                out=HT[:, j, 544:544 + 32], in0=ta[:, 0:32], scalar1=pp(PP_FLAG + hf), scalar2=None,
                                op0=ALU.mult), [taB, ppB, CVB[j]], [CVB[j]])
                    if hf == 0:
                        dump("vT0", HT[:, 0, 544:1088], CVB, 544)
                    s1, s1B = PS[6], PB[6]
                    s2, s2B = PS[7], PB[7]
                    def build_diag(kc):
                        for dh_, (j0, j1) in enumerate(((0, 16), (16, 31))):
                            P.op("dve", lambda e, kc=kc, j0=j0, j1=j1: e.tensor_tensor(
                                out=dgc[:, j0:j1, :], in0=identb[:].unsqueeze(1).to_broadcast([128, j1 - j0, 128]),
                                in1=ppar[:, PP_DWW + kc * 31 + j0:PP_DWW + kc * 31 + j1].unsqueeze(2).to_broadcast([128, j1 - j0, 128]),
                                op=ALU.mult), [identbB, ppB], [dgcB[dh_]])

                    def stats_mm(kc):
                        cb = tmpSb[kc % 2]
                        P.op("pe", lambda e: e.matmul(s1[:], lhsT=onesb[:], rhs=cb[:, 0, :], start=(kc == 0),
                                                      stop=(kc == KC - 1)), [onesbB, tmpSB[kc % 2]], [s1B])
                        P.op("pe", lambda e: e.matmul(s2[:], lhsT=onesb[:], rhs=cb[:, 1, :], start=(kc == 0),
                                                      stop=(kc == KC - 1)), [onesbB, tmpSB[kc % 2]], [s2B])

                    build_diag(0)
                    pend = None
                    for kc in range(KC):
                        pc, pcB = PS[kc % 6], PB[kc % 6]
                        for j in range(31):
                            P.op("pe", lambda e, pc=pc, kc=kc, j=j: e.matmul(
                                pc[:], lhsT=dgc[:, j, :], rhs=HT[:, kc, 544 + 2 + j:544 + 2 + j + 512],
                                start=(j == 0), stop=(j == 30)), [dgcB[0 if j < 16 else 1], CVB[kc]], [pcB])
                        if kc + 1 < KC:
                            build_diag(kc + 1)
                        P.op("dve", lambda e, pc=pc, kc=kc: e.tensor_scalar(
                            out=CT[:, kc, :], in0=pc[:], scalar1=pp(PP_DWB + kc), scalar2=None, op0=ALU.add),
                            [pcB, ppB], [CTB[kc]])
                        k = kc % 2
                        cb = tmpSb[k]
                        P.op("dve", lambda e, kc=kc, cb=cb: e.tensor_copy(out=cb[:, 0, :], in_=CT[:, kc, :]),
                             [CTB[kc]], [tmpSB[k]])
                        P.op("dve", lambda e, kc=kc, cb=cb: e.tensor_tensor(out=cb[:, 1, :], in0=CT[:, kc, :],
                                                                             in1=CT[:, kc, :], op=ALU.mult),
                             [CTB[kc], tmpSB[k]], [tmpSB[k]])
                        if pend is not None:
                            stats_mm(pend)
                        pend = kc
                    stats_mm(pend)
                    if hf == 0:
                        dump("ct0", CT[:, 0, :], CTB, 512)
                    P.op("dve", lambda e: e.tensor_scalar(out=mean_t, in0=s1[:], scalar1=1.0 / D, scalar2=None,
                                                           op0=ALU.mult), [s1B], [lnB, tmpAB[0], tmpAB[1]])
                    P.op("dve", lambda e: e.tensor_tensor(out=tmpS[0], in0=mean_t, in1=mean_t, op=ALU.mult),
                         [lnB], [tmpSB[0]])
                    P.op("dve", lambda e: e.scalar_tensor_tensor(out=rstd_t, in0=s2[:], scalar=1.0 / D, in1=tmpS[0],
                                                                  op0=ALU.mult, op1=ALU.subtract), [s2B, tmpSB[0]], [lnB])
                    P.op("dve", lambda e: e.tensor_scalar(out=rstd_t, in0=rstd_t, scalar1=LN_EPS, scalar2=None,
                                                           op0=ALU.add), [lnB], [lnB])
                    P.op("act", lambda e: e.activation(out=rstd_t, in_=rstd_t, func=AF.Sqrt), [lnB], [lnB])
                    P.op("dve", lambda e: e.reciprocal(out=rstd_t, in_=rstd_t), [lnB], [lnB])
                    for kc in range(KC):
                        k = kc % 2
                        P.op("pool", lambda e, kc=kc, k=k: e.tensor_tensor(out=tmpS[k], in0=CT[:, kc, :], in1=mean_t,
                                                                            op=ALU.subtract), [CTB[kc], lnB], [tmpSB[k]])
                        P.op("dve", lambda e, k=k: e.tensor_tensor(out=tmpS[k], in0=tmpS[k], in1=rstd_t, op=ALU.mult),
                             [tmpSB[k], lnB], [tmpSB[k]])
                        P.op("act", lambda e, kc=kc, k=k: e.activation(
                            out=HT[:, kc, 32:544], in_=tmpS[k], func=AF.Silu, bias=pp(PP_CLNB + kc), scale=pp(PP_CLNG + kc)),
                            [tmpSB[k], ppB], [CHB[kc]])
                    if hf == 0:
                        dump("sT0", HT[:, 0, 32:544], CHB, 512)
                    if hf == 0:
                        derive(0, "f", PP_B2)
                    for ng in range(4):
                        w, wB = wget(("pw2", ng))
                        gt, gB = BC[ng], BCB[ng]
                        for q, t in enumerate(tiles):
                            py, pyB = nextps()
                            for kc in range(KC):
                                P.op("pe", lambda e, py=py, kc=kc, q=q, w=w: e.matmul(
                                    py[:], lhsT=HT[:, kc, 32 + q * 128:32 + (q + 1) * 128], rhs=w[:, kc, :],
                                    start=(kc == 0), stop=(kc == KC - 1)), [CHB[kc], wB], [pyB])
                            evac_y(py, pyB, t, ng, gt[:], gB)
                layer_norm_X(ln_mix_g[0:1, :], ln_mix_b[0:1, :], final=(stage == 1))
            HTB = [[Buf(f"HT{k}_{h}") for h in range(2)] for k in range(KC)]

            def mlp(i, old, extra_sched=None, final=False):
                off[0] = base_off
                U = carve(8192).bitcast(BF16).rearrange("p (a b) -> p a b", a=16)
                UB = [[Buf(f"U{i}_{k}_{h}") for h in range(2)] for k in range(KC)]
                rt = [carve(512) for _ in range(2)]
                rtB = [Buf(f"rt{i}_{k}") for k in range(2)]
                SCRT["ytmp"] = [carve(512) for _ in range(2)]
                SCRB["ytmp"] = [Buf(f"ytmp{i}_{k}") for k in range(2)]
                phase(old, [b for r in UB for b in r] + rtB + SCRB["ytmp"] + [b for r in HTB for b in r])
                make_hT(lambda kc: sm[:, 32 + kc:33 + kc], lambda kc: modT[i][:, 48 + kc:49 + kc], [modB[i], smB],
                        list(range(NTILE)), 0, lambda kc, g0: HTB[kc][g0 // 4])
                for ng in range(4):
                    bcast_cols(BC[ng][:], BCB[ng], lambda c: sm[:, 48 + c:49 + c], [smB], ng)
                    prescale_X(ng, BC[ng][:], BCB[ng], list(range(NTILE)))
                for ng in range(4):
                    bcast_cols(BC[ng][:], BCB[ng], lambda c: modT[i][:, 80 + c:81 + c], [modB[i]], ng)
                for g in range(4):
                    for q in range(4):
                        w, wB = wget((f"w1_{i}", g * 4 + q))
                        for n in range(4):
                            hc = q * 4 + n
                            bcol = PP_B1 + 64 * i + g * 16 + hc
                            for h in range(2):
                                pu, puB = nextps()
                                for kc in range(KC):
                                    P.op("pe", lambda e, pu=pu, kc=kc, n=n, h=h, w=w: e.matmul(
                                        pu[:], lhsT=w[:, kc, n * 128:(n + 1) * 128], rhs=HT[:, kc, h * 512:(h + 1) * 512],
                                        start=(kc == 0), stop=(kc == KC - 1)), [wB, HTB[kc][h]], [puB])
                                k = (hc * 2 + h) % 2
                                P.op("act", lambda e, pu=pu, k=k, bcol=bcol: e.activation(
                                    out=rt[k], in_=pu[:], func=AF.Relu, bias=pp(bcol), scale=1.0), [puB, ppB], [rtB[k]])
                                P.op("pool", lambda e, k=k, hc=hc, h=h: e.tensor_tensor(
                                    out=U[:, hc, h * 512:(h + 1) * 512], in0=rt[k], in1=rt[k], op=ALU.mult),
                                    [rtB[k]], [UB[hc][h]])
                    for ng in range(4):
                        w, wB = wget((f"w2_{i}", g * 4 + ng))
                        gt, gB = BC[ng], BCB[ng]
                        for t in range(NTILE):
                            py, pyB = nextps()
                            for hc in range(KC):
                                P.op("pe", lambda e, py=py, hc=hc, t=t, w=w: e.matmul(
                                    py[:], lhsT=U[:, hc, t * 128:(t + 1) * 128], rhs=w[:, hc, :],
                                    start=(hc == 0), stop=(hc == KC - 1)), [UB[hc][t // 4], wB], [pyB])
                            evac_y(py, pyB, t, ng, gt[:], gB)
                    if extra_sched is not None:
                        extra_sched(g)
                lg = ln_ffn_g[i:i + 1, :]
                lb = ln_ffn_b[i:i + 1, :]
                layer_norm_X(lg, lb, final=final)
                return [b for r in UB for b in r] + rtB + SCRB["ytmp"]

            old = []
            if stage >= 2 and not attn_only:
                old = mlp(0, conv_bufs, final=(stage == 2))
            if stage >= 3:

                TWO_PI = 2.0 * math.pi
                off[0] = base_off
                cs_all = carve(2048); cosT = cs_all[:, 0:1024]; sinT = cs_all[:, 1024:2048]; csB = Buf("cs")
                tq = [carve(512) for _ in range(3)]; tqB = [Buf(f"tq{k}") for k in range(3)]
                stg = [carve(256).bitcast(BF16) for _ in range(2)]; stgB = [Buf(f"stg{k}") for k in range(2)]
                Qh = carve(1024).bitcast(BF16).rearrange("p (m t) -> p m t", m=2); QhB = Buf("Qh")
                Kh = carve(2048).bitcast(BF16).rearrange("p (m t) -> p m t", m=2); KhB = Buf("Kh")
                Vh = carve(2056).bitcast(BF16).rearrange("p (t e) -> p t e", t=16); VhB = Buf("Vh")
                ETf = carve(3712); ET = ETf.bitcast(BF16); ETB = Buf("ET")
                oacc = carve(514).rearrange("p (m e) -> p m e", m=2); oaB = Buf("oacc")
                osm = carve(16); osB = Buf("osm")
                ob = carve(128).bitcast(BF16); obB = Buf("ob")
                subg = carve(256); sgB = Buf("subg")
                lqf = carve(512); lqt = lqf.rearrange("p (a b) -> p a b", a=4); lqB = Buf("lqt")
                SCRT["o1"] = [cs_all[:, j * 257:(j + 1) * 257] for j in range(4)]
                SCRB["o1"] = [csB] * 4
                SCRT["ytmp"] = [tq[0], tq[1]]
                SCRB["ytmp"] = [tqB[0], tqB[1]]
                attn_bufs = [csB, QhB, KhB, VhB, ETB, oaB, osB, obB, sgB, lqB] + tqB + stgB
                phase(old, attn_bufs + [b for r in HTB for b in r])
                kvlB = [Buf(f"kvl{h}") for h in range(NHEAD)]; kvaB = [Buf(f"kva{h}") for h in range(NHEAD)]; qscB = Buf("qsc")

                for a4 in range(4):
                    P.dma(lqt[:, a4, :], lqk_d[a4:a4 + 1, :].to_broadcast([128, 128]), [], [lqB], q="sp")
                P.dma(subg, subg_d.to_broadcast([128, 256]), [], [sgB], q="sp")
                P.op("dve", lambda e: e.tensor_tensor(out=lqt[:, 0, :], in0=lqt[:, 0, :], in1=lqt[:, 1, :], op=ALU.mult), [lqB], [lqB])
                P.op("dve", lambda e: e.tensor_tensor(out=lqt[:, 2, :], in0=lqt[:, 2, :], in1=lqt[:, 3, :], op=ALU.mult), [lqB], [lqB])
                P.op("dve", lambda e: e.tensor_reduce(out=osm[:, 0:1], in_=lqt[:, 0, :], axis=mybir.AxisListType.X, op=ALU.add), [lqB], [osB])
                P.op("dve", lambda e: e.tensor_reduce(out=osm[:, 1:2], in_=lqt[:, 2, :], axis=mybir.AxisListType.X, op=ALU.add), [lqB, osB], [osB])
                P.op("act", lambda e: e.activation(out=osm[:, 2:4], in_=osm[:, 0:2], func=AF.Exp), [osB], [osB])
                P.op("dve", lambda e: e.tensor_tensor(out=osm[:, 4:5], in0=osm[:, 3:4], in1=osm[:, 2:3], op=ALU.subtract), [osB], [osB])
                P.op("dve", lambda e: e.tensor_scalar(out=osm[:, 4:5], in0=osm[:, 4:5], scalar1=-LAMBDA_INIT1, scalar2=None, op0=ALU.add), [osB], [osB])
                P.op("dve", lambda e: e.tensor_scalar(out=subg, in0=subg, scalar1=(1.0 - LAMBDA_INIT1), scalar2=None, op0=ALU.mult), [sgB], [sgB])
                P.op("dve", lambda e: e.memset(Vh[:, :, 256:257], 1.0), [], [VhB])

                posi = ETf.bitcast(I32)[:, 0:1024]
                P.dma(posi, pos_d.to_broadcast([128, 1024]), [], [ETB], q="sp")
                for (dst, add) in ((sinT, 0.0), (cosT, 0.25)):
                    for h in range(2):
                        sl = slice(h * 512, (h + 1) * 512)
                        P.op("dve", lambda e, sl=sl: e.tensor_copy(out=tq[0], in_=posi[:, sl]), [ETB], [tqB[0]])
                        P.op("dve", lambda e, add=add: e.tensor_scalar(out=tq[0], in0=tq[0], scalar1=pp(PP_INVF), scalar2=1.0 / TWO_PI,
                                                                        op0=ALU.mult, op1=ALU.mult), [tqB[0], ppB], [tqB[0]])
                        if add:
                            P.op("dve", lambda e, add=add: e.tensor_scalar(out=tq[0], in0=tq[0], scalar1=add, scalar2=None, op0=ALU.add), [tqB[0]], [tqB[0]])
                        ti = tq[1].bitcast(I32)
                        P.op("dve", lambda e: e.tensor_copy(out=ti, in_=tq[0]), [tqB[0]], [tqB[1]])
                        P.op("dve", lambda e: e.tensor_copy(out=tq[2], in_=ti), [tqB[1]], [tqB[2]])
                        P.op("dve", lambda e: e.tensor_tensor(out=tq[0], in0=tq[0], in1=tq[2], op=ALU.subtract), [tqB[0], tqB[2]], [tqB[0]])
                        P.op("dve", lambda e: e.tensor_scalar(out=tq[2], in0=tq[0], scalar1=0.5, scalar2=None, op0=ALU.is_gt), [tqB[0]], [tqB[2]])
                        P.op("dve", lambda e: e.tensor_tensor(out=tq[0], in0=tq[0], in1=tq[2], op=ALU.subtract), [tqB[0], tqB[2]], [tqB[0]])
                        P.op("dve", lambda e: e.tensor_scalar(out=tq[2], in0=tq[0], scalar1=-0.5, scalar2=None, op0=ALU.is_lt), [tqB[0]], [tqB[2]])
                        P.op("dve", lambda e: e.tensor_tensor(out=tq[0], in0=tq[0], in1=tq[2], op=ALU.add), [tqB[0], tqB[2]], [tqB[0]])
                        P.op("act", lambda e, dst=dst, sl=sl: e.activation(out=dst[:, sl], in_=tq[0], func=AF.Sin, scale=TWO_PI), [tqB[0]], [csB])

                P.op("dve", lambda e: e.tensor_scalar(out=sm[:, 0:16], in0=modT[1][:, 16:32], scalar1=1.0, scalar2=None, op0=ALU.add), [modB[1]], [smB])
                make_hT(lambda kc: sm[:, kc:kc + 1], lambda kc: modT[1][:, kc:kc + 1], [modB[1], smB],
                        list(range(NTILE)), 0, lambda kc, g0: HTB[kc][g0 // 4])
                for ng in range(4):
                    bcast_cols(BC[ng][:], BCB[ng], lambda c: modT[1][:, 32 + c:33 + c], [modB[1]], ng)
                sidx = [0]
                for q in range(8):
                    w, wB = wget(("qkv", q))
                    for n in range(4):
                        mp = (q % 4) * 4 + n
                        for h in range(2):
                            pq, pqB = nextps()
                            for kc in range(KC):
                                P.op("pe", lambda e, pq=pq, kc=kc, n=n, h=h, w=w: e.matmul(
                                    pq[:], lhsT=w[:, kc, n * 128:(n + 1) * 128], rhs=HT[:, kc, h * 512:(h + 1) * 512],
                                    start=(kc == 0), stop=(kc == KC - 1)), [wB, HTB[kc][h]], [pqB])
                            k = sidx[0] % 2
                            sidx[0] += 1
                            P.op("act", lambda e, pq=pq: e.activation(out=tq[2], in_=pq[:], func=AF.Copy), [pqB], [tqB[2]])
                            pr, prB = nextps()
                            P.op("pe", lambda e, pr=pr: e.matmul(pr[:], lhsT=PT, rhs=tq[2], start=True, stop=True), [cstB, tqB[2]], [prB])
                            sl = slice(h * 512, (h + 1) * 512)
                            P.op("dve", lambda e, sl=sl: e.tensor_tensor(out=tq[2], in0=tq[2], in1=cosT[:, sl], op=ALU.mult), [tqB[2], csB], [tqB[2]])
                            P.op("dve", lambda e, pr=pr, sl=sl, k=k: e.tensor_tensor(out=tq[k], in0=pr[:], in1=sinT[:, sl], op=ALU.mult), [prB, csB], [tqB[k]])
                            P.op("dve", lambda e, k=k: e.tensor_tensor(out=stg[k], in0=tq[k], in1=tq[2], op=ALU.add), [tqB[k], tqB[2]], [stgB[k]])
                            if q < 4:
                                P.dma(qsc_d[mp * 128:(mp + 1) * 128, sl], stg[k], [stgB[k]], [qscB], q="sp")
                            else:
                                hd, m = mp // 2, mp % 2
                                P.dma(kvl_h[hd][0:128, m * 1024 + h * 512:m * 1024 + (h + 1) * 512], stg[k], [stgB[k]], [kvlB[hd]], q="sp")
                for q in range(8, 12):
                    w, wB = wget(("qkv", q))
                    for t in range(NTILE):
                        pv, pvB = nextps()
                        for kc in range(KC):
                            P.op("pe", lambda e, pv=pv, kc=kc, t=t, w=w: e.matmul(
                                pv[:], lhsT=HT[:, kc, t * 128:(t + 1) * 128], rhs=w[:, kc, :],
                                start=(kc == 0), stop=(kc == KC - 1)), [HTB[kc][t // 4], wB], [pvB])
                        k = sidx[0] % 2
                        sidx[0] += 1
                        P.op("act", lambda e, pv=pv, k=k: e.activation(out=stg[k], in_=pv[:], func=AF.Copy), [pvB], [stgB[k]])
                        for hh in range(2):
                            hd = (q - 8) * 2 + hh
                            P.dma(kvl_h[hd][128:256, t * 256:(t + 1) * 256], stg[k][:, hh * 256:(hh + 1) * 256],
                                  [stgB[k]], [kvlB[hd]], q="sp")
                    for hh in range(2):
                        hd = (q - 8) * 2 + hh
                        P.op("pool", lambda e, hd=hd: e.collective_compute(
                            "AllGather", ALU.bypass, replica_groups=[[0, 1], [2, 3], [4, 5], [6, 7]],
                            ins=[kvl_h[hd].opt()], outs=[kva_h[hd].opt()]), [kvlB[hd]], [kvaB[hd]], dma=True, inc=1)
                SCALE = HD ** -0.5
                OTB = [[Buf(f"OT{k}_{h}") for h in range(2)] for k in range(KC)]
                phase([b for r in HTB for b in r], [b for r in OTB for b in r])
                ETpB = ETB; EToB = Buf("ETo")
                attn_bufs.append(EToB)
                ETO0 = 4096

                def tiles_of(qh, partner):
                    if partner:
                        return [(t, 0, t * 512) for t in range(8)]
                    res = []
                    eo = ETO0
                    for t in range(4 * qh + 4):
                        j0 = max(0, t - 4 * qh)
                        res.append((8 + t, j0, eo))
                        eo += 512 - j0 * 128
                    return res

                sps = [0]

                def emit_S(hd, qh, m, partner):
                    q0 = qh * 512
                    eB = ETpB if partner else EToB
                    for (kt, j0, eo) in tiles_of(qh, partner):
                        ncol = 512 - j0 * 128
                        bi = 4 + sps[0] % 3
                        sps[0] += 1
                        pss, pssB = PS[bi], PB[bi]
                        P.op("pe", lambda e, pss=pss, kt=kt, j0=j0, ncol=ncol: e.matmul(
                            pss[:, 0:ncol], lhsT=Kh[:, m, kt * 128:(kt + 1) * 128], rhs=Qh[:, m, q0 + j0 * 128:q0 + 512],
                            start=True, stop=True), [KhB, QhB], [pssB])
                        if partner:
                            P.op("act", lambda e, pss=pss, eo=eo, ncol=ncol: e.activation(
                                out=ET[:, eo:eo + ncol], in_=pss[:, 0:ncol], func=AF.Exp, bias=pp(PP_FLAG + 2), scale=SCALE),
                                [pssB, ppB], [eB])
                        else:
                            P.op("act", lambda e, pss=pss, eo=eo, ncol=ncol: e.activation(
                                out=ET[:, eo:eo + ncol], in_=pss[:, 0:ncol], func=AF.Exp, scale=SCALE), [pssB], [eB])
                            if (kt - 8) >= 4 * qh:
                                P.op("dve", lambda e, eo=eo: e.tensor_tensor(out=ET[:, eo:eo + 128], in0=ET[:, eo:eo + 128],
                                                                             in1=tri, op=ALU.mult), [eB, cstB], [eB])

                def emit_PV(qh, partner):
                    eB = ETpB if partner else EToB
                    tl = tiles_of(qh, partner)
                    for j in range(4):
                        use = [(kt, j0, eo) for (kt, j0, eo) in tl if j0 <= j]
                        for ui, (kt, j0, eo) in enumerate(use):
                            c0 = eo + (j - j0) * 128
                            P.op("pe", lambda e, j=j, kt=kt, c0=c0, st_=(partner and ui == 0), sp_=((not partner) and ui == len(use) - 1): e.matmul(
                                PS[j][:, 0:257], lhsT=ET[:, c0:c0 + 128], rhs=Vh[:, kt, :], start=st_, stop=sp_),
                                [eB, VhB], [PB[j]])

                def epilogue(hd, qh, m):
                    for j in range(4):
                        po, poB = PS[j], PB[j]
                        jj = qh * 4 + j
                        oq = SCRT["o1"][j]
                        oqB = SCRB["o1"][j]
                        if m == 0:
                            P.op("act", lambda e, po=po, oq=oq: e.activation(out=oq, in_=po[:, 0:257], func=AF.Copy), [poB], [oqB])
                            continue
                        P.op("dve", lambda e, oq=oq: e.reciprocal(out=osm[:, 8:9], in_=oq[:, 256:257]), [oqB, osB], [osB])
                        P.op("dve", lambda e, po=po: e.reciprocal(out=osm[:, 9:10], in_=po[:, 256:257]), [poB, osB], [osB])
                        P.op("dve", lambda e: e.tensor_tensor(out=osm[:, 9:10], in0=osm[:, 9:10], in1=osm[:, 4:5], op=ALU.mult), [osB], [osB])
                        P.op("dve", lambda e, oq=oq: e.tensor_scalar(out=oq[:, 0:256], in0=oq[:, 0:256], scalar1=osm[:, 8:9], scalar2=None,
                                                                      op0=ALU.mult), [oqB, osB], [oqB])
                        P.op("dve", lambda e, oq=oq, po=po: e.scalar_tensor_tensor(out=oq[:, 0:256], in0=po[:, 0:256], scalar=osm[:, 9:10],
                                                                                   in1=oq[:, 0:256], op0=ALU.mult, op1=ALU.add),
                             [poB, osB, oqB], [oqB])
                        P.op("dve", lambda e, oq=oq: e.tensor_tensor(out=oacc[:, 0, 0:256], in0=oq[:, 0:256], in1=oq[:, 0:256], op=ALU.mult),
                             [oqB], [oaB])
                        P.op("dve", lambda e: e.tensor_reduce(out=osm[:, 10:11], in_=oacc[:, 0, 0:256], axis=mybir.AxisListType.X, op=ALU.add),
                             [oaB, osB], [osB])
                        P.op("dve", lambda e: e.tensor_scalar(out=osm[:, 10:11], in0=osm[:, 10:11], scalar1=1.0 / 256, scalar2=RMS_EPS,
                                                              op0=ALU.mult, op1=ALU.add), [osB], [osB])
                        P.op("act", lambda e: e.activation(out=osm[:, 11:12], in_=osm[:, 10:11], func=AF.Sqrt), [osB], [osB])
                        P.op("dve", lambda e: e.reciprocal(out=osm[:, 12:13], in_=osm[:, 11:12]), [osB], [osB])
                        P.op("dve", lambda e, oq=oq: e.scalar_tensor_tensor(out=ob, in0=oq[:, 0:256], scalar=osm[:, 12:13], in1=subg,
                                                                           op0=ALU.mult, op1=ALU.mult), [oqB, osB, sgB], [obB])
                        for cc in range(2):
                            pt, ptB = PSB16[1], PB[7]
                            P.op("pe", lambda e, pt=pt, cc=cc: e.transpose(pt[:, cc * 128:(cc + 1) * 128], ob[:, cc * 128:(cc + 1) * 128], identb[:]),
                                 [obB, identbB], [ptB])
                            P.op("act", lambda e, pt=pt, cc=cc, jj=jj, hd=hd: e.activation(
                                out=HT[:, hd * 2 + cc, jj * 128:(jj + 1) * 128], in_=pt[:, cc * 128:(cc + 1) * 128], func=AF.Copy),
                                [ptB], [OTB[hd * 2 + cc][jj // 4]])

                def load_qk(hd):
                    P.dma(Qh[:], qsc_d[hd * 256:(hd + 1) * 256, :].rearrange("(m p) t -> p m t", p=128), [qscB], [QhB], q="sp")
                    P.dma(Kh[:, :, 0:1024], kva_h[hd][0:128, :].rearrange("p (m t) -> p m t", m=2), [kvaB[hd]], [KhB], q="sp")
                    P.dma(Kh[:, :, 1024:2048], kvl_h[hd][0:128, :].rearrange("p (m t) -> p m t", m=2), [kvlB[hd]], [KhB], q="sp")

                def load_v(hd):
                    P.dma(Vh[:, 0:8, 0:256], kva_h[hd][128:256, :].rearrange("p (t e) -> p t e", t=8), [kvaB[hd]], [VhB], q="sp")
                    P.dma(Vh[:, 8:16, 0:256], kvl_h[hd][128:256, :].rearrange("p (t e) -> p t e", t=8), [kvlB[hd]], [VhB], q="sp")

                iters = [(hd, qh, m) for hd in range(NHEAD) for qh in range(2) for m in range(2)]
                load_qk(0)
                emit_S(0, 0, 0, True)
                emit_S(0, 0, 0, False)
                for ii, (hd, qh, m) in enumerate(iters):
                    nxt = iters[ii + 1] if ii + 1 < len(iters) else None
                    if qh == 0 and m == 0:
                        load_v(hd)
                    emit_PV(qh, True)
                    if nxt is not None:
                        if nxt[0] != hd:
                            load_qk(nxt[0])
                        emit_S(nxt[0], nxt[1], nxt[2], True)
                    emit_PV(qh, False)
                    if nxt is not None:
                        emit_S(nxt[0], nxt[1], nxt[2], False)
                    epilogue(hd, qh, m)
                for ng in range(4):
                    w, wB = wget(("ow", ng))
                    gt, gB = BC[ng], BCB[ng]
                    for t in range(NTILE):
                        py, pyB = nextps()
                        for kc in range(KC):
                            P.op("pe", lambda e, py=py, kc=kc, t=t, w=w: e.matmul(
                                py[:], lhsT=HT[:, kc, t * 128:(t + 1) * 128], rhs=w[:, kc, :],
                                start=(kc == 0), stop=(kc == KC - 1)), [OTB[kc][t // 4], wB], [pyB])
                        evac_y(py, pyB, t, ng, gt[:], gB, alpha=ALPHA)
                layer_norm_X(ln_mix_g[1:2, :], ln_mix_b[1:2, :], final=attn_only)
                if not attn_only:
                    derive(1, "f", PP_B2 + 16)
                    HTB = [[Buf(f"HTb{k}_{h}") for h in range(2)] for k in range(KC)]
                    mlp(1, attn_bufs + [b for r in OTB for b in r], final=True)


        except _Stop:
            pass
        P.final_wait([outB])
        st = P.emit()
    return nc, st


def _pcol(v):
    v = np.asarray(v, np.float32).reshape(-1, 128)
    return np.ascontiguousarray(v.T)


def make_in_maps(inp, stage=99, attn_only=False):
    x = np.asarray(inp["x"], np.float32)
    c = np.asarray(inp["c"], np.float32)
    pos = np.asarray(inp["positions"], np.int32)
    cst = np.zeros((128, 640), np.float32)
    cst[:, 0:128] = np.eye(128, dtype=np.float32)
    cst[:, 128:256] = 1.0
    PT = np.zeros((128, 128), np.float32)
    for do in range(64):
        PT[do + 64, do] = -1.0
        PT[do, do + 64] = 1.0
    cst[:, 256:384] = PT
    kk = np.arange(128)[:, None]
    qq = np.arange(128)[None, :]
    cst[:, 384:512] = (kk <= qq).astype(np.float32)
    inv_freq = (10000.0 ** (-np.arange(0, 128, 2, dtype=np.float32) / 128.0)).astype(np.float32)

    shared = {}
    for k in ("ln_mix_g", "ln_mix_b", "ln_ffn_g", "ln_ffn_b", "conv_pw1_w", "conv_pw2_w",
              "attn_qkv_w", "attn_o_w", "attn_subln_g", "mlp_w1", "mlp_w2"):
        shared[k] = np.ascontiguousarray(np.asarray(inp[k], np.float32))
    if stage < 3 and not attn_only:
        shared["attn_qkv_w"] = shared["attn_qkv_w"][:, 0:1, 0:1]
        shared["attn_o_w"] = shared["attn_o_w"][:, 0:1, 0:1]
        shared["mlp_w1"] = shared["mlp_w1"][0:1]
        shared["mlp_w2"] = shared["mlp_w2"][0:1]
    if attn_only:
        shared["attn_qkv_w"] = np.asarray(inp["attn_qkv_w"], np.float32)
        shared["attn_o_w"] = np.asarray(inp["attn_o_w"], np.float32)
        shared["conv_pw1_w"] = shared["conv_pw1_w"][:, 0:1, 0:1]
        shared["conv_pw2_w"] = shared["conv_pw2_w"][:, 0:1, 0:1]
    if stage < 2 or attn_only:
        shared["mlp_w1"] = shared["mlp_w1"][0:1, 0:1, 0:1]
        shared["mlp_w2"] = shared["mlp_w2"][0:1, 0:1, 0:1]
    shared = {k: np.ascontiguousarray(v) for k, v in shared.items()}
    lqk = np.concatenate([np.asarray(inp[k], np.float32).reshape(1, 128)
                          for k in ("attn_lq1", "attn_lk1", "attn_lq2", "attn_lk2")], 0)

    pbase = np.zeros((128, NPP), np.float32)
    pbase[:, PP_PW1B:PP_PW1B + 32] = _pcol(inp["conv_pw1_b"][0])
    pbase[:, PP_DWB:PP_DWB + 16] = _pcol(inp["conv_dw_b"][0])
    pbase[:, PP_CLNG:PP_CLNG + 16] = _pcol(inp["conv_ln_g"][0])
    pbase[:, PP_CLNB:PP_CLNB + 16] = _pcol(inp["conv_ln_b"][0])
    for i in range(2):
        pbase[:, PP_B1 + 64 * i:PP_B1 + 64 * (i + 1)] = _pcol(inp["mlp_b1"][i])
        pbase[:, PP_B2 + 16 * i:PP_B2 + 16 * (i + 1)] = _pcol(inp["mlp_b2"][i])
    dw = np.asarray(inp["conv_dw_w"], np.float32)[0]
    pbase[:, PP_DWW:PP_DWW + 16 * 31] = dw.T.reshape(16, 128, 31).transpose(1, 0, 2).reshape(128, 16 * 31)
    pbase[:, PP_INVF] = np.concatenate([inv_freq, inv_freq])
    pbase[:, PP_PW2B:PP_PW2B + 16] = _pcol(inp["conv_pw2_b"][0])

    maps = []
    for core in range(8):
        b, half = core // 2, core % 2
        t0 = half * NTOK
        pp = pbase.copy()
        pp[:, PP_FLAG + 0] = float(half)
        pp[:, PP_FLAG + 1] = 1.0
        pp[:, PP_FLAG + 2] = 0.0 if half == 1 else NEG
        for bb in range(4):
            pp[:, PP_CTA + bb:PP_CTA + 64:4] = _pcol(c[bb])
        for i in range(2):
            pp[:, PP_ADABS + 12 * i:PP_ADABS + 12 * (i + 1)] = _pcol(np.asarray(inp["ada_b"], np.float32)[i, core * 1536:(core + 1) * 1536])
        pp[:, PP_SEL:PP_SEL + 4] = 0.0
        pp[:, PP_SEL + b] = 1.0
        xs = np.ascontiguousarray(x[b, t0:t0 + NTOK])
        xh = np.zeros((64, D), np.float32)
        if half == 1:
            xh[0:32] = x[b, t0 - 32:t0]
        xh[32:64] = x[b, t0 + 480:t0 + 512]
        ada_s = np.ascontiguousarray(np.asarray(inp["ada_w"], np.float32)[:, :, core * 1536:(core + 1) * 1536])
        m = {"ada_s": ada_s, "x": xs, "xh": xh, "pos": np.ascontiguousarray(pos[b, t0:t0 + NTOK]).reshape(1, NTOK),
             "ppar": pp, "cst": cst, "lqk": lqk}
        m.update(shared)
        maps.append(m)
    return maps


_NC_CACHE = {}


def kernel(**inputs):
    stage = 99
    if stage not in _NC_CACHE:
        _NC_CACHE[stage] = build_nc(stage)[0]
    nc = _NC_CACHE[stage]
    maps = make_in_maps(inputs)
    res = run_bass_kernel_spmd(nc, maps, core_ids=list(range(8)))
    out = np.empty((4, 2048, D), np.float32)
    for core in range(8):
        b, half = core // 2, core % 2
        out[b, half * NTOK:(half + 1) * NTOK] = res.results[core]["out"]
    return out
```

```python
import math
from contextlib import ExitStack

import numpy as np
import concourse.bass as bass
import concourse.mybir as mybir
from concourse.bass_utils import run_bass_kernel_spmd

F32 = mybir.dt.float32
BF16 = mybir.dt.bfloat16
I32 = mybir.dt.int32
AF = mybir.ActivationFunctionType
ALU = mybir.AluOpType

ENGS = ("pe", "act", "dve", "pool", "sp")
SAME_ENG_SYNC = True

D = 2048
KC = 16
NTOK = 1024
NTILE = 8
DFF = 8192
ALPHA = 4.0 ** 0.25
LN_EPS = 1e-5
RMS_EPS = 1e-5
HD = 128
NHEAD = 8
LAMBDA_INIT1 = 0.8 - 0.6 * math.exp(-0.3 * 1)
NEG = -30000.0

PP_PW1B = 0
PP_DWB = 32
PP_CLNG = 48
PP_CLNB = 64
PP_B1 = 80
PP_DWW = 208
PP_FLAG = 704
PP_CT = 708
PP_CTA = 728
PP_ADABS = 792
PP_SEL = 816
PP_INVF = 724
PP_ADAB = 728
PP_PW2B = 920
PP_B2 = 936
NPP = 968


class Buf:
    __slots__ = ("name", "w", "rs", "dsem", "dcnt")

    def __init__(self, name):
        self.name = name
        self.w = None
        self.rs = {}
        self.dsem = None
        self.dcnt = 0


class Ins:
    __slots__ = ("eng", "fn", "deps", "sem", "val", "isdma", "needed", "inc")

    def __init__(self, eng, fn, isdma):
        self.eng = eng
        self.fn = fn
        self.deps = []
        self.sem = None
        self.val = 0
        self.isdma = isdma
        self.needed = False
        self.inc = 16


class Prog:
    def __init__(self, nc, es):
        self.nc = nc
        self.es = es
        self.streams = {e: [] for e in ENGS}
        self.esem = {e: es.enter_context(nc.semaphore("s_" + e)) for e in ("pe", "act", "dve", "pool")}
        self.nsem = 4

    def op(self, eng, fn, reads=(), writes=(), dma=False, inc=16):
        ins = Ins(eng, fn, dma)
        ins.inc = inc
        deps = {}
        for b in reads:
            if b.w is not None:
                deps[id(b.w)] = b.w
        for b in writes:
            if b.w is not None:
                deps[id(b.w)] = b.w
            for r in b.rs.values():
                deps[id(r)] = r
        if dma:
            dst = writes[0]
            if dst.dsem is None:
                dst.dsem = self.es.enter_context(self.nc.semaphore("d_" + dst.name))
                self.nsem += 1
            dst.dcnt += inc
            ins.sem = dst.dsem
            ins.val = dst.dcnt
            ins.needed = True
        for d in deps.values():
            if d is ins:
                continue
            if (not d.isdma) and d.eng == eng and (eng == "pe" or not SAME_ENG_SYNC):
                continue
            ins.deps.append(d)
            d.needed = True
        k = ins.eng if not dma else ("d", id(ins.sem))
        for b in reads:
            b.rs[k] = ins
        for b in writes:
            b.w = ins
            b.rs = {}
        self.streams[eng].append(ins)
        return ins

    def dma(self, out, in_, reads, writes, q="sp", **kw):
        return self.op(q, lambda e: e.dma_start(out=out, in_=in_, **kw), reads, writes, dma=True)

    def final_wait(self, bufs, eng="sp"):
        return self.op(eng, lambda e: None, reads=list(bufs), writes=())

    def emit(self):
        nc = self.nc
        for e in ("pe", "act", "dve", "pool"):
            c = 0
            for ins in self.streams[e]:
                if ins.isdma:
                    continue
                if ins.needed:
                    c += 1
                    ins.sem = self.esem[e]
                    ins.val = c
        stats = {}

        def run(eng_obj, ename):
            known = {}
            nw = 0
            for ins in self.streams[ename]:
                req = {}
                for d in ins.deps:
                    k = id(d.sem)
                    if k not in req or req[k][1] < d.val:
                        req[k] = (d.sem, d.val)
                for k, (sem, val) in req.items():
                    if known.get(k, 0) >= val:
                        continue
                    eng_obj.wait_ge(sem, val)
                    known[k] = val
                    nw += 1
                r = ins.fn(eng_obj)
                if r is None:
                    continue
                if ins.isdma:
                    r.then_inc(ins.sem, ins.inc)
                elif ins.needed:
                    r.then_inc(ins.sem, 1)
            stats[ename] = (len(self.streams[ename]), nw)

        with nc.Block() as block:
            @block.tensor
            def _(e):
                run(e, "pe")

            @block.scalar
            def _(e):
                run(e, "act")

            @block.vector
            def _(e):
                run(e, "dve")

            @block.gpsimd
            def _(e):
                run(e, "pool")

            @block.sync
            def _(e):
                run(e, "sp")
        return stats


class _Stop(Exception):
    pass


def build_nc(stage=99, dbg=None, attn_only=False):
    nc = bass.Bass("TRN2", target_bir_lowering=False)

    def din(name, shape, dt=F32):
        return nc.dram_tensor(name, list(shape), dt, kind="ExternalInput").ap()

    x_d = din("x", [NTOK, D])
    xh_d = din("xh", [64, D])
    pos_d = din("pos", [1, NTOK], I32)
    ppar_d = din("ppar", [128, NPP])
    cst_d = din("cst", [128, 640])
    lqk_d = din("lqk", [4, 128])
    ada_s = din("ada_s", [2, D, 1536])
    modl_d = nc.dram_tensor("modl", [128, 96], F32).ap()
    moda_d = nc.dram_tensor("moda", [1024, 96], F32).ap()
    ln_mix_g = din("ln_mix_g", [2, D]); ln_mix_b = din("ln_mix_b", [2, D])
    ln_ffn_g = din("ln_ffn_g", [2, D]); ln_ffn_b = din("ln_ffn_b", [2, D])
    pw1_w = din("conv_pw1_w", [1, D, 2 * D] if not attn_only else [1, 1, 1])
    pw2_w = din("conv_pw2_w", [1, D, D] if not attn_only else [1, 1, 1])
    qkv_w = din("attn_qkv_w", [1, D, 3 * D] if stage >= 3 else [1, 1, 1])
    o_w = din("attn_o_w", [1, D, D] if stage >= 3 else [1, 1, 1])
    subg_d = din("attn_subln_g", [1, 256])
    nl = 2 if stage >= 3 else 1
    w1_d = din("mlp_w1", [nl, D, DFF] if (stage >= 2 and not attn_only) else [1, 1, 1])
    w2_d = din("mlp_w2", [nl, DFF, D] if (stage >= 2 and not attn_only) else [1, 1, 1])
    out_d = nc.dram_tensor("out", [NTOK, D], F32, kind="ExternalOutput").ap()
    kvl_h = [nc.dram_tensor(f"kvl{h}", [256, 2048], BF16).ap() for h in range(NHEAD)]
    kva_h = [nc.dram_tensor(f"kva{h}", [512, 2048], BF16).ap() for h in range(NHEAD)]
    qsc_d = nc.dram_tensor("qsc", [16 * 128, NTOK], BF16).ap()

    with ExitStack() as es:
        P = Prog(nc, es)

        def sb(name, shape, dt=F32):
            return es.enter_context(nc.sbuf_tensor("sb_" + name, list(shape), dt))

        X = sb("X", [128, NTILE, D])
        XB = [[Buf(f"X{t}_{g}") for g in range(4)] for t in range(NTILE)]
        HT = sb("HT", [128, KC, 1088], BF16)
        NSLOT = 2
        WS = [sb(f"WS{i}", [128, KC, 512], BF16) for i in range(NSLOT)]
        WSB = [Buf(f"WS{i}") for i in range(NSLOT)]
        BC = [sb(f"BC{i}", [128, 512]) for i in range(4)]
        BCB = [Buf(f"BC{i}") for i in range(4)]
        ppar = sb("ppar", [128, NPP]); ppB = Buf("ppar")
        cst = sb("cst", [128, 640]); cstB = Buf("cst")
        identb = sb("identb", [128, 128], BF16); identbB = Buf("identb")
        onesb = sb("onesb", [128, 128], BF16); onesbB = Buf("onesb")
        condT = sb("condT", [128, KC, 4], BF16); condB = Buf("condT")
        modp = sb("modp", [128, 24, 4]); modpB = Buf("modp")
        modallB = Buf("modall")
        modT = [sb(f"modT{i}", [128, 96]) for i in range(2)]
        modB = [Buf(f"modT{i}") for i in range(2)]
        sm = sb("sm", [128, 256]); smB = Buf("sm")
        junk = sb("junk", [128, 8]); junkB = Buf("junk")
        SCR = sb("SCR", [128, 14880])
        PS = [es.enter_context(nc.psum_tensor(f"ps{i}", [128, 512], F32)) for i in range(8)]
        PB = [Buf(f"ps{i}") for i in range(8)]
        PSB16 = [PS[6][:].bitcast(BF16), PS[7][:].bitcast(BF16)]
        outB = Buf("out")

        ident = cst[:, 0:128]
        ones = cst[:, 128:256]
        PT = cst[:, 256:384]
        tri = cst[:, 384:512]

        def pp(c0, n=1):
            return ppar[:, c0:c0 + n]

        psr = [0]

        def nextps():
            i = psr[0] % 8
            psr[0] += 1
            return PS[i], PB[i]

        def dump(name, ap, bufs, ncols):
            if dbg == name:
                P.dma(out_d[0:128, 0:ncols], ap, list(bufs), [outB], q="pool")
                raise _Stop()

        try:
            P.dma(ppar[:], ppar_d, [], [ppB], q="sp")
            P.dma(cst[:], cst_d, [], [cstB], q="sp")
            for t in range(NTILE):
                P.dma(X[:, t, :], x_d[t * 128:(t + 1) * 128, :], [], XB[t], q="sp")
            P.op("dve", lambda e: e.tensor_copy(out=identb[:], in_=ident), [cstB], [identbB])
            P.op("dve", lambda e: e.tensor_copy(out=onesb[:], in_=ones), [cstB], [onesbB])
            P.op("act", lambda e: e.activation(out=condT[:].rearrange("p a b -> p (a b)"), in_=pp(PP_CTA, 64), func=AF.Silu), [ppB], [condB])

            PLAN = [("adas", k) for k in range(6)]
            if attn_only:
                PLAN += [("qkv", q) for q in range(12)] + [("ow", ng) for ng in range(4)]
            if not attn_only:
                for hf in range(2):
                    for jp in range(4):
                        PLAN.append(("pw1", jp)); PLAN.append(("pw1", 4 + jp))
                    for ng in range(4):
                        PLAN.append(("pw2", ng))
                if stage >= 2:
                    for g in range(4):
                        for q in range(4):
                            PLAN.append(("w1_0", g * 4 + q))
                        for ng in range(4):
                            PLAN.append(("w2_0", g * 4 + ng))
                if stage >= 3:
                    for q in range(12):
                        PLAN.append(("qkv", q))
                    for ng in range(4):
                        PLAN.append(("ow", ng))
                    for g in range(4):
                        for q in range(4):
                            PLAN.append(("w1_1", g * 4 + q))
                        for ng in range(4):
                            PLAN.append(("w2_1", g * 4 + ng))

            def piece_src(spec):
                kind, k = spec
                if kind == "adas":
                    m, r0, c0 = ada_s[k // 3], 0, (k % 3) * 512
                elif kind == "pw1":
                    m, r0, c0 = pw1_w[0], 0, (k % 4) * 512 + (2048 if k >= 4 else 0)
                elif kind == "pw2":
                    m, r0, c0 = pw2_w[0], 0, k * 512
                elif kind.startswith("w1_"):
                    m, r0, c0 = w1_d[int(kind[3])], 0, k * 512
                elif kind.startswith("w2_"):
                    m, r0, c0 = w2_d[int(kind[3])], (k // 4) * 2048, (k % 4) * 512
                elif kind == "qkv":
                    m, r0, c0 = qkv_w[0], 0, k * 512
                elif kind == "ow":
                    m, r0, c0 = o_w[0], 0, k * 512
                return m[r0:r0 + 2048, c0:c0 + 512].rearrange("(kc p) n -> p kc n", p=128)

            wst = {"issued": 0, "cur": 0}

            def wget(spec):
                i = wst["cur"]
                assert PLAN[i] == spec, (i, PLAN[i], spec)
                while wst["issued"] < min(i + NSLOT, len(PLAN)):
                    j = wst["issued"]
                    sl = j % NSLOT
                    P.dma(WS[sl][:], piece_src(PLAN[j]), [], [WSB[sl]], q="pool")
                    wst["issued"] += 1
                wst["cur"] += 1
                return WS[i % NSLOT], WSB[i % NSLOT]

            modall = HT[:, 0:2, :].rearrange("p a b -> p (a b)").bitcast(F32)[:, 0:768].rearrange(
                "p (r c b) -> p r c b", r=8, c=24)

            def compute_mod_all():
                ps, psB = nextps()
                for k in range(6):
                    w, wB = wget(("adas", k))
                    for n in range(4):
                        col = k * 4 + n
                        for kc in range(KC):
                            P.op("pe", lambda e, w=w, n=n, kc=kc, col=col: e.matmul(
                                ps[:, col * 4:col * 4 + 4], lhsT=w[:, kc, n * 128:(n + 1) * 128], rhs=condT[:, kc, :],
                                start=(kc == 0), stop=(kc == KC - 1)), [wB, condB], [psB])
                P.op("dve", lambda e: e.tensor_tensor(
                    out=modp[:], in0=ps[:, 0:96].rearrange("p (c b) -> p c b", b=4),
                    in1=pp(PP_ADABS, 24).unsqueeze(2).to_broadcast([128, 24, 4]), op=ALU.add), [psB, ppB], [modpB])
                modlB = Buf("modl"); modaB = Buf("moda")
                P.dma(modl_d, modp[:].rearrange("p c b -> p (c b)"), [modpB], [modlB], q="sp")
                P.op("pool", lambda e: e.collective_compute(
                    "AllGather", ALU.bypass, replica_groups=[list(range(8))],
                    ins=[modl_d.opt()], outs=[moda_d.opt()]), [modlB], [modaB], dma=True, inc=1)
                P.dma(modall[:].rearrange("p r c b -> p r (c b)"), moda_d.rearrange("(r p) c -> p r c", p=128),
                      [modaB], [modallB], q="sp")
                for i in range(2):
                    dst = modT[i][:].rearrange("p (r c) -> p r c", r=8)
                    P.op("dve", lambda e, i=i, dst=dst: e.tensor_scalar(
                        out=dst, in0=modall[:, :, i * 12:(i + 1) * 12, 0], scalar1=pp(PP_SEL + 0), scalar2=None,
                        op0=ALU.mult), [modallB, ppB], [modB[i]])
                    for bb in range(1, 4):
                        P.op("dve", lambda e, i=i, dst=dst, bb=bb: e.scalar_tensor_tensor(
                            out=dst, in0=modall[:, :, i * 12:(i + 1) * 12, bb], scalar=pp(PP_SEL + bb), in1=dst,
                            op0=ALU.mult, op1=ALU.add), [modallB, ppB, modB[i]], [modB[i]])

            def bcast_cols(dst, dstB, srcT_ap_fn, srcBs, ng):
                ps, psB = nextps()
                dg = SCRT["diag32"]
                dh = SCRT["diaghl"]
                for n in range(4):
                    c = ng * 4 + n
                    dB = SCRB["diag32"][0]
                    P.op("dve", lambda e, n=n, c=c: e.tensor_scalar(
                        out=dg, in0=ident, scalar1=srcT_ap_fn(c), scalar2=None, op0=ALU.mult),
                        [cstB] + srcBs, [dB])
                    P.op("dve", lambda e, n=n: e.tensor_copy(out=dh[:, 0, :], in_=dg), [dB], [dB])
                    P.op("dve", lambda e, n=n: e.tensor_tensor(out=dh[:, 1, :], in0=dg, in1=dh[:, 0, :],
                                                                op=ALU.subtract), [dB], [dB])
                    for hl in range(2):
                        P.op("pe", lambda e, ps=ps, n=n, hl=hl: e.matmul(
                            ps[:, n * 128:(n + 1) * 128], lhsT=onesb[:], rhs=dh[:, hl, :],
                            start=(hl == 0), stop=(hl == 1)), [onesbB, dB], [psB])
                P.op("act", lambda e, ps=ps: e.activation(out=dst, in_=ps[:], func=AF.Copy), [psB], [dstB])

            def make_hT(scale_col, shift_col, srcBs, tiles, col0, hbufs):
                for kc in range(KC):
                    for g0 in range(0, len(tiles), 4):
                        grp = tiles[g0:g0 + 4]
                        ps, psB = nextps()
                        for q, t in enumerate(grp):
                            P.op("pe", lambda e, ps=ps, q=q, t=t, kc=kc: e.transpose(
                                ps[:, q * 128:(q + 1) * 128], X[:, t, kc * 128:(kc + 1) * 128], ident),
                                [XB[t][kc // 4], cstB], [psB])
                        n = len(grp) * 128
                        c = col0 + g0 * 128
                        P.op("act", lambda e, ps=ps, kc=kc, n=n, c=c: e.activation(
                            out=HT[:, kc, c:c + n], in_=ps[:, 0:n], func=AF.Identity,
                            bias=shift_col(kc), scale=scale_col(kc)), [psB] + srcBs, [hbufs(kc, g0)])

            def prescale_X(ng, gb2_tile, gb2B, tiles):
                for t in tiles:
                    P.op("dve", lambda e, t=t: e.scalar_tensor_tensor(
                        out=X[:, t, ng * 512:(ng + 1) * 512], in0=X[:, t, ng * 512:(ng + 1) * 512], scalar=ALPHA,
                        in1=gb2_tile, op0=ALU.mult, op1=ALU.add), [XB[t][ng], gb2B], [XB[t][ng]])

            ytmp_i = [0]

            def evac_y(ps, psB, t, ng, g_tile, gB, alpha=None):
                k = ytmp_i[0] % 2
                ytmp_i[0] += 1
                yt = SCRT["ytmp"][k]
                ytB = SCRB["ytmp"][k]
                P.op("dve", lambda e: e.tensor_tensor(out=yt, in0=ps[:], in1=g_tile, op=ALU.mult), [psB, gB], [ytB])
                if alpha is not None:
                    P.op("dve", lambda e: e.scalar_tensor_tensor(
                        out=X[:, t, ng * 512:(ng + 1) * 512], in0=X[:, t, ng * 512:(ng + 1) * 512], scalar=alpha, in1=yt,
                        op0=ALU.mult, op1=ALU.add), [ytB, XB[t][ng]], [XB[t][ng]])
                    return
                P.op("pool", lambda e: e.tensor_tensor(out=X[:, t, ng * 512:(ng + 1) * 512],
                                                       in0=X[:, t, ng * 512:(ng + 1) * 512], in1=yt, op=ALU.add),
                     [ytB, XB[t][ng]], [XB[t][ng]])

            def layer_norm_X(g_row, b_row, final=False):
                st = SCRT["lnst"]
                stB = SCRB["lnst"]
                for t in range(NTILE):
                    for c in range(4):
                        P.op("dve", lambda e, t=t, c=c: e.bn_stats(out=st[:, t, c, :], in_=X[:, t, c * 512:(c + 1) * 512]),
                             [XB[t][c]], [stB[t]])
                    P.op("dve", lambda e, t=t: e.bn_aggr(out=st[:, t, 4, 0:2], in_=st[:, t, 0:4, :]), [stB[t]], [stB[t]])
                    P.op("dve", lambda e, t=t: e.tensor_scalar(out=st[:, t, 4, 2:3], in0=st[:, t, 4, 1:2], scalar1=LN_EPS,
                                                                scalar2=None, op0=ALU.add), [stB[t]], [stB[t]])
                    P.op("act", lambda e, t=t: e.activation(out=st[:, t, 4, 3:4], in_=st[:, t, 4, 2:3], func=AF.Sqrt),
                         [stB[t]], [stB[t]])
                    P.op("dve", lambda e, t=t: e.reciprocal(out=st[:, t, 4, 4:5], in_=st[:, t, 4, 3:4]), [stB[t]], [stB[t]])
                for ng in range(4):
                    gt, gB = BC[0 + (ng % 2) * 2], BCB[0 + (ng % 2) * 2]
                    bt, bB = BC[1 + (ng % 2) * 2], BCB[1 + (ng % 2) * 2]
                    P.dma(gt[:], g_row[:, ng * 512:(ng + 1) * 512].to_broadcast([128, 512]), [], [gB], q="sp")
                    P.dma(bt[:], b_row[:, ng * 512:(ng + 1) * 512].to_broadcast([128, 512]), [], [bB], q="sp")
                    for t in range(NTILE):
                        xs = X[:, t, ng * 512:(ng + 1) * 512]
                        P.op("dve", lambda e, xs=xs, t=t, gt=gt: e.scalar_tensor_tensor(
                            out=xs, in0=xs, scalar=st[:, t, 4, 0:1], in1=gt[:], op0=ALU.subtract, op1=ALU.mult),
                            [XB[t][ng], stB[t], gB], [XB[t][ng]])
                        P.op("dve", lambda e, xs=xs, t=t, bt=bt: e.scalar_tensor_tensor(
                            out=xs, in0=xs, scalar=st[:, t, 4, 4:5], in1=bt[:], op0=ALU.mult, op1=ALU.add),
                            [XB[t][ng], stB[t], bB], [XB[t][ng]])
                        if final:
                            P.dma(out_d[t * 128:(t + 1) * 128, ng * 512:(ng + 1) * 512], xs, [XB[t][ng]], [outB], q="sp")

            def phase(old, new):
                P.op("pool", lambda e: e.memset(junk[:, 0:1], 0.0), [], list(old) + list(new) + [junkB])

            SCRT = {}
            SCRB = {}
            off = [0]

            def carve(n_f32):
                a = off[0]
                off[0] += n_f32
                assert off[0] <= 14880, off[0]
                return SCR[:, a:a + n_f32]

            SCRT["diag32"] = carve(128)
            SCRT["diaghl"] = carve(128).bitcast(BF16).rearrange("p (h b) -> p h b", h=2)
            SCRB["diag32"] = [Buf(f"diag32_{n}") for n in range(4)]
            SCRT["lnst"] = carve(NTILE * 5 * 6).rearrange("p (t c s) -> p t c s", t=NTILE, c=5)
            SCRB["lnst"] = [Buf(f"lnst{t}") for t in range(NTILE)]
            base_off = off[0]


            def derive(i, which, b2col):
                base = 0 if which == "m" else 32
                mo = 0 if which == "m" else 48
                P.op("dve", lambda e: e.tensor_scalar(out=sm[:, base:base + 16], in0=modT[i][:, mo + 16:mo + 32],
                                                       scalar1=1.0, scalar2=None, op0=ALU.add), [modB[i]], [smB])
                P.op("dve", lambda e: e.tensor_tensor(out=sm[:, base + 16:base + 32], in0=modT[i][:, mo + 32:mo + 48],
                                                       in1=pp(b2col, 16), op=ALU.mult), [modB[i], ppB], [smB])

            compute_mod_all()
            if not attn_only:
                dump("condT", ppar[:, 0:64], [ppB, condB], 64)
                dump("mod0", modT[0][:], [modB[0]], 96)
                derive(0, "m", PP_PW2B)

                off[0] = base_off
                CT = carve(8192).rearrange("p (a b) -> p a b", a=16)
                CTB = [Buf(f"CT{k}") for k in range(KC)]
                dgc = carve(1984).bitcast(BF16).rearrange("p (a b) -> p a b", a=31)
                dgcB = [Buf("dgc0"), Buf("dgc1")]
                xh = carve(2048)
                xhB = Buf("xh")
                tmpA = [carve(544) for _ in range(2)]; tmpAB = [Buf(f"tmpA{k}") for k in range(2)]
                tmpS = [carve(512) for _ in range(2)]; tmpSB = [Buf(f"tmpS{k}") for k in range(2)]
                tmpSb = [t.bitcast(BF16).rearrange("p (a b) -> p a b", a=2) for t in tmpS]
                mean_t = tmpA[0][:, 0:512]; rstd_t = tmpA[1][:, 0:512]; lnB = Buf("convln")
                SCRT["ytmp"] = tmpS
                SCRB["ytmp"] = tmpSB
                CHB = [Buf(f"CH{k}") for k in range(KC)]
                CVB = [Buf(f"CV{k}") for k in range(KC)]
                conv_bufs = CTB + dgcB + [xhB, lnB] + tmpAB + tmpSB + CHB + CVB

                P.dma(xh[0:64, :], xh_d, [], [xhB], q="sp")


                for hf in range(2):
                    tiles = [4 * hf + q for q in range(4)]
                    for kc in range(KC):
                        ps, psB = nextps()
                        P.op("pe", lambda e, ps=ps, kc=kc, hf=hf: e.transpose(
                            ps[:, 0:32], xh[hf * 32:(hf + 1) * 32, kc * 128:(kc + 1) * 128],
                            cst[hf * 32:(hf + 1) * 32, hf * 32:(hf + 1) * 32]), [xhB, cstB], [psB])
                        P.op("act", lambda e, ps=ps, kc=kc: e.activation(
                            out=HT[:, kc, 0:32], in_=ps[:, 0:32], func=AF.Identity,
                            bias=modT[0][:, kc:kc + 1], scale=sm[:, kc:kc + 1]), [psB, modB[0], smB], [CHB[kc]])
                    make_hT(lambda kc: sm[:, kc:kc + 1], lambda kc: modT[0][:, kc:kc + 1], [modB[0], smB], tiles, 32,
                            lambda kc, g0: CHB[kc])
                    for ng in range(4):
                        if hf == 0:
                            gbt, gbB = BC[ng][:], BCB[ng]
                        else:
                            gbt, gbB = tmpA[ng % 2][:, 0:512], tmpAB[ng % 2]
                        bcast_cols(gbt, gbB, lambda c: sm[:, 16 + c:17 + c], [smB], ng)
                        prescale_X(ng, gbt, gbB, tiles)
                    if hf == 0:
                        for ng in range(4):
                            bcast_cols(BC[ng][:], BCB[ng], lambda c: modT[0][:, 32 + c:33 + c], [modB[0]], ng)
                    if hf == 0:
                        dump("hT0", HT[:, 0, 0:544], CHB, 544)
                    for jp in range(4):
                        wa, waB = wget(("pw1", jp))
                        ph, phB = PS[4], PB[4]
                        for n in range(4):
                            pa, paB = PS[n], PB[n]
                            for kc in range(KC):
                                P.op("pe", lambda e, pa=pa, kc=kc, n=n: e.matmul(
                                    pa[:], lhsT=wa[:, kc, n * 128:(n + 1) * 128], rhs=HT[:, kc, 32:544],
                                    start=(kc == 0), stop=(kc == KC - 1)), [waB, CHB[kc]], [paB])
                            for kc in range(KC):
                                P.op("pe", lambda e, kc=kc, n=n: e.matmul(
                                    ph[:, n * 32:(n + 1) * 32], lhsT=wa[:, kc, n * 128:(n + 1) * 128], rhs=HT[:, kc, 0:32],
                                    start=(kc == 0), stop=(kc == KC - 1)), [waB, CHB[kc]], [phB])
                        wg, wgB = wget(("pw1", 4 + jp))
                        for n in range(4):
                            j = jp * 4 + n
                            pa, paB = PS[n], PB[n]
                            pg, pgB = PS[5 + (n % 2)], PB[5 + (n % 2)]
                            for kc in range(KC):
                                P.op("pe", lambda e, pg=pg, kc=kc, n=n: e.matmul(
                                    pg[:], lhsT=wg[:, kc, n * 128:(n + 1) * 128], rhs=HT[:, kc, 32:544],
                                    start=(kc == 0), stop=(kc == KC - 1)), [wgB, CHB[kc]], [pgB])
                            for kc in range(KC):
                                P.op("pe", lambda e, kc=kc, n=n: e.matmul(
                                    ph[:, 128 + n * 32:128 + (n + 1) * 32], lhsT=wg[:, kc, n * 128:(n + 1) * 128],
                                    rhs=HT[:, kc, 0:32], start=(kc == 0), stop=(kc == KC - 1)), [wgB, CHB[kc]], [phB])
                            k = j % 2
                            ta, taB = tmpA[k], tmpAB[k]
                            P.op("act", lambda e, pg=pg, ta=ta, j=j: e.activation(
                                out=ta[:, 32:544], in_=pg[:], func=AF.Sigmoid, bias=pp(PP_PW1B + 16 + j), scale=1.0),
                                [pgB, ppB], [taB])
                            P.op("act", lambda e, ta=ta, j=j, n=n: e.activation(
                                out=ta[:, 0:32], in_=ph[:, 128 + n * 32:128 + (n + 1) * 32], func=AF.Sigmoid,
                                bias=pp(PP_PW1B + 16 + j), scale=1.0), [phB, ppB, taB], [taB])
                            P.op("dve", lambda e, pa=pa, ta=ta, j=j: e.scalar_tensor_tensor(
                                out=HT[:, j, 544 + 32:544 + 544], in0=pa[:], scalar=pp(PP_PW1B + j), in1=ta[:, 32:544],
                                op0=ALU.add, op1=ALU.mult), [paB, ppB, taB], [CVB[j]])
                            P.op("dve", lambda e, ta=ta, j=j, n=n: e.scalar_tensor_tensor(
                                out=ta[:, 0:32], in0=ph[:, n * 32:(n + 1) * 32], scalar=pp(PP_PW1B + j), in1=ta[:, 0:32],
                                op0=ALU.add, op1=ALU.mult), [phB, ppB, taB], [taB])
                            P.op("dve", lambda e, ta=ta, j=j, hf=hf: e.tensor_scalar(
                                out=HT[:, j, 544:544 + 32], in0=ta[:, 0:32], scalar1=pp(PP_FLAG + hf), scalar2=None,
                                op0=ALU.mult), [taB, ppB, CVB[j]], [CVB[j]])
                    if hf == 0:
                        dump("vT0", HT[:, 0, 544:1088], CVB, 544)
                    s1, s1B = PS[6], PB[6]
                    s2, s2B = PS[7], PB[7]
                    def build_diag(kc):
                        for dh_, (j0, j1) in enumerate(((0, 16), (16, 31))):
                            P.op("dve", lambda e, kc=kc, j0=j0, j1=j1: e.tensor_tensor(
                                out=dgc[:, j0:j1, :], in0=identb[:].unsqueeze(1).to_broadcast([128, j1 - j0, 128]),
                                in1=ppar[:, PP_DWW + kc * 31 + j0:PP_DWW + kc * 31 + j1].unsqueeze(2).to_broadcast([128, j1 - j0, 128]),
                                op=ALU.mult), [identbB, ppB], [dgcB[dh_]])

                    def stats_mm(kc):
                        cb = tmpSb[kc % 2]
                        P.op("pe", lambda e: e.matmul(s1[:], lhsT=onesb[:], rhs=cb[:, 0, :], start=(kc == 0),
                                                      stop=(kc == KC - 1)), [onesbB, tmpSB[kc % 2]], [s1B])
                        P.op("pe", lambda e: e.matmul(s2[:], lhsT=onesb[:], rhs=cb[:, 1, :], start=(kc == 0),
                                                      stop=(kc == KC - 1)), [onesbB, tmpSB[kc % 2]], [s2B])

                    build_diag(0)
                    pend = None
                    for kc in range(KC):
                        pc, pcB = PS[kc % 6], PB[kc % 6]
                        for j in range(31):
                            P.op("pe", lambda e, pc=pc, kc=kc, j=j: e.matmul(
                                pc[:], lhsT=dgc[:, j, :], rhs=HT[:, kc, 544 + 2 + j:544 + 2 + j + 512],
                                start=(j == 0), stop=(j == 30)), [dgcB[0 if j < 16 else 1], CVB[kc]], [pcB])
                        if kc + 1 < KC:
                            build_diag(kc + 1)
                        P.op("dve", lambda e, pc=pc, kc=kc: e.tensor_scalar(
                            out=CT[:, kc, :], in0=pc[:], scalar1=pp(PP_DWB + kc), scalar2=None, op0=ALU.add),
                            [pcB, ppB], [CTB[kc]])
                        k = kc % 2
                        cb = tmpSb[k]
                        P.op("dve", lambda e, kc=kc, cb=cb: e.tensor_copy(out=cb[:, 0, :], in_=CT[:, kc, :]),
                             [CTB[kc]], [tmpSB[k]])
                        P.op("dve", lambda e, kc=kc, cb=cb: e.tensor_tensor(out=cb[:, 1, :], in0=CT[:, kc, :],
                                                                             in1=CT[:, kc, :], op=ALU.mult),
                             [CTB[kc], tmpSB[k]], [tmpSB[k]])
                        if pend is not None:
                            stats_mm(pend)
                        pend = kc
                    stats_mm(pend)
                    if hf == 0:
                        dump("ct0", CT[:, 0, :], CTB, 512)
                    P.op("dve", lambda e: e.tensor_scalar(out=mean_t, in0=s1[:], scalar1=1.0 / D, scalar2=None,
                                                           op0=ALU.mult), [s1B], [lnB, tmpAB[0], tmpAB[1]])
                    P.op("dve", lambda e: e.tensor_tensor(out=tmpS[0], in0=mean_t, in1=mean_t, op=ALU.mult),
                         [lnB], [tmpSB[0]])
                    P.op("dve", lambda e: e.scalar_tensor_tensor(out=rstd_t, in0=s2[:], scalar=1.0 / D, in1=tmpS[0],
                                                                  op0=ALU.mult, op1=ALU.subtract), [s2B, tmpSB[0]], [lnB])
                    P.op("dve", lambda e: e.tensor_scalar(out=rstd_t, in0=rstd_t, scalar1=LN_EPS, scalar2=None,
                                                           op0=ALU.add), [lnB], [lnB])
                    P.op("act", lambda e: e.activation(out=rstd_t, in_=rstd_t, func=AF.Sqrt), [lnB], [lnB])
                    P.op("dve", lambda e: e.reciprocal(out=rstd_t, in_=rstd_t), [lnB], [lnB])
                    for kc in range(KC):
                        k = kc % 2
                        P.op("pool", lambda e, kc=kc, k=k: e.tensor_tensor(out=tmpS[k], in0=CT[:, kc, :], in1=mean_t,
                                                                            op=ALU.subtract), [CTB[kc], lnB], [tmpSB[k]])
                        P.op("dve", lambda e, k=k: e.tensor_tensor(out=tmpS[k], in0=tmpS[k], in1=rstd_t, op=ALU.mult),
                             [tmpSB[k], lnB], [tmpSB[k]])
                        P.op("act", lambda e, kc=kc, k=k: e.activation(
                            out=HT[:, kc, 32:544], in_=tmpS[k], func=AF.Silu, bias=pp(PP_CLNB + kc), scale=pp(PP_CLNG + kc)),
                            [tmpSB[k], ppB], [CHB[kc]])
                    if hf == 0:
                        dump("sT0", HT[:, 0, 32:544], CHB, 512)
                    if hf == 0:
                        derive(0, "f", PP_B2)
                    for ng in range(4):
                        w, wB = wget(("pw2", ng))
                        gt, gB = BC[ng], BCB[ng]
                        for q, t in enumerate(tiles):
                            py, pyB = nextps()
                            for kc in range(KC):
                                P.op("pe", lambda e, py=py, kc=kc, q=q, w=w: e.matmul(
                                    py[:], lhsT=HT[:, kc, 32 + q * 128:32 + (q + 1) * 128], rhs=w[:, kc, :],
                                    start=(kc == 0), stop=(kc == KC - 1)), [CHB[kc], wB], [pyB])
                            evac_y(py, pyB, t, ng, gt[:], gB)
                layer_norm_X(ln_mix_g[0:1, :], ln_mix_b[0:1, :], final=(stage == 1))
            HTB = [[Buf(f"HT{k}_{h}") for h in range(2)] for k in range(KC)]

            def mlp(i, old, extra_sched=None, final=False):
                off[0] = base_off
                U = carve(8192).bitcast(BF16).rearrange("p (a b) -> p a b", a=16)
                UB = [[Buf(f"U{i}_{k}_{h}") for h in range(2)] for k in range(KC)]
                rt = [carve(512) for _ in range(2)]
                rtB = [Buf(f"rt{i}_{k}") for k in range(2)]
                SCRT["ytmp"] = [carve(512) for _ in range(2)]
                SCRB["ytmp"] = [Buf(f"ytmp{i}_{k}") for k in range(2)]
                phase(old, [b for r in UB for b in r] + rtB + SCRB["ytmp"] + [b for r in HTB for b in r])
                make_hT(lambda kc: sm[:, 32 + kc:33 + kc], lambda kc: modT[i][:, 48 + kc:49 + kc], [modB[i], smB],
                        list(range(NTILE)), 0, lambda kc, g0: HTB[kc][g0 // 4])
                for ng in range(4):
                    bcast_cols(BC[ng][:], BCB[ng], lambda c: sm[:, 48 + c:49 + c], [smB], ng)
                    prescale_X(ng, BC[ng][:], BCB[ng], list(range(NTILE)))
                for ng in range(4):
                    bcast_cols(BC[ng][:], BCB[ng], lambda c: modT[i][:, 80 + c:81 + c], [modB[i]], ng)
                for g in range(4):
                    for q in range(4):
                        w, wB = wget((f"w1_{i}", g * 4 + q))
                        for n in range(4):
                            hc = q * 4 + n
                            bcol = PP_B1 + 64 * i + g * 16 + hc
                            for h in range(2):
                                pu, puB = nextps()
                                for kc in range(KC):
                                    P.op("pe", lambda e, pu=pu, kc=kc, n=n, h=h, w=w: e.matmul(
                                        pu[:], lhsT=w[:, kc, n * 128:(n + 1) * 128], rhs=HT[:, kc, h * 512:(h + 1) * 512],
                                        start=(kc == 0), stop=(kc == KC - 1)), [wB, HTB[kc][h]], [puB])
                                k = (hc * 2 + h) % 2
                                P.op("act", lambda e, pu=pu, k=k, bcol=bcol: e.activation(
                                    out=rt[k], in_=pu[:], func=AF.Relu, bias=pp(bcol), scale=1.0), [puB, ppB], [rtB[k]])
                                P.op("pool", lambda e, k=k, hc=hc, h=h: e.tensor_tensor(
                                    out=U[:, hc, h * 512:(h + 1) * 512], in0=rt[k], in1=rt[k], op=ALU.mult),
                                    [rtB[k]], [UB[hc][h]])
                    for ng in range(4):
                        w, wB = wget((f"w2_{i}", g * 4 + ng))
                        gt, gB = BC[ng], BCB[ng]
                        for t in range(NTILE):
                            py, pyB = nextps()
                            for hc in range(KC):
                                P.op("pe", lambda e, py=py, hc=hc, t=t, w=w: e.matmul(
                                    py[:], lhsT=U[:, hc, t * 128:(t + 1) * 128], rhs=w[:, hc, :],
                                    start=(hc == 0), stop=(hc == KC - 1)), [UB[hc][t // 4], wB], [pyB])
                            evac_y(py, pyB, t, ng, gt[:], gB)
                    if extra_sched is not None:
                        extra_sched(g)
                lg = ln_ffn_g[i:i + 1, :]
                lb = ln_ffn_b[i:i + 1, :]
                layer_norm_X(lg, lb, final=final)
                return [b for r in UB for b in r] + rtB + SCRB["ytmp"]

            old = []
            if stage >= 2 and not attn_only:
                old = mlp(0, conv_bufs, final=(stage == 2))
            if stage >= 3:

                TWO_PI = 2.0 * math.pi
                off[0] = base_off
                cs_all = carve(2048); cosT = cs_all[:, 0:1024]; sinT = cs_all[:, 1024:2048]; csB = Buf("cs")
                tq = [carve(512) for _ in range(3)]; tqB = [Buf(f"tq{k}") for k in range(3)]
                stg = [carve(256).bitcast(BF16) for _ in range(2)]; stgB = [Buf(f"stg{k}") for k in range(2)]
                Qh = carve(1024).bitcast(BF16).rearrange("p (m t) -> p m t", m=2); QhB = Buf("Qh")
                Kh = carve(2048).bitcast(BF16).rearrange("p (m t) -> p m t", m=2); KhB = Buf("Kh")
                Vh = carve(2056).bitcast(BF16).rearrange("p (t e) -> p t e", t=16); VhB = Buf("Vh")
                ETf = carve(3712); ET = ETf.bitcast(BF16); ETB = Buf("ET")
                oacc = carve(514).rearrange("p (m e) -> p m e", m=2); oaB = Buf("oacc")
                osm = carve(32); osB = Buf("osm")
                ob = carve(128).bitcast(BF16); obB = Buf("ob")
                subg = carve(256); sgB = Buf("subg")
                lqf = carve(512); lqt = lqf.rearrange("p (a b) -> p a b", a=4); lqB = Buf("lqt")
                o1t = [cs_all[:, j * 256:(j + 1) * 256] for j in range(4)]
                o2t = [cs_all[:, 1024 + j * 256:1024 + (j + 1) * 256] for j in range(4)]
                o1B = [Buf(f"o1_{j}") for j in range(4)]; o2B = [Buf(f"o2_{j}") for j in range(4)]
                obs = [ob] + [lqf[:, k * 128:(k + 1) * 128].bitcast(BF16) for k in range(3)]
                obBs = [obB] + [Buf(f"ob{k}") for k in range(1, 4)]
                SCRT["ytmp"] = [tq[0], tq[1]]
                SCRB["ytmp"] = [tqB[0], tqB[1]]
                attn_bufs = [csB, QhB, KhB, VhB, ETB, oaB, osB, obB, sgB, lqB] + tqB + stgB + o1B + o2B + obBs[1:]
                phase(old, attn_bufs + [b for r in HTB for b in r])
                kvlB = [Buf(f"kvl{h}") for h in range(NHEAD)]; kvaB = [Buf(f"kva{h}") for h in range(NHEAD)]; qscB = Buf("qsc")

                for a4 in range(4):
                    P.dma(lqt[:, a4, :], lqk_d[a4:a4 + 1, :].to_broadcast([128, 128]), [], [lqB], q="sp")
                P.dma(subg, subg_d.to_broadcast([128, 256]), [], [sgB], q="sp")
                P.op("dve", lambda e: e.tensor_tensor(out=lqt[:, 0, :], in0=lqt[:, 0, :], in1=lqt[:, 1, :], op=ALU.mult), [lqB], [lqB])
                P.op("dve", lambda e: e.tensor_tensor(out=lqt[:, 2, :], in0=lqt[:, 2, :], in1=lqt[:, 3, :], op=ALU.mult), [lqB], [lqB])
                P.op("dve", lambda e: e.tensor_reduce(out=osm[:, 0:1], in_=lqt[:, 0, :], axis=mybir.AxisListType.X, op=ALU.add), [lqB], [osB])
                P.op("dve", lambda e: e.tensor_reduce(out=osm[:, 1:2], in_=lqt[:, 2, :], axis=mybir.AxisListType.X, op=ALU.add), [lqB, osB], [osB])
                P.op("act", lambda e: e.activation(out=osm[:, 2:4], in_=osm[:, 0:2], func=AF.Exp), [osB], [osB])
                P.op("dve", lambda e: e.tensor_tensor(out=osm[:, 4:5], in0=osm[:, 3:4], in1=osm[:, 2:3], op=ALU.subtract), [osB], [osB])
                P.op("dve", lambda e: e.tensor_scalar(out=osm[:, 4:5], in0=osm[:, 4:5], scalar1=-LAMBDA_INIT1, scalar2=None, op0=ALU.add), [osB], [osB])
                P.op("dve", lambda e: e.tensor_scalar(out=subg, in0=subg, scalar1=(1.0 - LAMBDA_INIT1), scalar2=None, op0=ALU.mult), [sgB], [sgB])
                P.op("dve", lambda e: e.memset(Vh[:, :, 256:257], 1.0), [], [VhB])

                posi = ETf.bitcast(I32)[:, 0:1024]
                P.dma(posi, pos_d.to_broadcast([128, 1024]), [], [ETB], q="sp")
                for (dst, add) in ((sinT, 0.0), (cosT, 0.25)):
                    for h in range(2):
                        sl = slice(h * 512, (h + 1) * 512)
                        P.op("dve", lambda e, sl=sl: e.tensor_copy(out=tq[0], in_=posi[:, sl]), [ETB], [tqB[0]])
                        P.op("dve", lambda e, add=add: e.tensor_scalar(out=tq[0], in0=tq[0], scalar1=pp(PP_INVF), scalar2=1.0 / TWO_PI,
                                                                        op0=ALU.mult, op1=ALU.mult), [tqB[0], ppB], [tqB[0]])
                        if add:
                            P.op("dve", lambda e, add=add: e.tensor_scalar(out=tq[0], in0=tq[0], scalar1=add, scalar2=None, op0=ALU.add), [tqB[0]], [tqB[0]])
                        ti = tq[1].bitcast(I32)
                        P.op("dve", lambda e: e.tensor_copy(out=ti, in_=tq[0]), [tqB[0]], [tqB[1]])
                        P.op("dve", lambda e: e.tensor_copy(out=tq[2], in_=ti), [tqB[1]], [tqB[2]])
                        P.op("dve", lambda e: e.tensor_tensor(out=tq[0], in0=tq[0], in1=tq[2], op=ALU.subtract), [tqB[0], tqB[2]], [tqB[0]])
                        P.op("dve", lambda e: e.tensor_scalar(out=tq[2], in0=tq[0], scalar1=0.5, scalar2=None, op0=ALU.is_gt), [tqB[0]], [tqB[2]])
                        P.op("dve", lambda e: e.tensor_tensor(out=tq[0], in0=tq[0], in1=tq[2], op=ALU.subtract), [tqB[0], tqB[2]], [tqB[0]])
                        P.op("dve", lambda e: e.tensor_scalar(out=tq[2], in0=tq[0], scalar1=-0.5, scalar2=None, op0=ALU.is_lt), [tqB[0]], [tqB[2]])
                        P.op("dve", lambda e: e.tensor_tensor(out=tq[0], in0=tq[0], in1=tq[2], op=ALU.add), [tqB[0], tqB[2]], [tqB[0]])
                        P.op("act", lambda e, dst=dst, sl=sl: e.activation(out=dst[:, sl], in_=tq[0], func=AF.Sin, scale=TWO_PI), [tqB[0]], [csB])

                P.op("dve", lambda e: e.tensor_scalar(out=sm[:, 0:16], in0=modT[1][:, 16:32], scalar1=1.0, scalar2=None, op0=ALU.add), [modB[1]], [smB])
                make_hT(lambda kc: sm[:, kc:kc + 1], lambda kc: modT[1][:, kc:kc + 1], [modB[1], smB],
                        list(range(NTILE)), 0, lambda kc, g0: HTB[kc][g0 // 4])
                for ng in range(4):
                    bcast_cols(BC[ng][:], BCB[ng], lambda c: modT[1][:, 32 + c:33 + c], [modB[1]], ng)
                sidx = [0]
                for q in range(8):
                    w, wB = wget(("qkv", q))
                    for n in range(4):
                        mp = (q % 4) * 4 + n
                        for h in range(2):
                            pq, pqB = nextps()
                            for kc in range(KC):
                                P.op("pe", lambda e, pq=pq, kc=kc, n=n, h=h, w=w: e.matmul(
                                    pq[:], lhsT=w[:, kc, n * 128:(n + 1) * 128], rhs=HT[:, kc, h * 512:(h + 1) * 512],
                                    start=(kc == 0), stop=(kc == KC - 1)), [wB, HTB[kc][h]], [pqB])
                            k = sidx[0] % 2
                            sidx[0] += 1
                            P.op("act", lambda e, pq=pq: e.activation(out=tq[2], in_=pq[:], func=AF.Copy), [pqB], [tqB[2]])
                            pr, prB = nextps()
                            P.op("pe", lambda e, pr=pr: e.matmul(pr[:], lhsT=PT, rhs=tq[2], start=True, stop=True), [cstB, tqB[2]], [prB])
                            sl = slice(h * 512, (h + 1) * 512)
                            P.op("dve", lambda e, sl=sl: e.tensor_tensor(out=tq[2], in0=tq[2], in1=cosT[:, sl], op=ALU.mult), [tqB[2], csB], [tqB[2]])
                            P.op("dve", lambda e, pr=pr, sl=sl, k=k: e.tensor_tensor(out=tq[k], in0=pr[:], in1=sinT[:, sl], op=ALU.mult), [prB, csB], [tqB[k]])
                            P.op("dve", lambda e, k=k: e.tensor_tensor(out=stg[k], in0=tq[k], in1=tq[2], op=ALU.add), [tqB[k], tqB[2]], [stgB[k]])
                            if q < 4:
                                P.dma(qsc_d[mp * 128:(mp + 1) * 128, sl], stg[k], [stgB[k]], [qscB], q="sp")
                            else:
                                hd, m = mp // 2, mp % 2
                                P.dma(kvl_h[hd][0:128, m * 1024 + h * 512:m * 1024 + (h + 1) * 512], stg[k], [stgB[k]], [kvlB[hd]], q="sp")
                for q in range(8, 12):
                    w, wB = wget(("qkv", q))
                    for t in range(NTILE):
                        pv, pvB = nextps()
                        for kc in range(KC):
                            P.op("pe", lambda e, pv=pv, kc=kc, t=t, w=w: e.matmul(
                                pv[:], lhsT=HT[:, kc, t * 128:(t + 1) * 128], rhs=w[:, kc, :],
                                start=(kc == 0), stop=(kc == KC - 1)), [HTB[kc][t // 4], wB], [pvB])
                        k = sidx[0] % 2
                        sidx[0] += 1
                        P.op("act", lambda e, pv=pv, k=k: e.activation(out=stg[k], in_=pv[:], func=AF.Copy), [pvB], [stgB[k]])
                        for hh in range(2):
                            hd = (q - 8) * 2 + hh
                            P.dma(kvl_h[hd][128:256, t * 256:(t + 1) * 256], stg[k][:, hh * 256:(hh + 1) * 256],
                                  [stgB[k]], [kvlB[hd]], q="sp")
                    for hh in range(2):
                        hd = (q - 8) * 2 + hh
                        P.op("pool", lambda e, hd=hd: e.collective_compute(
                            "AllGather", ALU.bypass, replica_groups=[[0, 1], [2, 3], [4, 5], [6, 7]],
                            ins=[kvl_h[hd].opt()], outs=[kva_h[hd].opt()]), [kvlB[hd]], [kvaB[hd]], dma=True, inc=1)
                SCALE = HD ** -0.5
                OTB = [[Buf(f"OT{k}_{h}") for h in range(2)] for k in range(KC)]
                phase([b for r in HTB for b in r] + [csB, lqB], [b for r in OTB for b in r] + o1B + o2B + obBs[1:])
                ETpB = ETB; EToB = Buf("ETo")
                attn_bufs.append(EToB)
                ETO0 = 4096

                def tiles_of(qh, partner):
                    if partner:
                        return [(t, 0, t * 512) for t in range(8)]
                    res = []
                    eo = ETO0
                    for t in range(4 * qh + 4):
                        j0 = max(0, t - 4 * qh)
                        res.append((8 + t, j0, eo))
                        eo += 512 - j0 * 128
                    return res

                sps = [0]

                def emit_S(hd, qh, m, partner):
                    q0 = qh * 512
                    eB = ETpB if partner else EToB
                    for (kt, j0, eo) in tiles_of(qh, partner):
                        ncol = 512 - j0 * 128
                        bi = 4 + sps[0] % 3
                        sps[0] += 1
                        pss, pssB = PS[bi], PB[bi]
                        P.op("pe", lambda e, pss=pss, kt=kt, j0=j0, ncol=ncol: e.matmul(
                            pss[:, 0:ncol], lhsT=Kh[:, m, kt * 128:(kt + 1) * 128], rhs=Qh[:, m, q0 + j0 * 128:q0 + 512],
                            start=True, stop=True), [KhB, QhB], [pssB])
                        if partner:
                            P.op("act", lambda e, pss=pss, eo=eo, ncol=ncol: e.activation(
                                out=ET[:, eo:eo + ncol], in_=pss[:, 0:ncol], func=AF.Exp, bias=pp(PP_FLAG + 2), scale=SCALE),
                                [pssB, ppB], [eB])
                        else:
                            P.op("act", lambda e, pss=pss, eo=eo, ncol=ncol: e.activation(
                                out=ET[:, eo:eo + ncol], in_=pss[:, 0:ncol], func=AF.Exp, scale=SCALE), [pssB], [eB])
                            if (kt - 8) >= 4 * qh:
                                P.op("dve", lambda e, eo=eo: e.tensor_tensor(out=ET[:, eo:eo + 128], in0=ET[:, eo:eo + 128],
                                                                             in1=tri, op=ALU.mult), [eB, cstB], [eB])

                def emit_PV(qh, partner):
                    eB = ETpB if partner else EToB
                    tl = tiles_of(qh, partner)
                    for j in range(4):
                        use = [(kt, j0, eo) for (kt, j0, eo) in tl if j0 <= j]
                        for ui, (kt, j0, eo) in enumerate(use):
                            c0 = eo + (j - j0) * 128
                            P.op("pe", lambda e, j=j, kt=kt, c0=c0, st_=(partner and ui == 0), sp_=((not partner) and ui == len(use) - 1): e.matmul(
                                PS[j][:, 0:257], lhsT=ET[:, c0:c0 + 128], rhs=Vh[:, kt, :], start=st_, stop=sp_),
                                [eB, VhB], [PB[j]])

                def epi_a(hd, qh, m):
                    ot, oB_ = (o1t, o1B) if m == 0 else (o2t, o2B)
                    lc = 16 if m == 0 else 20
                    for j in range(4):
                        P.op("act", lambda e, j=j: e.activation(out=ot[j], in_=PS[j][:, 0:256], func=AF.Copy), [PB[j]], [oB_[j]])
                        P.op("act", lambda e, j=j: e.activation(out=osm[:, lc + j:lc + j + 1], in_=PS[j][:, 256:257], func=AF.Copy),
                             [PB[j], osB], [osB])
                    if m == 0:
                        return
                    for j in range(4):
                        oq, oqB, o2, o2B_ = o1t[j], o1B[j], o2t[j], o2B[j]
                        P.op("dve", lambda e, j=j: e.reciprocal(out=osm[:, 8:10], in_=osm[:, 16 + j:24 + j:4]), [osB], [osB])
                        P.op("dve", lambda e: e.tensor_tensor(out=osm[:, 9:10], in0=osm[:, 9:10], in1=osm[:, 4:5], op=ALU.mult), [osB], [osB])
                        P.op("dve", lambda e, oq=oq: e.tensor_scalar(out=oq, in0=oq, scalar1=osm[:, 8:9], scalar2=None,
                                                                      op0=ALU.mult), [oqB, osB], [oqB])
                        P.op("dve", lambda e, oq=oq, o2=o2: e.scalar_tensor_tensor(out=oq, in0=o2, scalar=osm[:, 9:10],
                                                                                   in1=oq, op0=ALU.mult, op1=ALU.add),
                             [o2B_, osB, oqB], [oqB])
                        P.op("dve", lambda e, oq=oq: e.tensor_tensor(out=oacc[:, 0, 0:256], in0=oq, in1=oq, op=ALU.mult),
                             [oqB], [oaB])
                        P.op("dve", lambda e: e.tensor_reduce(out=osm[:, 10:11], in_=oacc[:, 0, 0:256], axis=mybir.AxisListType.X, op=ALU.add),
                             [oaB, osB], [osB])
                        P.op("dve", lambda e: e.tensor_scalar(out=osm[:, 10:11], in0=osm[:, 10:11], scalar1=1.0 / 256, scalar2=RMS_EPS,
                                                              op0=ALU.mult, op1=ALU.add), [osB], [osB])
                        P.op("act", lambda e: e.activation(out=osm[:, 11:12], in_=osm[:, 10:11], func=AF.Sqrt), [osB], [osB])
                        P.op("dve", lambda e: e.reciprocal(out=osm[:, 12:13], in_=osm[:, 11:12]), [osB], [osB])
                        P.op("dve", lambda e, oq=oq, j=j: e.scalar_tensor_tensor(out=obs[j], in0=oq, scalar=osm[:, 12:13], in1=subg,
                                                                                op0=ALU.mult, op1=ALU.mult), [oqB, osB, sgB], [obBs[j]])

                def epi_b(hd, qh):
                    for j in range(4):
                        jj = qh * 4 + j
                        for cc in range(2):
                            pt, ptB = PSB16[1], PB[7]
                            P.op("pe", lambda e, pt=pt, cc=cc, j=j: e.transpose(pt[:, cc * 128:(cc + 1) * 128], obs[j][:, cc * 128:(cc + 1) * 128], identb[:]),
                                 [obBs[j], identbB], [ptB])
                            P.op("act", lambda e, pt=pt, cc=cc, jj=jj, hd=hd: e.activation(
                                out=HT[:, hd * 2 + cc, jj * 128:(jj + 1) * 128], in_=pt[:, cc * 128:(cc + 1) * 128], func=AF.Copy),
                                [ptB], [OTB[hd * 2 + cc][jj // 4]])

                def load_qk(hd):
                    P.dma(Qh[:], qsc_d[hd * 256:(hd + 1) * 256, :].rearrange("(m p) t -> p m t", p=128), [qscB], [QhB], q="sp")
                    P.dma(Kh[:, :, 0:1024], kva_h[hd][0:128, :].rearrange("p (m t) -> p m t", m=2), [kvaB[hd]], [KhB], q="sp")
                    P.dma(Kh[:, :, 1024:2048], kvl_h[hd][0:128, :].rearrange("p (m t) -> p m t", m=2), [kvlB[hd]], [KhB], q="sp")

                def load_v(hd):
                    P.dma(Vh[:, 0:8, 0:256], kva_h[hd][128:256, :].rearrange("p (t e) -> p t e", t=8), [kvaB[hd]], [VhB], q="sp")
                    P.dma(Vh[:, 8:16, 0:256], kvl_h[hd][128:256, :].rearrange("p (t e) -> p t e", t=8), [kvlB[hd]], [VhB], q="sp")

                iters = [(hd, qh, m) for hd in range(NHEAD) for qh in range(2) for m in range(2)]
                pend_b = None
                load_qk(0)
                emit_S(0, 0, 0, True)
                emit_S(0, 0, 0, False)
                for ii, (hd, qh, m) in enumerate(iters):
                    nxt = iters[ii + 1] if ii + 1 < len(iters) else None
                    if qh == 0 and m == 0:
                        load_v(hd)
                    emit_PV(qh, True)
                    if nxt is not None:
                        if nxt[0] != hd:
                            load_qk(nxt[0])
                        emit_S(nxt[0], nxt[1], nxt[2], True)
                    emit_PV(qh, False)
                    if nxt is not None:
                        emit_S(nxt[0], nxt[1], nxt[2], False)
                    if pend_b is not None:
                        epi_b(*pend_b)
                        pend_b = None
                    epi_a(hd, qh, m)
                    if m == 1:
                        pend_b = (hd, qh)
                if pend_b is not None:
                    epi_b(*pend_b)
                for ng in range(4):
                    w, wB = wget(("ow", ng))
                    gt, gB = BC[ng], BCB[ng]
                    for t in range(NTILE):
                        py, pyB = nextps()
                        for kc in range(KC):
                            P.op("pe", lambda e, py=py, kc=kc, t=t, w=w: e.matmul(
                                py[:], lhsT=HT[:, kc, t * 128:(t + 1) * 128], rhs=w[:, kc, :],
                                start=(kc == 0), stop=(kc == KC - 1)), [OTB[kc][t // 4], wB], [pyB])
                        evac_y(py, pyB, t, ng, gt[:], gB, alpha=ALPHA)
                layer_norm_X(ln_mix_g[1:2, :], ln_mix_b[1:2, :], final=attn_only)
                if not attn_only:
                    derive(1, "f", PP_B2 + 16)
                    HTB = [[Buf(f"HTb{k}_{h}") for h in range(2)] for k in range(KC)]
                    mlp(1, attn_bufs + [b for r in OTB for b in r], final=True)


        except _Stop:
            pass
        P.final_wait([outB])
        st = P.emit()
    return nc, st


def _pcol(v):
    v = np.asarray(v, np.float32).reshape(-1, 128)
    return np.ascontiguousarray(v.T)


def make_in_maps(inp, stage=99, attn_only=False):
    x = np.asarray(inp["x"], np.float32)
    c = np.asarray(inp["c"], np.float32)
    pos = np.asarray(inp["positions"], np.int32)
    cst = np.zeros((128, 640), np.float32)
    cst[:, 0:128] = np.eye(128, dtype=np.float32)
    cst[:, 128:256] = 1.0
    PT = np.zeros((128, 128), np.float32)
    for do in range(64):
        PT[do + 64, do] = -1.0
        PT[do, do + 64] = 1.0
    cst[:, 256:384] = PT
    kk = np.arange(128)[:, None]
    qq = np.arange(128)[None, :]
    cst[:, 384:512] = (kk <= qq).astype(np.float32)
    inv_freq = (10000.0 ** (-np.arange(0, 128, 2, dtype=np.float32) / 128.0)).astype(np.float32)

    shared = {}
    for k in ("ln_mix_g", "ln_mix_b", "ln_ffn_g", "ln_ffn_b", "conv_pw1_w", "conv_pw2_w",
              "attn_qkv_w", "attn_o_w", "attn_subln_g", "mlp_w1", "mlp_w2"):
        shared[k] = np.ascontiguousarray(np.asarray(inp[k], np.float32))
    if stage < 3 and not attn_only:
        shared["attn_qkv_w"] = shared["attn_qkv_w"][:, 0:1, 0:1]
        shared["attn_o_w"] = shared["attn_o_w"][:, 0:1, 0:1]
        shared["mlp_w1"] = shared["mlp_w1"][0:1]
        shared["mlp_w2"] = shared["mlp_w2"][0:1]
    if attn_only:
        shared["attn_qkv_w"] = np.asarray(inp["attn_qkv_w"], np.float32)
        shared["attn_o_w"] = np.asarray(inp["attn_o_w"], np.float32)
        shared["conv_pw1_w"] = shared["conv_pw1_w"][:, 0:1, 0:1]
        shared["conv_pw2_w"] = shared["conv_pw2_w"][:, 0:1, 0:1]
    if stage < 2 or attn_only:
        shared["mlp_w1"] = shared["mlp_w1"][0:1, 0:1, 0:1]
        shared["mlp_w2"] = shared["mlp_w2"][0:1, 0:1, 0:1]
    shared = {k: np.ascontiguousarray(v) for k, v in shared.items()}
    lqk = np.concatenate([np.asarray(inp[k], np.float32).reshape(1, 128)
                          for k in ("attn_lq1", "attn_lk1", "attn_lq2", "attn_lk2")], 0)

    pbase = np.zeros((128, NPP), np.float32)
    pbase[:, PP_PW1B:PP_PW1B + 32] = _pcol(inp["conv_pw1_b"][0])
    pbase[:, PP_DWB:PP_DWB + 16] = _pcol(inp["conv_dw_b"][0])
    pbase[:, PP_CLNG:PP_CLNG + 16] = _pcol(inp["conv_ln_g"][0])
    pbase[:, PP_CLNB:PP_CLNB + 16] = _pcol(inp["conv_ln_b"][0])
    for i in range(2):
        pbase[:, PP_B1 + 64 * i:PP_B1 + 64 * (i + 1)] = _pcol(inp["mlp_b1"][i])
        pbase[:, PP_B2 + 16 * i:PP_B2 + 16 * (i + 1)] = _pcol(inp["mlp_b2"][i])
    dw = np.asarray(inp["conv_dw_w"], np.float32)[0]
    pbase[:, PP_DWW:PP_DWW + 16 * 31] = dw.T.reshape(16, 128, 31).transpose(1, 0, 2).reshape(128, 16 * 31)
    pbase[:, PP_INVF] = np.concatenate([inv_freq, inv_freq])
    pbase[:, PP_PW2B:PP_PW2B + 16] = _pcol(inp["conv_pw2_b"][0])

    maps = []
    for core in range(8):
        b, half = core // 2, core % 2
        t0 = half * NTOK
        pp = pbase.copy()
        pp[:, PP_FLAG + 0] = float(half)
        pp[:, PP_FLAG + 1] = 1.0
        pp[:, PP_FLAG + 2] = 0.0 if half == 1 else NEG
        for bb in range(4):
            pp[:, PP_CTA + bb:PP_CTA + 64:4] = _pcol(c[bb])
        for i in range(2):
            pp[:, PP_ADABS + 12 * i:PP_ADABS + 12 * (i + 1)] = _pcol(np.asarray(inp["ada_b"], np.float32)[i, core * 1536:(core + 1) * 1536])
        pp[:, PP_SEL:PP_SEL + 4] = 0.0
        pp[:, PP_SEL + b] = 1.0
        xs = np.ascontiguousarray(x[b, t0:t0 + NTOK])
        xh = np.zeros((64, D), np.float32)
        if half == 1:
            xh[0:32] = x[b, t0 - 32:t0]
        xh[32:64] = x[b, t0 + 480:t0 + 512]
        ada_s = np.ascontiguousarray(np.asarray(inp["ada_w"], np.float32)[:, :, core * 1536:(core + 1) * 1536])
        m = {"ada_s": ada_s, "x": xs, "xh": xh, "pos": np.ascontiguousarray(pos[b, t0:t0 + NTOK]).reshape(1, NTOK),
             "ppar": pp, "cst": cst, "lqk": lqk}
        m.update(shared)
        maps.append(m)
    return maps


_NC_CACHE = {}


def kernel(**inputs):
    stage = 99
    if stage not in _NC_CACHE:
        _NC_CACHE[stage] = build_nc(stage)[0]
    nc = _NC_CACHE[stage]
    maps = make_in_maps(inputs)
    res = run_bass_kernel_spmd(nc, maps, core_ids=list(range(8)))
    out = np.empty((4, 2048, D), np.float32)
    for core in range(8):
        b, half = core // 2, core % 2
        out[b, half * NTOK:(half + 1) * NTOK] = res.results[core]["out"]
    return out
```
